# Optimizing a Trainium2 kernel written in Bass

```python
import math
import jax, jax.numpy as jnp
from jax import lax
import numpy as np

D_MODEL = 1024
BATCH = 16
SEQ = 256
DEPTH = 2
DEC_BATCH = 8
DEC_SEQ = 2048
PAST_LEN = 256

GRID_W = 64
N_EVEN = (DEPTH + 1) // 2
N_ODD = DEPTH // 2
MIX_A = D_MODEL // 2
A_GROUP = 16
A_GROUPS = MIX_A // A_GROUP
S5_N = 64
MIX_B = D_MODEL - MIX_A
NA_HEADS = 8
NA_HD = MIX_B // NA_HEADS
NA_KH = 8
NA_KW = 16
QBLK = 128
DN_HEADS = 8
DN_DK = D_MODEL // DN_HEADS
DN_DV = D_MODEL // DN_HEADS
DN_CONV = 5
DN_CHUNK = 64
D_FF = 4 * D_MODEL
EPS = 1e-6

kernel_name = "hybrid_s5_natten_gdn_diffusion_step"


def _rmsnorm(x, g):
    x32 = x.astype(jnp.float32)
    y = x32 * lax.rsqrt(jnp.mean(x32 * x32, axis=-1, keepdims=True) + EPS)
    return (y * g.astype(jnp.float32)).astype(x.dtype)


def _modulation(cond, w_mod, b_mod):
    m = jax.nn.silu(cond) @ w_mod + b_mod
    return jnp.split(m.reshape(-1, 1, 6 * D_MODEL), 6, axis=-1)


def _modulate(h, shift, scale):
    return h * (1 + scale) + shift


def _mlp(h, w1, w2):
    return jnp.square(jax.nn.relu(h @ w1)) @ w2


def _s5_scan(u, lam_re, lam_im, log_dt, b_re, b_im, c_re, c_im, x0):
    f32 = jnp.float32
    lam = lax.complex(lam_re.astype(f32), lam_im.astype(f32))
    lam_bar = jnp.exp(lam * jnp.exp(log_dt.astype(f32))[:, None])
    b_bar = ((lam_bar - 1) / lam)[..., None] * lax.complex(b_re.astype(f32), b_im.astype(f32))
    bu = jnp.einsum('gnp,blgp->blgn', b_bar, u.astype(jnp.complex64))
    bu = bu.at[:, 0].add(lam_bar * x0)
    a = jnp.broadcast_to(lam_bar, bu.shape)

    def comb(left, right):
        return (left[0] * right[0], right[0] * left[1] + right[1])

    _, xs = lax.associative_scan(comb, (a, bu), axis=1)
    c_mat = lax.complex(c_re.astype(f32), c_im.astype(f32))
    y = jnp.einsum('gpn,blgn->blgp', c_mat, xs).real
    return y, xs[:, -1]


def _s5_mixer(u, x0_re, x0_im, lam_re, lam_im, log_dt, b_re, b_im, c_re, c_im, d, w_glu, b_glu):
    bsz, length, _ = u.shape
    ug = u.astype(jnp.float32).reshape(bsz, length, A_GROUPS, A_GROUP)
    x0 = lax.complex(x0_re.astype(jnp.float32), x0_im.astype(jnp.float32))
    y_f, xl_f = _s5_scan(ug, lam_re[0], lam_im[0], log_dt[0], b_re[0], b_im[0], c_re[0], c_im[0], x0[:, 0])
    y_b, xl_b = _s5_scan(ug[:, ::-1], lam_re[1], lam_im[1], log_dt[1], b_re[1], b_im[1], c_re[1], c_im[1], x0[:, 1])
    y = y_f + y_b[:, ::-1] + d.astype(jnp.float32).reshape(A_GROUPS, A_GROUP) * ug
    z = jax.nn.gelu(y.reshape(bsz, length, MIX_A))
    out = z * jax.nn.sigmoid(z @ w_glu.astype(jnp.float32) + b_glu.astype(jnp.float32))
    xl = jnp.stack([xl_f, xl_b], axis=1)
    return out, xl.real, xl.imag


def _context_attention(q, k, v):
    bsz, s_len, h, d = q.shape
    qb = jnp.swapaxes(q.reshape(bsz, s_len // QBLK, QBLK, h, d), 0, 1)

    def blk(qi):
        s = jnp.einsum('bqhd,bkhd->bhqk', qi, k).astype(jnp.float32)
        p = jax.nn.softmax(s, axis=-1).astype(v.dtype)
        return jnp.einsum('bhqk,bkhd->bqhd', p, v)

    o = lax.map(blk, qb)
    return jnp.swapaxes(o, 0, 1).reshape(bsz, s_len, h * d)


def _neighbourhood_attention(q, k, v, ck, cv, rpb):
    bsz, length, h, d = q.shape
    rows = length // GRID_W
    kh = min(NA_KH, rows)
    kw = NA_KW
    qg = q.reshape(bsz, rows, GRID_W, h, d)
    kg = k.reshape(bsz, rows, GRID_W, h, d)
    vg = v.reshape(bsz, rows, GRID_W, h, d)
    col = jnp.arange(GRID_W)
    cs = jnp.clip(col - kw // 2, 0, GRID_W - kw)
    col_idx = cs[:, None] + jnp.arange(kw)[None, :]
    dc = col_idx - col[:, None] + (NA_KW - 1)

    def row_block(r):
        rs = jnp.clip(r - kh // 2, 0, rows - kh)
        q_r = lax.dynamic_index_in_dim(qg, r, axis=1, keepdims=False)
        k_band = lax.dynamic_slice_in_dim(kg, rs, kh, axis=1)
        v_band = lax.dynamic_slice_in_dim(vg, rs, kh, axis=1)
        k_win = k_band[:, :, col_idx]
        v_win = v_band[:, :, col_idx]
        dr = rs + jnp.arange(kh) - r + (NA_KH - 1)
        bias = rpb[:, dr[None, :, None], dc[:, None, :]]
        s_loc = jnp.einsum('bqhd,biqjhd->bhqij', q_r, k_win) + bias[None]
        s_loc = s_loc.reshape(bsz, h, GRID_W, kh * kw)
        s_ctx = jnp.einsum('bqhd,bphd->bhqp', q_r, ck)
        p = jax.nn.softmax(jnp.concatenate([s_loc, s_ctx], axis=-1).astype(jnp.float32), axis=-1)
        p_loc = p[..., :kh * kw].reshape(bsz, h, GRID_W, kh, kw).astype(v.dtype)
        p_ctx = p[..., kh * kw:].astype(cv.dtype)
        return (jnp.einsum('bhqij,biqjhd->bqhd', p_loc, v_win)
                + jnp.einsum('bhqp,bphd->bqhd', p_ctx, cv))

    o = lax.map(row_block, jnp.arange(rows))
    return jnp.moveaxis(o, 0, 1).reshape(bsz, length, h * d)


def _even_split(h, w_in):
    proj = h @ w_in
    u = proj[..., :MIX_A]
    q, k, v = jnp.split(proj[..., MIX_A:], 3, axis=-1)
    shp = h.shape[:2] + (NA_HEADS, NA_HD)
    return u, q.reshape(shp) * (NA_HD ** -0.5), k.reshape(shp), v.reshape(shp)


def _even_mixer_context(h, w_in, w_out, s5p):
    u, q, k, v = _even_split(h, w_in)
    zeros = jnp.zeros((h.shape[0], 2, A_GROUPS, S5_N), jnp.float32)
    a_out, s_re, s_im = _s5_mixer(u, zeros, zeros, *s5p)
    b_out = _context_attention(q, k, v)
    out = jnp.concatenate([a_out.astype(h.dtype), b_out.astype(h.dtype)], axis=-1) @ w_out
    return out, k, v, s_re, s_im


def _even_mixer_latent(h, ck, cv, x0_re, x0_im, w_in, w_out, s5p, rpb):
    u, q, k, v = _even_split(h, w_in)
    a_out, _, _ = _s5_mixer(u, x0_re, x0_im, *s5p)
    b_out = _neighbourhood_attention(q, k, v, ck, cv, rpb)
    return jnp.concatenate([a_out.astype(h.dtype), b_out.astype(h.dtype)], axis=-1) @ w_out


def _dwconv(x, w):
    kk, ch = w.shape
    return lax.conv_general_dilated(x, w.astype(x.dtype)[:, None, :], window_strides=(1,),
                                    padding=[((kk - 1) // 2, kk // 2)],
                                    dimension_numbers=('NWC', 'WIO', 'NWC'),
                                    feature_group_count=ch)


def _chunk(t):
    b_, l_, h_ = t.shape[:3]
    t = t.reshape((b_, l_ // DN_CHUNK, DN_CHUNK, h_) + t.shape[3:])
    return jnp.moveaxis(t, 3, 1)


def _gdn_chunked(q, k, v, g, beta, s0):
    bsz, length, h, _ = q.shape
    dv = v.shape[-1]
    qc, kc, vc = _chunk(q), _chunk(k), _chunk(v)
    gc, bc = _chunk(g), _chunk(beta)
    gcum = jnp.cumsum(gc, axis=-1)
    tri_incl = jnp.tril(jnp.ones((DN_CHUNK, DN_CHUNK), bool))
    tri_strict = jnp.tril(jnp.ones((DN_CHUNK, DN_CHUNK), bool), -1)
    decay = jnp.exp(jnp.where(tri_incl, gcum[..., :, None] - gcum[..., None, :], -jnp.inf))
    kb = kc * bc[..., None]
    a_mat = jnp.where(tri_strict, jnp.einsum('bhncd,bhnsd->bhncs', kb, kc) * decay, 0.0)
    eye = jnp.eye(DN_CHUNK, dtype=jnp.float32)
    t_mat = lax.linalg.triangular_solve(eye + a_mat, jnp.broadcast_to(eye, a_mat.shape),
                                        left_side=True, lower=True)
    u_c = t_mat @ (vc * bc[..., None])
    w_c = t_mat @ (kb * jnp.exp(gcum)[..., None])
    attn = jnp.where(tri_incl, jnp.einsum('bhncd,bhnsd->bhncs', qc, kc) * decay, 0.0)

    def step(s, xs):
        q_i, k_i, u_i, w_i, g_i, at_i = xs
        v_new = u_i - w_i @ s
        o = (q_i * jnp.exp(g_i)[..., None]) @ s + at_i @ v_new
        g_last = g_i[..., -1]
        s = (s * jnp.exp(g_last)[..., None, None]
             + jnp.einsum('bhcd,bhce->bhde', k_i * jnp.exp(g_last[..., None] - g_i)[..., None], v_new))
        return s, o

    xs = tuple(jnp.moveaxis(t, 2, 0) for t in (qc, kc, u_c, w_c, gcum, attn))
    s_last, o = lax.scan(step, s0, xs)
    return jnp.transpose(o, (1, 0, 3, 2, 4)).reshape(bsz, length, h, dv), s_last


def _deltanet_mixer(h, s0, w_in, conv_w, a_log, dt_bias, norm_g, w_out):
    f32 = jnp.float32
    bsz, length, _ = h.shape
    proj = h @ w_in
    qkv = jax.nn.silu(_dwconv(proj[..., :3 * D_MODEL], conv_w)).astype(f32)
    z = proj[..., 3 * D_MODEL:4 * D_MODEL].astype(f32).reshape(bsz, length, DN_HEADS, DN_DV)
    a_in = proj[..., 4 * D_MODEL:4 * D_MODEL + 2 * DN_HEADS].astype(f32).reshape(bsz, length, 2, DN_HEADS)
    b_in = proj[..., 4 * D_MODEL + 2 * DN_HEADS:].astype(f32).reshape(bsz, length, 2, DN_HEADS)
    g = -jnp.exp(a_log.astype(f32)) * jax.nn.softplus(a_in + dt_bias.astype(f32))
    beta = jax.nn.sigmoid(b_in)
    q, k, v = jnp.split(qkv, 3, axis=-1)
    q = q.reshape(bsz, length, DN_HEADS, DN_DK)
    k = k.reshape(bsz, length, DN_HEADS, DN_DK)
    v = v.reshape(bsz, length, DN_HEADS, DN_DV)
    q = q * lax.rsqrt(jnp.sum(q * q, axis=-1, keepdims=True) + EPS) * (DN_DK ** -0.5)
    k = k * lax.rsqrt(jnp.sum(k * k, axis=-1, keepdims=True) + EPS)
    s0 = s0.astype(f32)
    o_f, s_f = _gdn_chunked(q, k, v, g[:, :, 0], beta[:, :, 0], s0[:, 0])
    o_b, s_b = _gdn_chunked(q[:, ::-1], k[:, ::-1], v[:, ::-1], g[:, ::-1, 1], beta[:, ::-1, 1], s0[:, 1])
    o = o_f + o_b[:, ::-1]
    o = _rmsnorm(o, norm_g) * jax.nn.silu(z)
    out = o.reshape(bsz, length, D_MODEL).astype(h.dtype) @ w_out
    return out, jnp.stack([s_f, s_b], axis=1)


def setup_inputs(seed: int = 0) -> dict:
    key = jax.random.key(seed)
    ks = iter(jax.random.split(key, 48))
    f32 = jnp.float32

    def nrm(shape, scale):
        return jax.random.normal(next(ks), shape, f32) * scale

    def unif(shape, lo, hi):
        return jax.random.uniform(next(ks), shape, f32, minval=lo, maxval=hi)

    dn_dt = jnp.exp(unif((N_ODD, 2, DN_HEADS), math.log(1e-3), math.log(1e-1)))
    return {
        "x_prompt": nrm((BATCH, SEQ, D_MODEL), 1.0),
        "x_sample": nrm((DEC_BATCH, DEC_SEQ, D_MODEL), 1.0),
        "c": nrm((DEC_BATCH, D_MODEL), 1.0),
        "cache_na_k": nrm((DEC_BATCH, N_EVEN, PAST_LEN, NA_HEADS, NA_HD), 1.0),
        "cache_na_v": nrm((DEC_BATCH, N_EVEN, PAST_LEN, NA_HEADS, NA_HD), 1.0),
        "state_s5_re": nrm((DEC_BATCH, N_EVEN, 2, A_GROUPS, S5_N), 0.5),
        "state_s5_im": nrm((DEC_BATCH, N_EVEN, 2, A_GROUPS, S5_N), 0.5),
        "state_dn": nrm((DEC_BATCH, N_ODD, 2, DN_HEADS, DN_DK, DN_DV), 0.3),
        "c_ctx": nrm((D_MODEL,), 1.0),
        "norm_mix_g": 1.0 + nrm((DEPTH, D_MODEL), 0.02),
        "norm_ff_g": 1.0 + nrm((DEPTH, D_MODEL), 0.02),
        "w_mod": nrm((DEPTH, D_MODEL, 6 * D_MODEL), D_MODEL ** -0.5),
        "b_mod": nrm((DEPTH, 6 * D_MODEL), 0.02),
        "w_ff1": nrm((DEPTH, D_MODEL, D_FF), D_MODEL ** -0.5),
        "w_ff2": nrm((DEPTH, D_FF, D_MODEL), D_FF ** -0.5),
        "w_in_e": nrm((N_EVEN, D_MODEL, MIX_A + 3 * MIX_B), D_MODEL ** -0.5),
        "w_out_e": nrm((N_EVEN, D_MODEL, D_MODEL), D_MODEL ** -0.5),
        "s5_lam_re": -0.5 + nrm((N_EVEN, 2, A_GROUPS, S5_N), 0.01),
        "s5_lam_im": math.pi * jnp.arange(S5_N, dtype=f32) + nrm((N_EVEN, 2, A_GROUPS, S5_N), 0.01),
        "s5_log_dt": unif((N_EVEN, 2, A_GROUPS), math.log(1e-3), math.log(1e-1)),
        "s5_b_re": nrm((N_EVEN, 2, A_GROUPS, S5_N, A_GROUP), (2 * A_GROUP) ** -0.5),
        "s5_b_im": nrm((N_EVEN, 2, A_GROUPS, S5_N, A_GROUP), (2 * A_GROUP) ** -0.5),
        "s5_c_re": nrm((N_EVEN, 2, A_GROUPS, A_GROUP, S5_N), S5_N ** -0.5),
        "s5_c_im": nrm((N_EVEN, 2, A_GROUPS, A_GROUP, S5_N), S5_N ** -0.5),
        "s5_d": nrm((N_EVEN, MIX_A), 1.0),
        "s5_w_glu": nrm((N_EVEN, MIX_A, MIX_A), MIX_A ** -0.5),
        "s5_b_glu": nrm((N_EVEN, MIX_A), 0.02),
        "na_rpb": nrm((N_EVEN, NA_HEADS, 2 * NA_KH - 1, 2 * NA_KW - 1), 0.1),
        "w_in_o": nrm((N_ODD, D_MODEL, 4 * D_MODEL + 4 * DN_HEADS), D_MODEL ** -0.5),
        "dn_conv_w": nrm((N_ODD, DN_CONV, 3 * D_MODEL), DN_CONV ** -0.5),
        "dn_a_log": jnp.log(unif((N_ODD, 2, DN_HEADS), 1.0, 16.0)),
        "dn_dt_bias": dn_dt + jnp.log(-jnp.expm1(-dn_dt)),
        "dn_norm_g": 1.0 + nrm((N_ODD, DN_DV), 0.02),
        "w_out_o": nrm((N_ODD, D_MODEL, D_MODEL), D_MODEL ** -0.5),
        "final_norm_g": 1.0 + nrm((D_MODEL,), 0.02),
    }


def reference(x_prompt, x_sample, c, cache_na_k, cache_na_v, state_s5_re, state_s5_im, state_dn,
              c_ctx, norm_mix_g, norm_ff_g, w_mod, b_mod, w_ff1, w_ff2, w_in_e, w_out_e,
              s5_lam_re, s5_lam_im, s5_log_dt, s5_b_re, s5_b_im, s5_c_re, s5_c_im, s5_d,
              s5_w_glu, s5_b_glu, na_rpb, w_in_o, dn_conv_w, dn_a_log, dn_dt_bias, dn_norm_g,
              w_out_o, final_norm_g):
    xp, xs = x_prompt, x_sample
    nk, nv, ns_re, ns_im, nd = [], [], [], [], []
    for l in range(DEPTH):
        mp = _modulation(c_ctx, w_mod[l], b_mod[l])
        ms = _modulation(c, w_mod[l], b_mod[l])
        hp = _modulate(_rmsnorm(xp, norm_mix_g[l]), mp[0], mp[1])
        hs = _modulate(_rmsnorm(xs, norm_mix_g[l]), ms[0], ms[1])
        if l % 2 == 0:
            e = l // 2
            s5p = (s5_lam_re[e], s5_lam_im[e], s5_log_dt[e], s5_b_re[e], s5_b_im[e],
                   s5_c_re[e], s5_c_im[e], s5_d[e], s5_w_glu[e], s5_b_glu[e])
            op, kp, vp, sre, sim = _even_mixer_context(hp, w_in_e[e], w_out_e[e], s5p)
            os_ = _even_mixer_latent(hs, cache_na_k[:, e], cache_na_v[:, e], state_s5_re[:, e],
                                     state_s5_im[:, e], w_in_e[e], w_out_e[e], s5p, na_rpb[e])
            nk.append(kp)
            nv.append(vp)
            ns_re.append(sre)
            ns_im.append(sim)
        else:
            o = l // 2
            zeros = jnp.zeros((xp.shape[0], 2, DN_HEADS, DN_DK, DN_DV), jnp.float32)
            op, sdn = _deltanet_mixer(hp, zeros, w_in_o[o], dn_conv_w[o], dn_a_log[o],
                                      dn_dt_bias[o], dn_norm_g[o], w_out_o[o])
            os_, _ = _deltanet_mixer(hs, state_dn[:, o], w_in_o[o], dn_conv_w[o], dn_a_log[o],
                                     dn_dt_bias[o], dn_norm_g[o], w_out_o[o])
            nd.append(sdn)
        xp = xp + mp[2] * op
        xs = xs + ms[2] * os_
        xp = xp + mp[5] * _mlp(_modulate(_rmsnorm(xp, norm_ff_g[l]), mp[3], mp[4]), w_ff1[l], w_ff2[l])
        xs = xs + ms[5] * _mlp(_modulate(_rmsnorm(xs, norm_ff_g[l]), ms[3], ms[4]), w_ff1[l], w_ff2[l])
    y_prompt = _rmsnorm(xp, final_norm_g)
    y_sample = _rmsnorm(xs, final_norm_g)
    new_cache_na_k = jnp.stack(nk, axis=1)
    new_cache_na_v = jnp.stack(nv, axis=1)
    new_state_s5_re = jnp.stack(ns_re, axis=1)
    new_state_s5_im = jnp.stack(ns_im, axis=1)
    new_state_dn = jnp.stack(nd, axis=1)
    return (y_prompt, y_sample, new_cache_na_k, new_cache_na_v, new_state_s5_re, new_state_s5_im, new_state_dn)
```

```python
import numpy as np
from contextlib import ExitStack
import concourse.bass as bass
import concourse.mybir as mybir
from concourse.bass_utils import run_bass_kernel_spmd

F32 = mybir.dt.float32
BF16 = mybir.dt.bfloat16
F32R = mybir.dt.float32r
AF = mybir.ActivationFunctionType
ALU = mybir.AluOpType

D = 1024
T = 2560
LS = 2048
LP = 256
NB = 5
EPS = 1e-6
NDMA = 16


class KB:
    def __init__(self):
        self.nc = nc = bass.Bass("TRN2", target_bir_lowering=False)
        self.es = ExitStack()
        self.eng = {"pe": nc.tensor, "act": nc.scalar, "dve": nc.vector, "pool": nc.gpsimd, "sp": nc.sync}
        self.sem = {}
        for e in self.eng:
            self.sem[e] = self.es.enter_context(nc.semaphore("s_" + e))
        for i in range(NDMA):
            self.sem["d%d" % i] = self.es.enter_context(nc.semaphore("sd%d" % i))
        self.cnt = {s: 0 for s in self.sem}
        self.waited = {e: {} for e in self.eng}
        self.vc = {}
        self.seq = {}
        self.nseq = 0
        self.lastw = {}
        self.readers = {}
        self.dnext = 0
        self.uid = 0
        self.ps_next = 0

    def sb(self, shape, dt=F32, name=None, stack=None):
        self.uid += 1
        t = (stack or self.es).enter_context(self.nc.sbuf_tensor("%s_%d" % (name or "t", self.uid), list(shape), dt))
        return t

    def dram(self, name, shape, kind, dt=F32):
        return self.nc.dram_tensor(name, list(shape), dt, kind=kind).ap()

    def _learn(self, e, s, v):
        kn = self.waited[e]
        if kn.get(s, 0) < v:
            kn[s] = v
        for s2, v2 in self.vc.get((s, v), {}).items():
            if kn.get(s2, 0) < v2:
                kn[s2] = v2

    def wait(self, e, s, v):
        if self.waited[e].get(s, 0) < v:
            self.eng[e].wait_ge(self.sem[s], v)
            self._learn(e, s, v)

    def _need(self, e, r, w):
        need = {}
        for k in r:
            lw = self.lastw.get(k)
            if lw:
                need[lw[0]] = max(need.get(lw[0], 0), lw[1])
        for k in w:
            lw = self.lastw.get(k)
            if lw:
                need[lw[0]] = max(need.get(lw[0], 0), lw[1])
            for s, v in self.readers.get(k, {}).items():
                need[s] = max(need.get(s, 0), v)
        out = []
        for s, v in need.items():
            if e == "pe" and s == "pe":
                continue
            if self.waited[e].get(s, 0) < v:
                out.append((s, v))
        return out

    def _issue(self, e, need, fn):
        if not need:
            for s, v in need:
                self.wait(e, s, v)
            return fn()
        need = sorted(need, key=lambda sv: self.seq.get(sv, 0), reverse=True)
        ds, dv = need[0]
        self._learn(e, ds, dv)
        for s, v in need[1:]:
            self.wait(e, s, v)
        ins = fn()
        ins._wait_ge(self.sem[ds], dv)
        return ins

    def _record(self, s, idx, r, w):
        for k in r:
            self.readers.setdefault(k, {})[s] = idx
        for k in w:
            self.lastw[k] = (s, idx)
            self.readers[k] = {}

    def op(self, e, fn, r=(), w=()):
        psr = [k_ for k_ in r if isinstance(k_, tuple) and k_[0] == "ps"]
        if psr:
            w = list(w) + psr
        ins = self._issue(e, self._need(e, r, w), fn)
        self.cnt[e] += 1
        ins.then_inc(self.sem[e], 1)
        self.vc[(e, self.cnt[e])] = dict(self.waited[e])
        self.nseq += 1
        self.seq[(e, self.cnt[e])] = self.nseq
        self._record(e, self.cnt[e], r, w)
        return ins

    def dma(self, out, in_, r=(), w=(), q="sp", **kw):
        ch = self.dnext
        self.dnext = (ch + 1) % NDMA
        s = "d%d" % ch
        if self.cnt[s] > 0:
            self.wait(q, s, self.cnt[s])
        ins = self._issue(q, self._need(q, r, w), lambda: self.eng[q].dma_start(out=out, in_=in_, **kw))
        self.cnt[s] += 16
        ins.then_inc(self.sem[s], 16)
        self.vc[(s, self.cnt[s])] = dict(self.waited[q])
        self.nseq += 1
        self.seq[(s, self.cnt[s])] = self.nseq
        self._record(s, self.cnt[s], r, w)

    def barrier(self):
        for e in self.eng:
            for s in self.sem:
                if s != e and self.cnt[s] > 0:
                    self.wait(e, s, self.cnt[s])

    def finish(self):
        for s in self.sem:
            if s.startswith("d") and self.cnt[s] > 0:
                self.wait("sp", s, self.cnt[s])
        for e in ("pe", "act", "dve", "pool"):
            if self.cnt[e] > 0:
                self.wait("sp", e, self.cnt[e])

    def mm(self, out, lhsT, rhs, start, stop, r=(), w=()):
        return self.op("pe", lambda: self.nc.tensor.matmul(out, lhsT, rhs, start=start, stop=stop), r, w)

    def tr(self, out, in_, ident, r=(), w=()):
        return self.op("pe", lambda: self.nc.tensor.transpose(out, in_, ident), r, w)

    def act(self, out, in_, func, r=(), w=(), **kw):
        return self.op("act", lambda: self.nc.scalar.activation(out=out, in_=in_, func=func, **kw), r, w)

    def ts(self, e, out, in0, s1, s2, op0, op1=None, r=(), w=()):
        eng = self.eng[e]
        if op1 is None:
            return self.op(e, lambda: eng.tensor_scalar(out=out, in0=in0, scalar1=s1, scalar2=None, op0=op0), r, w)
        return self.op(e, lambda: eng.tensor_scalar(out=out, in0=in0, scalar1=s1, scalar2=s2, op0=op0, op1=op1), r, w)

    def tt(self, e, out, in0, in1, op, r=(), w=()):
        eng = self.eng[e]
        return self.op(e, lambda: eng.tensor_tensor(out=out, in0=in0, in1=in1, op=op), r, w)

    def stt(self, e, out, in0, scalar, in1, op0, op1, r=(), w=()):
        eng = self.eng[e]
        return self.op(e, lambda: eng.scalar_tensor_tensor(out=out, in0=in0, scalar=scalar, in1=in1, op0=op0, op1=op1), r, w)

    def copy(self, e, out, in_, r=(), w=()):
        if e == "act":
            return self.op(e, lambda: self.nc.scalar.copy(out=out, in_=in_), r, w)
        eng = self.eng[e]
        return self.op(e, lambda: eng.tensor_copy(out=out, in_=in_), r, w)

    def memset(self, e, ap, val, r=(), w=()):
        eng = self.eng[e]
        return self.op(e, lambda: eng.memset(ap, val), r, w)


def build(stage="full"):
    k = KB()
    nc = k.nc
    xs_d = k.dram("xs", [LS, D], "ExternalInput")
    xp_d = k.dram("xp", [2 * LP, D], "ExternalInput")
    cond_d = k.dram("cond", [16, 128], "ExternalInput")
    gains_d = k.dram("gains", [40, 128], "ExternalInput")
    wmod_d = k.dram("w_mod", [2, D, 6 * D], "ExternalInput")
    bmod_d = k.dram("b_mod", [2, 48, 128], "ExternalInput")
    wff1_d = k.dram("w_ff1", [2, D, 4 * D], "ExternalInput")
    wff2_d = k.dram("w_ff2", [2, 4 * D, D], "ExternalInput")
    wine_d = k.dram("w_in_e", [D, 2048], "ExternalInput")
    woute_d = k.dram("w_out_e", [D, D], "ExternalInput")
    ck_d = k.dram("ck", [LP, 512], "ExternalInput")
    cv_d = k.dram("cv", [LP, 512], "ExternalInput")
    rpb_d = k.dram("rpb", [8, 15, 31], "ExternalInput")
    lam_d = k.dram("s5lam", [2, 32, 128], "ExternalInput")
    x0_d = k.dram("s5x0", [2, 32, 128], "ExternalInput")
    ldt_d = k.dram("s5ldt", [32, 2], "ExternalInput")
    sb_d = k.dram("s5b", [2, 4096, 16], "ExternalInput")
    sc_d = k.dram("s5c", [2, 1024, 64], "ExternalInput")
    sd_d = k.dram("s5d", [4, 128], "ExternalInput")
    bg_d = k.dram("s5bg", [4, 128], "ExternalInput")
    wglu_d = k.dram("s5wg", [512, 512], "ExternalInput")
    nsre_d = k.dram("nsre", [64, 128], "ExternalOutput")
    nsim_d = k.dram("nsim", [64, 128], "ExternalOutput")
    wino_d = k.dram("w_in_o", [D, 4128], "ExternalInput")
    woo_d = k.dram("w_out_o", [D, D], "ExternalInput")
    cw_d = k.dram("convw", [120, 128], "ExternalInput")
    alog_d = k.dram("alog", [1, 16], "ExternalInput")
    dtb_d = k.dram("dtb", [1, 16], "ExternalInput")
    ng_d = k.dram("dnng", [1, 128], "ExternalInput")
    sdn_d = k.dram("sdn", [16, 128, 128], "ExternalInput")
    xsp_d = k.dram("xspill", [128, 8 * T], "Internal")
    nd_d = k.dram("nd", [4096, 128], "ExternalOutput")
    nk_d = k.dram("nk", [2 * LP, 512], "ExternalOutput")
    nv_d = k.dram("nv", [2 * LP, 512], "ExternalOutput")
    ys_d = k.dram("ys", [LS, D], "ExternalOutput")
    yp_d = k.dram("yp", [2 * LP, D], "ExternalOutput")

    ident = k.sb([128, 128], F32, "ident")
    k.op("pool", lambda: nc.gpsimd.iota(ident[:], pattern=[[1, 128]], base=0, channel_multiplier=-1,
                                          allow_small_or_imprecise_dtypes=True), w=["ident"])
    k.ts("dve", ident[:], ident[:], 0.0, None, ALU.is_equal, r=["ident"], w=["ident"])
    ones_bf = k.sb([128, 128], BF16, "ones")
    k.memset("dve", ones_bf[:], 1.0, w=["ones"])

    ps = [k.es.enter_context(nc.psum_tensor("ps%d" % i, [128, 512], F32)) for i in range(8)]

    k.ps_pool = list(range(8))

    def psum():
        k.ps_next = (k.ps_next + 1) % len(k.ps_pool)
        i = k.ps_pool[k.ps_next]
        return ps[i], ("ps", i)

    xT = k.sb([128, 8, T], F32, "xT")
    hT = k.sb([128, 8, T], BF16, "hT")

    def xk(b, c):
        return ("x", b, c)

    def load_cols(src_ap, nrows, name, stack=None):
        st = k.sb([nrows, 128], F32, name + "_rows", stack=stack)
        k.dma(st[:], src_ap, w=[name + "_rows"])
        p, pk = psum()
        k.tr(p[:, 0:nrows], st[:], ident[0:nrows, 0:nrows], r=[name + "_rows", "ident"], w=[pk])
        dst = k.sb([128, nrows], F32, name, stack=stack)
        k.copy("dve", dst[:], p[:, 0:nrows], r=[pk], w=[name])
        return dst

    condT = load_cols(cond_d[:, :], 16, "cond")
    gT = load_cols(gains_d[:, :], 40, "gains")
    bmodT = [load_cols(bmod_d[l, :, :], 48, "bmod%d" % l) for l in range(2)]
    scond = k.sb([128, 16], F32, "scond")
    k.act(scond[:], condT[:], AF.Silu, r=["cond"], w=["scond"])

    modv = [k.sb([128, 48, 2], F32, "modv%d" % l) for l in range(2)]
    gs = {}
    for l in range(2):
        for sub in range(2):
            gs[(l, sub)] = k.sb([128, 8, 2], F32, "gs%d%d" % (l, sub))
    sqb = [k.sb([128, 512], BF16, "sq%d" % i) for i in range(2)]
    tmpb = [k.sb([128, 512], F32, "ntmp%d" % i) for i in range(3)]
    rstd_t = [k.sb([128, 512], F32, "rstd%d" % i) for i in range(2)]
    ph = ExitStack()
    stage_x = [k.sb([128, D], F32, "xstage%d" % i, stack=ph) for i in range(2)]
    for t in range(T // 128):
        st = stage_x[t % 2]
        sk = "xstage%d" % (t % 2)
        src = xs_d[t * 128:(t + 1) * 128, :] if t < 16 else xp_d[(t - 16) * 128:(t - 15) * 128, :]
        k.dma(st[:], src, w=[sk])
        b = t // 4
        for half in range(2):
            p, pk = psum()
            for j in range(4):
                c = half * 4 + j
                k.tr(p[:, j * 128:(j + 1) * 128], st[:, c * 128:(c + 1) * 128], ident[:], r=[sk, "ident"], w=[pk])
            k.copy("act" if half else "dve", xT[:, half * 4:half * 4 + 4, t * 128:(t + 1) * 128],
                   p[:, :].rearrange("p (c t) -> p c t", c=4), r=[pk], w=[xk(b, c) for c in range(half * 4, half * 4 + 4)])

    wmp = [k.sb([128, 8, 512], BF16, "wmodp%d" % i, stack=ph) for i in range(2)]
    scond_bf = k.sb([128, 16], BF16, "scond_bf", stack=ph)
    k.copy("dve", scond_bf[:], scond[:], r=["scond"], w=["scond_bf"])
    npan = 0
    for l in range(2):
        pm, pmk = psum()
        for pj in range(12):
            wp = wmp[npan % 2]
            wk = "wmodp%d" % (npan % 2)
            npan += 1
            k.dma(wp[:], wmod_d[l, :, pj * 512:(pj + 1) * 512].rearrange("(kc p) m -> p kc m", p=128), w=[wk], q="pool")
            for jj in range(4):
                j = pj * 4 + jj
                for kc in range(8):
                    k.mm(pm[:, 2 * j:2 * j + 2], wp[:, kc, jj * 128:(jj + 1) * 128],
                         scond_bf[:, :].rearrange("p (c k) -> p c k", c=2)[:, :, kc],
                         start=(kc == 0), stop=(kc == 7), r=[wk, "scond_bf"], w=[pmk])
        k.tt("dve", modv[l][:], pm[:, 0:96].rearrange("p (j c) -> p j c", c=2),
             bmodT[l][:, :].unsqueeze(2).to_broadcast([128, 48, 2]), ALU.add, r=[pmk, "bmod%d" % l], w=["modv%d" % l])
    for l in range(2):
        for sub in range(2):
            t_ = gs[(l, sub)]
            gcol = gT[:, (sub * 2 + l) * 8:(sub * 2 + l) * 8 + 8]
            sc = modv[l][:, (sub * 3 + 1) * 8:(sub * 3 + 1) * 8 + 8, :]
            k.stt("dve", t_[:], sc, 1.0, gcol.unsqueeze(2).to_broadcast([128, 8, 2]), ALU.add, ALU.mult,
                  r=["modv%d" % l, "gains"], w=["gs%d%d" % (l, sub)])

    k.barrier()
    ph.close()

    def shift_ap(l, sub, c, cond):
        return modv[l][:, (sub * 3) * 8 + c, cond:cond + 1]

    def gate_ap(l, sub, c, cond):
        return modv[l][:, (sub * 3 + 2) * 8 + c, cond:cond + 1]

    cnt = {"sq": 0, "tmp": 0, "rstd": 0}

    def rstd_block(b):
        p, pk = psum()
        sl = slice(b * 512, (b + 1) * 512)
        for c in range(8):
            i = cnt["sq"] % 2
            cnt["sq"] += 1
            k.act(sqb[i][:], xT[:, c, sl], AF.Square, r=[xk(b, c)], w=["sq%d" % i])
            k.mm(p[:, :], ones_bf[:], sqb[i][:], start=(c == 0), stop=(c == 7), r=["ones", "sq%d" % i], w=[pk])
        i = cnt["rstd"] % 2
        cnt["rstd"] += 1
        rk = "rstd%d" % i
        k.act(rstd_t[i][:], p[:, :], AF.Sqrt, r=[pk], w=[rk], bias=EPS, scale=1.0 / D)
        k.op("dve", lambda: nc.vector.reciprocal(out=rstd_t[i][:], in_=rstd_t[i][:]), r=[rk], w=[rk])
        return rstd_t[i], rk

    def norm_block(l, sub, b):
        cond = 0 if b < 4 else 1
        sl = slice(b * 512, (b + 1) * 512)
        rt, rk = rstd_block(b)
        for c in range(8):
            i = cnt["tmp"] % 3
            cnt["tmp"] += 1
            tk = "ntmp%d" % i
            k.tt("dve", tmpb[i][:], xT[:, c, sl], rt[:], ALU.mult, r=[xk(b, c), rk], w=[tk])
            k.act(hT[:, c, sl], tmpb[i][:], AF.Identity, r=[tk, "gs%d%d" % (l, sub), "modv%d" % l], w=[("h", b, c)],
                  bias=shift_ap(l, sub, c, cond), scale=gs[(l, sub)][:, c, cond:cond + 1])

    NQ = 8
    mcnt = {"p": 0, "relu": 0}

    def mlp(l):
        ph = ExitStack()
        h1T = k.sb([128, 4, T], BF16, "h1T", stack=ph)
        w1p = [k.sb([128, 8, 512], BF16, "w1p%d" % i, stack=ph) for i in range(2)]
        w2p = [k.sb([128, 4, D], BF16, "w2p%d" % i, stack=ph) for i in range(2)]
        relu_t = [k.sb([128, 512], F32, "relu%d" % i, stack=ph) for i in range(2)]
        for b in range(NB):
            norm_block(l, 1, b)
        def load_q(q_):
            i_ = q_ % 2
            k.dma(w1p[i_][:], wff1_d[l, :, q_ * 512:(q_ + 1) * 512].rearrange("(kc p) m -> p kc m", p=128), w=["w1p%d" % i_], q="pool")
            k.dma(w2p[i_][:], wff2_d[l, q_ * 512:(q_ + 1) * 512, :].rearrange("(kc p) m -> p kc m", p=128), w=["w2p%d" % i_], q="pool")
        load_q(0)
        for q in range(NQ):
            i = q % 2
            w1, w1k, w2, w2k = w1p[i], "w1p%d" % i, w2p[i], "w2p%d" % i
            if q + 1 < NQ:
                load_q(q + 1)
            for fc in range(4):
                for b in range(NB):
                    sl = slice(b * 512, (b + 1) * 512)
                    p, pk = psum()
                    for kc in range(8):
                        k.mm(p[:, :], w1[:, kc, fc * 128:(fc + 1) * 128], hT[:, kc, sl], start=(kc == 0), stop=(kc == 7),
                             r=[w1k, ("h", b, kc)], w=[pk])
                    j = mcnt["relu"] % 2
                    mcnt["relu"] += 1
                    rk = "relu%d" % j
                    k.act(relu_t[j][:], p[:, :], AF.Relu, r=[pk], w=[rk])
                    k.tt("pool", h1T[:, fc, sl], relu_t[j][:], relu_t[j][:], ALU.mult, r=[rk], w=[("h1", b, fc)])
            for c in range(8):
                for b in range(NB):
                    cond = 0 if b < 4 else 1
                    sl = slice(b * 512, (b + 1) * 512)
                    p, pk = psum()
                    for fc in range(4):
                        k.mm(p[:, :], w2[:, fc, c * 128:(c + 1) * 128], h1T[:, fc, sl], start=(fc == 0), stop=(fc == 3),
                             r=[w2k, ("h1", b, fc)], w=[pk])
                    k.stt("dve", xT[:, c, sl], p[:, :], gate_ap(l, 1, c, cond), xT[:, c, sl], ALU.mult, ALU.add,
                          r=[pk, "modv%d" % l, xk(b, c)], w=[xk(b, c)])
        k.barrier()
        ph.close()

    def mixer0():
        ph = ExitStack()
        for b in range(NB):
            norm_block(0, 0, b)
        boT = k.sb([128, 4, T], BF16, "boT", stack=ph)
        ph_outer = ph
        ph = ExitStack()
        abig = k.sb([31, 127], F32, "abig", stack=ph)
        k.op("pool", lambda: nc.gpsimd.iota(abig[:], pattern=[[1, 127]], base=-78, channel_multiplier=1,
                                              allow_small_or_imprecise_dtypes=True), w=["abig"])
        k.ts("dve", abig[:], abig[:], 0.0, None, ALU.is_equal, r=["abig"], w=["abig"])
        maskb = k.sb([64, 64], F32, "maskb", stack=ph)
        csv = k.sb([64, 1], F32, "csv", stack=ph)
        k.op("pool", lambda: nc.gpsimd.iota(csv[:], pattern=[[0, 1]], base=-8, channel_multiplier=1,
                                              allow_small_or_imprecise_dtypes=True), w=["csv"])
        k.ts("dve", csv[:], csv[:], 0.0, 48.0, ALU.max, ALU.min, r=["csv"], w=["csv"])
        k.op("pool", lambda: nc.gpsimd.iota(maskb[:], pattern=[[1, 64]], base=0, channel_multiplier=0,
                                              allow_small_or_imprecise_dtypes=True), w=["maskb"])
        k.ts("dve", maskb[:], maskb[:], csv[:, 0:1], None, ALU.subtract, r=["maskb", "csv"], w=["maskb"])
        m2 = k.sb([64, 64], F32, "m2", stack=ph)
        k.ts("dve", m2[:], maskb[:], 16.0, None, ALU.is_lt, r=["maskb"], w=["m2"])
        k.ts("dve", maskb[:], maskb[:], 0.0, None, ALU.is_ge, r=["maskb"], w=["maskb"])
        k.tt("dve", maskb[:], maskb[:], m2[:], ALU.mult, r=["maskb", "m2"], w=["maskb"])
        k.ts("dve", maskb[:], maskb[:], -1.0, 30000.0, ALU.add, ALU.mult, r=["maskb"], w=["maskb"])
        rpbT = k.sb([31, 8, 15], F32, "rpbT", stack=ph)
        for h in range(8):
            st = k.sb([15, 31], F32, "rpbrow%d" % h, stack=ph)
            k.dma(st[:], rpb_d[h, :, :], w=["rpbrow%d" % h])
            p, pk = psum()
            k.tr(p[0:31, 0:15], st[:], ident[0:15, 0:15], r=["rpbrow%d" % h, "ident"], w=[pk])
            k.copy("dve", rpbT[:, h, :], p[0:31, 0:15], r=[pk], w=["rpbT"])
        abig_bf = k.sb([31, 127], BF16, "abig_bf", stack=ph)
        k.copy("dve", abig_bf[:], abig[:], r=["abig"], w=["abig_bf"])
        rpbT_bf = k.sb([31, 8, 15], BF16, "rpbT_bf", stack=ph)
        k.copy("dve", rpbT_bf[:], rpbT[:], r=["rpbT"], w=["rpbT_bf"])
        S_sb = [k.sb([128, 896], F32, "S%d" % i, stack=ph) for i in range(4)]
        for i in range(4):
            k.memset("pool", S_sb[i][:], -30000.0, w=["S%d" % i])
        PT_sb = [k.sb([128, 448], BF16, "PT%d" % i, stack=ph) for i in range(4)]
        small = [k.sb([128, 4], F32, "sm%d" % i, stack=ph) for i in range(4)]
        wq = k.sb([128, 8, 384], BF16, "wq", stack=ph)
        qT = k.sb([128, T], BF16, "qT", stack=ph)
        kT = k.sb([128, T], BF16, "kT", stack=ph)
        vtm = k.sb([128, 20, 128], BF16, "vtm", stack=ph)
        ckst = k.sb([128, 2, 128], F32, "ckst", stack=ph)
        ckT = k.sb([128, LP], BF16, "ckT", stack=ph)
        cvt = k.sb([128, 2, 128], BF16, "cvt", stack=ph)
        tab = k.sb([64, 2, 960], F32, "tab", stack=ph)
        kvst = [tmpb[0], tmpb[1]]
        it = {"n": 0, "kv": 0}

        ppool2 = {0: [0, [0, 1]], 1: [0, [2, 3]], 2: [0, [4, 5]], 3: [0, [6, 7]]}

        def psd2(hh):
            st = ppool2[hh]
            st[0] = (st[0] + 1) % 2
            i_ = st[1][st[0]]
            return ps[i_], ("ps", i_)

        def run_many(gl_):
            gens = list(gl_)
            while gens:
                for g_ in list(gens):
                    try:
                        next(g_)
                    except StopIteration:
                        gens.remove(g_)

        def run_pair(g0, g1):
            gens = [g0, g1]
            while gens:
                for g_ in list(gens):
                    try:
                        next(g_)
                    except StopIteration:
                        gens.remove(g_)

        def attend(c, hh, qsl, nq, band, ctx_kT, ctx_tiles, bias_ap, cid=None):
            pb = hh * 64
            i = hh if cid is None else cid
            S, Sk, PT, PTk, sm, smk = S_sb[i], "S%d" % i, PT_sb[i], "PT%d" % i, small[i], "sm%d" % i
            P, Pk = S, Sk
            nctx = len(ctx_tiles) * 128
            if band is not None:
                rs = band
                pA, pAk = psd2(i)
                k.mm(pA[0:nq, 0:512], qT[pb:pb + 64, qsl], kT[pb:pb + 64, rs * 64:rs * 64 + 512], True, True,
                     r=["qT", "kT"], w=[pAk])
                k.tt("dve", S[0:nq, 64:576], pA[0:nq, 0:512], bias_ap, ALU.add, r=[pAk, "tab"], w=[Sk])
            pB, pBk = psd2(i)
            k.mm(pB[0:nq, 0:nctx], qT[pb:pb + 64, qsl], ctx_kT, True, True, r=["qT", "kT", "ckT"], w=[pBk])
            k.copy("act", S[0:nq, 640:640 + nctx], pB[0:nq, 0:nctx], r=[pBk], w=[Sk])
            yield
            lo = 0 if band is not None else 640
            k.op("dve", lambda: nc.vector.tensor_reduce(out=sm[0:nq, 1:2], in_=S[0:nq, lo:640 + nctx], axis=mybir.AxisListType.X,
                                                          op=ALU.max, negate=True), r=[Sk], w=[smk])
            k.act(P[0:nq, lo:640 + nctx], S[0:nq, lo:640 + nctx], AF.Exp, r=[Sk, smk], w=[Pk, smk],
                  bias=sm[0:nq, 1:2], scale=1.0, accum_out=sm[0:nq, 2:3])
            yield
            k.op("dve", lambda: nc.vector.reciprocal(out=sm[0:nq, 3:4], in_=sm[0:nq, 2:3]), r=[smk], w=[smk])
            k.ts("dve", P[0:nq, lo:640 + nctx], P[0:nq, lo:640 + nctx], sm[0:nq, 3:4], None, ALU.mult, r=[Pk, smk], w=[Pk])
            yield
            chunks = []
            if band is not None:
                if rs % 2 == 0:
                    for j in range(4):
                        chunks.append((64 + j * 128, vtm[:, rs // 2 + j, pb:pb + 64]))
                else:
                    for j in range(5):
                        chunks.append((j * 128, vtm[:, (rs - 1) // 2 + j, pb:pb + 64]))
            for j, vt in enumerate(ctx_tiles):
                chunks.append((640 + j * 128, vt))
            pT, pTk = psd2(i)
            for j, (c0, _) in enumerate(chunks):
                k.tr(pT[:, j * nq:(j + 1) * nq], P[0:nq, c0:c0 + 128], ident[0:nq, 0:nq], r=[Pk, "ident"], w=[pTk])
            nch = len(chunks)
            k.copy("act", PT[:, 0:nch * nq], pT[:, 0:nch * nq], r=[pTk], w=[PTk])
            if band is not None:
                k.memset("pool", S[0:nq, 0:64], -30000.0, w=[Sk])
                k.memset("pool", S[0:nq, 576:640], -30000.0, w=[Sk])
            yield
            pO, pOk = psd2(i)
            for j, (_, vt) in enumerate(chunks):
                k.mm(pO[pb:pb + 64, 0:nq], vt, PT[:, j * nq:(j + 1) * nq], j == 0, j == nch - 1,
                     r=[PTk, "vtm", "cvt"], w=[pOk])
            k.copy("act", boT[pb:pb + 64, c, qsl], pO[pb:pb + 64, 0:nq], r=[pOk], w=[("bo", c)])
            yield

        for c in range(4):
            for w_ in range(3):
                k.dma(wq[:, :, w_ * 128:(w_ + 1) * 128],
                      wine_d[:, 512 * (w_ + 1) + c * 128:512 * (w_ + 1) + (c + 1) * 128].rearrange("(kc p) m -> p kc m", p=128),
                      w=["wq"], q="pool")
            for b in range(NB):
                sl = slice(b * 512, (b + 1) * 512)
                for w_, dst, dk_ in ((0, qT, "qT"), (1, kT, "kT")):
                    p, pk = psum()
                    for kc in range(8):
                        k.mm(p[:, :], wq[:, kc, w_ * 128:(w_ + 1) * 128], hT[:, kc, sl], kc == 0, kc == 7,
                             r=["wq", ("h", b, kc)], w=[pk])
                    k.act(dst[:, sl], p[:, :], AF.Identity, r=[pk], w=[dk_], scale=(0.125 if w_ == 0 else 1.0))
            for t in range(20):
                p, pk = psum()
                for kc in range(8):
                    k.mm(p[:, 0:128], hT[:, kc, t * 128:(t + 1) * 128], wq[:, kc, 256:384], kc == 0, kc == 7,
                         r=["wq", ("h", t // 4, kc)], w=[pk])
                k.copy("dve", vtm[:, t, :], p[:, 0:128], r=[pk], w=["vtm"])
            k.dma(ckst[:], ck_d[:, c * 128:(c + 1) * 128].rearrange("(t p) m -> p t m", p=128), w=["ckst"])
            p, pk = psum()
            for t in range(2):
                k.tr(p[:, t * 128:(t + 1) * 128], ckst[:, t, :], ident[:], r=["ckst", "ident"], w=[pk])
            k.copy("dve", ckT[:], p[:, 0:LP], r=[pk], w=["ckT"])
            k.dma(cvt[:], cv_d[:, c * 128:(c + 1) * 128].rearrange("(t p) m -> p t m", p=128), w=["cvt"], q="pool")
            for hh in range(2):
                h = 2 * c + hh
                pt_, ptk = [], []
                for half in range(2):
                    a, b_ = psum()
                    pt_.append(a)
                    ptk.append(b_)
                for kc in range(64):
                    for half in range(2):
                        d0, d1 = (0, 8) if half == 0 else (8, 15)
                        k.mm(pt_[half][0:64, 0:(d1 - d0) * 64].rearrange("p (d k) -> p d k", k=64)[:, :, kc],
                             abig_bf[:, 63 - kc:127 - kc], rpbT_bf[:, h, d0:d1], True, True, r=["abig_bf", "rpbT_bf"], w=[ptk[half]])
                for half in range(2):
                    d0, d1 = (0, 8) if half == 0 else (8, 15)
                    k.tt("dve", tab[:, hh, d0 * 64:d1 * 64].rearrange("p (d k) -> p d k", k=64),
                         pt_[half][0:64, 0:(d1 - d0) * 64].rearrange("p (d k) -> p d k", k=64),
                         maskb[:, :].unsqueeze(1).to_broadcast([64, d1 - d0, 64]), ALU.add,
                         r=[ptk[half], "maskb"], w=["tab"])
            for r2 in range(16):
                gl_ = []
                for rr in range(2):
                    r_ = r2 * 2 + rr
                    rs = min(max(r_ - 4, 0), 24)
                    dr0 = rs - r_ + 7
                    for hh in range(2):
                        gl_.append(attend(c, hh, slice(r_ * 64, r_ * 64 + 64), 64, rs, ckT[hh * 64:hh * 64 + 64, :],
                                          [cvt[:, 0, hh * 64:hh * 64 + 64], cvt[:, 1, hh * 64:hh * 64 + 64]],
                                          tab[:, hh, dr0 * 64:dr0 * 64 + 512], cid=rr * 2 + hh))
                run_many(gl_)
            for pbatch in range(2):
                t0 = LS + pbatch * LP
                gl_ = []
                for qb in range(2):
                    for hh in range(2):
                        gl_.append(attend(c, hh, slice(t0 + qb * 128, t0 + qb * 128 + 128), 128, None,
                                          kT[hh * 64:hh * 64 + 64, t0:t0 + LP],
                                          [vtm[:, 16 + pbatch * 2, hh * 64:hh * 64 + 64], vtm[:, 17 + pbatch * 2, hh * 64:hh * 64 + 64]],
                                          None, cid=qb * 2 + hh))
                run_many(gl_)
            for w_, dst_d in ((1, nk_d), (2, nv_d)):
                for t in range(16, 20):
                    p, pk = psum()
                    for kc in range(8):
                        k.mm(p[:, 0:128], hT[:, kc, t * 128:(t + 1) * 128], wq[:, kc, w_ * 128:(w_ + 1) * 128], kc == 0, kc == 7,
                             r=["wq", ("h", 4, kc)], w=[pk])
                    i = it["kv"] % 2
                    it["kv"] += 1
                    k.copy("dve", kvst[i][:, 0:128], p[:, 0:128], r=[pk], w=["ntmp%d" % i])
                    k.dma(dst_d[(t - 16) * 128:(t - 15) * 128, c * 128:(c + 1) * 128], kvst[i][:, 0:128], r=["ntmp%d" % i])
        k.barrier()
        ph.close()
        ph = ph_outer
        aoT = k.sb([128, 4, T], BF16, "aoT", stack=ph)
        if stage != "noS5":
            k.ps_pool = list(range(8))
            ph_keep = ph
            ph = ExitStack()
            hflat = hT[:, :, :].rearrange("p c t -> p (c t)")
            hf32 = hflat.bitcast(F32)
            bufs = [(hf32[:, 0:2048], hf32[:, 2048:4096]), (hf32[:, 4096:6144], hf32[:, 6144:8192])]
            xbf = (hflat[:, 16384:18432], hflat[:, 18432:20480])
            lre = load_cols(lam_d[0, :, :], 32, "lre", ph)
            lim = load_cols(lam_d[1, :, :], 32, "lim", ph)
            x0r = load_cols(x0_d[0, :, :], 32, "x0r", ph)
            x0i = load_cols(x0_d[1, :, :], 32, "x0i", ph)
            ldr = k.sb([32, 2], F32, "ldr", stack=ph)
            k.dma(ldr[:], ldt_d[:, :], w=["ldr"])
            ldx = k.sb([32, 128], F32, "ldx", stack=ph)
            k.copy("dve", ldx[:, :].rearrange("p (g n) -> p g n", g=2), ldr[:, :].unsqueeze(2).to_broadcast([32, 2, 64]),
                   r=["ldr"], w=["ldx"])
            p, pk = psum()
            k.tr(p[:, 0:32], ldx[:], ident[0:32, 0:32], r=["ldx", "ident"], w=[pk])
            dtt = k.sb([128, 32], F32, "dtt", stack=ph)
            k.act(dtt[:], p[:, 0:32], AF.Exp, r=[pk], w=["dtt"])
            W = {}

            def tl(name):
                W[name] = k.sb([128, 32], F32, "s5_" + name, stack=ph)
                return W[name]

            def e2(out, a, b_, op, eng="dve"):
                k.tt(eng, W[out][:], W[a][:], W[b_][:], op, r=["s5_" + a, "s5_" + b_], w=["s5_" + out])

            for nme in ("er", "th", "s16", "s8", "c8", "t1", "t2", "cr", "ci", "nr", "ni", "den", "cfr", "cfi", "lx0r", "lx0i"):
                tl(nme)
            k.tt("dve", W["th"][:], lim[:], dtt[:], ALU.mult, r=["lim", "dtt"], w=["s5_th"])
            k.tt("dve", W["t1"][:], lre[:], dtt[:], ALU.mult, r=["lre", "dtt"], w=["s5_t1"])
            k.act(W["er"][:], W["t1"][:], AF.Exp, r=["s5_t1"], w=["s5_er"])
            k.act(W["s16"][:], W["th"][:], AF.Sin, r=["s5_th"], w=["s5_s16"], scale=1.0 / 16)
            k.act(W["s8"][:], W["th"][:], AF.Sin, r=["s5_th"], w=["s5_s8"], scale=1.0 / 8)
            e2("t1", "s16", "s16", ALU.mult)
            k.ts("dve", W["c8"][:], W["t1"][:], -2.0, 1.0, ALU.mult, ALU.add, r=["s5_t1"], w=["s5_c8"])
            cs_, sn_ = "c8", "s8"
            for it_ in range(3):
                e2("t1", cs_, cs_, ALU.mult)
                e2("t2", sn_, sn_, ALU.mult)
                e2("ni", cs_, sn_, ALU.mult)
                e2("cr", "t1", "t2", ALU.subtract)
                k.ts("dve", W["ci"][:], W["ni"][:], 2.0, None, ALU.mult, r=["s5_ni"], w=["s5_ci"])
                if it_ < 2:
                    k.copy("dve", W["c8"][:], W["cr"][:], r=["s5_cr"], w=["s5_c8"])
                    k.copy("dve", W["s8"][:], W["ci"][:], r=["s5_ci"], w=["s5_s8"])
            pw = []
            for kk in range(11):
                pw.append((tl("pr%d" % kk), tl("pi%d" % kk), tl("pn%d" % kk)))
            e2("pr0", "er", "cr", ALU.mult)
            e2("pi0", "er", "ci", ALU.mult)
            for kk in range(11):
                k.ts("dve", W["pn%d" % kk][:], W["pi%d" % kk][:], -1.0, None, ALU.mult, r=["s5_pi%d" % kk], w=["s5_pn%d" % kk])
                if kk < 10:
                    e2("t1", "pr%d" % kk, "pr%d" % kk, ALU.mult)
                    e2("t2", "pi%d" % kk, "pi%d" % kk, ALU.mult)
                    e2("pr%d" % (kk + 1), "t1", "t2", ALU.subtract)
                    e2("t1", "pr%d" % kk, "pi%d" % kk, ALU.mult)
                    k.ts("dve", W["pi%d" % (kk + 1)][:], W["t1"][:], 2.0, None, ALU.mult, r=["s5_t1"], w=["s5_pi%d" % (kk + 1)])
            k.ts("dve", W["nr"][:], W["pr0"][:], -1.0, None, ALU.add, r=["s5_pr0"], w=["s5_nr"])
            k.copy("dve", W["ni"][:], W["pi0"][:], r=["s5_pi0"], w=["s5_ni"])
            k.tt("dve", W["t1"][:], lre[:], lre[:], ALU.mult, r=["lre"], w=["s5_t1"])
            k.tt("dve", W["t2"][:], lim[:], lim[:], ALU.mult, r=["lim"], w=["s5_t2"])
            e2("den", "t1", "t2", ALU.add)
            k.op("dve", lambda: nc.vector.reciprocal(out=W["den"][:], in_=W["den"][:]), r=["s5_den"], w=["s5_den"])
            k.tt("dve", W["t1"][:], W["nr"][:], lre[:], ALU.mult, r=["s5_nr", "lre"], w=["s5_t1"])
            k.tt("dve", W["t2"][:], W["ni"][:], lim[:], ALU.mult, r=["s5_ni", "lim"], w=["s5_t2"])
            e2("cfr", "t1", "t2", ALU.add)
            e2("cfr", "cfr", "den", ALU.mult)
            k.tt("dve", W["t1"][:], W["ni"][:], lre[:], ALU.mult, r=["s5_ni", "lre"], w=["s5_t1"])
            k.tt("dve", W["t2"][:], W["nr"][:], lim[:], ALU.mult, r=["s5_nr", "lim"], w=["s5_t2"])
            e2("cfi", "t1", "t2", ALU.subtract)
            e2("cfi", "cfi", "den", ALU.mult)
            k.tt("dve", W["t1"][:], W["pr3"][:], x0r[:], ALU.mult, r=["s5_pr3", "x0r"], w=["s5_t1"])
            k.tt("dve", W["t2"][:], W["pi3"][:], x0i[:], ALU.mult, r=["s5_pi3", "x0i"], w=["s5_t2"])
            e2("lx0r", "t1", "t2", ALU.subtract)
            k.tt("dve", W["t1"][:], W["pi3"][:], x0r[:], ALU.mult, r=["s5_pi3", "x0r"], w=["s5_t1"])
            k.tt("dve", W["t2"][:], W["pr3"][:], x0i[:], ALU.mult, r=["s5_pr3", "x0i"], w=["s5_t2"])
            e2("lx0i", "t1", "t2", ALU.add)
            bre = k.sb([128, 32, 16], F32, "bre", stack=ph)
            bim = k.sb([128, 32, 16], F32, "bim", stack=ph)
            k.dma(bre[:], sb_d[0, :, :].rearrange("(dp gn) c -> gn dp c", gn=128), w=["bre"])
            k.dma(bim[:], sb_d[1, :, :].rearrange("(dp gn) c -> gn dp c", gn=128), w=["bim"])
            dcol = load_cols(sd_d[:, :], 4, "s5d", ph)
            bgl = load_cols(bg_d[:, :], 4, "bglu", ph)
            nst = [k.sb([128, 2, 32], F32, "nst%d" % i, stack=ph) for i in range(2)]
            wuf = k.sb([128, 2048], BF16, "wu", stack=ph)
            wu = wuf[:, :].rearrange("p (a b) -> p a b", a=8)
            for half in range(2):
                k.dma(wu, wine_d[:, half * 256:(half + 1) * 256].rearrange("(kc p) m -> p kc m", p=128), w=["wu"], q="pool")
                for mc2 in range(2):
                    mc = half * 2 + mc2
                    for b in range(NB):
                        sl = slice(b * 512, (b + 1) * 512)
                        p, pk = psum()
                        for kc in range(8):
                            k.mm(p[:, :], wu[:, kc, mc2 * 128:(mc2 + 1) * 128], hT[:, kc, sl], kc == 0, kc == 7,
                                 r=["wu", ("h", b, kc)], w=[pk])
                        k.copy("act", aoT[:, mc, sl], p[:, :], r=[pk], w=[("ao", mc)])
            k.barrier()
            s5tmp = k.sb([128, 2048], F32, "s5tmp", stack=ph)
            cbuf = []
            for c_ in range(2):
                if c_ == 0:
                    cbuf.append((hf32[:, 0:2048], hf32[:, 2048:4096], hf32[:, 4096:6144]))
                else:
                    cbuf.append((hf32[:, 6144:8192], hf32[:, 8192:10240], s5tmp[:, :]))
            bst = [[k.sb([128, 128], F32, "bst%d_%d" % (c_, i), stack=ph) for i in range(1)] for c_ in range(2)]
            cblks = [[k.sb([64, 128], F32, "cblk%d_%d" % (c_, i), stack=ph) for i in range(1)] for c_ in range(2)]
            tabs = [[k.sb([128, 128], BF16, "s5tab%d_%d" % (c_, j), stack=ph) for j in range(4)] for c_ in range(2)]
            bbs = [k.sb([128, 2, 16], F32, "bbar%d" % c_, stack=ph) for c_ in range(2)]
            bt1s = [k.sb([128, 16], F32, "bt1_%d" % c_, stack=ph) for c_ in range(2)]
            for c_ in range(2):
                for i in range(1):
                    k.memset("pool", cblks[c_][i][:], 0.0, w=["cblk%d_%d" % (c_, i)])
            k.ps_pool = [5, 6, 7]

            def rv(ap2, n0, n1):
                sub = ap2[:, n0:n1]
                return bass.AP(sub.tensor, sub.offset + (n1 - n0 - 1), [list(sub.ap[0]), [-1, n1 - n0]])

            def rv3(ap2, nseq, L, n0, n1):
                sub = ap2[:, 0:nseq * L].rearrange("p (s l) -> p s l", s=nseq)[:, :, n0:n1]
                return bass.AP(sub.tensor, sub.offset + (n1 - n0 - 1), [list(sub.ap[0]), list(sub.ap[1]), [-1, n1 - n0]])

            def fw3(ap2, nseq, L, n0, n1):
                return ap2[:, 0:nseq * L].rearrange("p (s l) -> p s l", s=nseq)[:, :, n0:n1]

            ycnt = {}

            def s5_item(c_, kc4, pp, yacc):
                d = c_
                pair = kc4 * 4 + pp
                col = d * 16 + pair
                tb = tabs[c_]
                tbk = ["s5tab%d_%d" % (c_, j) for j in range(4)]
                bb, bt1, bk_, b1k = bbs[c_], bt1s[c_], "bbar%d" % c_, "bt1_%d" % c_
                cfr, cfi = W["cfr"][:, col:col + 1], W["cfi"][:, col:col + 1]
                k.ts("dve", bt1[:], bim[:, col, :], cfi, None, ALU.mult, r=["bim", "s5_cfi"], w=[b1k])
                k.stt("dve", bb[:, 0, :], bre[:, col, :], cfr, bt1[:], ALU.mult, ALU.subtract, r=["bre", "s5_cfr", b1k], w=[bk_])
                k.ts("dve", bt1[:], bre[:, col, :], cfi, None, ALU.mult, r=["bre", "s5_cfi"], w=[b1k])
                k.stt("dve", bb[:, 1, :], bim[:, col, :], cfr, bt1[:], ALU.mult, ALU.add, r=["bim", "s5_cfr", b1k], w=[bk_])
                for ri in range(2):
                    bs_, bsk = bst[c_][0], "bst%d_0" % c_
                    k.memset("pool", bs_[:], 0.0, w=[bsk])
                    for gl in range(2):
                        c0 = (pp * 2 + gl) * 16
                        k.copy("dve", bs_[gl * 64:gl * 64 + 64, c0:c0 + 16], bb[gl * 64:gl * 64 + 64, ri, :], r=[bk_], w=[bsk])
                    p, pk = psum()
                    k.tr(p[:, 0:128], bs_[:], ident[:], r=[bsk, "ident"], w=[pk])
                    k.copy("act", tb[ri][:], p[:, 0:128], r=[pk], w=[tbk[ri]])
                yield
                for ri in range(2):
                    cblk, cbk = cblks[c_][0], "cblk%d_0" % c_
                    for gl in range(2):
                        k.dma(cblk[gl * 32:gl * 32 + 16, gl * 64:gl * 64 + 64],
                              sc_d[ri, (col * 2 + gl) * 16:(col * 2 + gl) * 16 + 16, :], w=[cbk])
                    p, pk = psum()
                    k.tr(p[:, 0:64], cblk[:], ident[0:64, 0:64], r=[cbk, "ident"], w=[pk])
                    k.memset("pool", tb[2 + ri][:], 0.0, w=[tbk[2 + ri]])
                    for gl in range(2):
                        c0 = (pp * 2 + gl) * 16
                        if ri == 0:
                            k.copy("act", tb[2][:, c0:c0 + 16], p[:, gl * 32:gl * 32 + 16], r=[pk], w=[tbk[2]])
                        else:
                            k.act(tb[3][:, c0:c0 + 16], p[:, gl * 32:gl * 32 + 16], AF.Identity, r=[pk], w=[tbk[3]], scale=-1.0)
                yield
                bre_, bim_, btm_ = cbuf[c_]
                kre, kim, ktm = ("sc", c_, 0), ("sc", c_, 1), ("sc", c_, 2)
                for seg in range(2):
                    L = LS if seg == 0 else LP
                    nseq = 1 if seg == 0 else 2
                    t0 = 0 if seg == 0 else LS
                    W_ = L * nseq
                    for b0 in range(0, W_, 512):
                        for ri in range(2):
                            p, pk = psum()
                            k.mm(p[:, :], tb[ri][:], aoT[:, kc4, t0 + b0:t0 + b0 + 512], True, True,
                                 r=[tbk[ri], ("ao", kc4)], w=[pk])
                            k.copy("act", cbuf[c_][ri][:, b0:b0 + 512], p[:, :], r=[pk], w=[("sc", c_, ri)])
                        yield
                    if seg == 0:
                        tcol = 0 if d == 0 else L - 1
                        k.tt("dve", bre_[:, tcol:tcol + 1], bre_[:, tcol:tcol + 1], W["lx0r"][:, col:col + 1], ALU.add,
                             r=[kre, "s5_lx0r"], w=[kre])
                        k.tt("dve", bim_[:, tcol:tcol + 1], bim_[:, tcol:tcol + 1], W["lx0i"][:, col:col + 1], ALU.add,
                             r=[kim, "s5_lx0i"], w=[kim])
                    nlev = 11 if seg == 0 else 8
                    for lv in range(nlev):
                        sh = 1 << lv
                        a_ = W["pr%d" % lv][:, col:col + 1]
                        b_ = W["pi%d" % lv][:, col:col + 1]
                        nb_ = W["pn%d" % lv][:, col:col + 1]
                        if d == 0:
                            V = lambda ap2, n0, n1: rv3(ap2, nseq, L, n0, n1)
                            hi, shd = (sh, L), (0, L - sh)
                        else:
                            V = lambda ap2, n0, n1: fw3(ap2, nseq, L, n0, n1)
                            hi, shd = (0, L - sh), (sh, L)
                        pk_ = ["s5_pr%d" % lv, "s5_pi%d" % lv, "s5_pn%d" % lv]
                        k.act(V(btm_, *hi), V(bre_, *shd), AF.Identity, r=[kre] + pk_, w=[ktm], scale=b_)
                        k.stt("dve", V(bre_, *hi), V(bre_, *shd), a_, V(bre_, *hi), ALU.mult, ALU.add, r=[kre, ktm] + pk_, w=[kre])
                        k.stt("dve", V(bre_, *hi), V(bim_, *shd), nb_, V(bre_, *hi), ALU.mult, ALU.add, r=[kre, kim] + pk_, w=[kre])
                        k.stt("dve", V(bim_, *hi), V(bim_, *shd), a_, V(bim_, *hi), ALU.mult, ALU.add, r=[kim] + pk_, w=[kim])
                        k.tt("pool", V(bim_, *hi), V(bim_, *hi), V(btm_, *hi), ALU.add, r=[kim, ktm], w=[kim])
                        yield
                    if seg == 1:
                        lc = L - 1 if d == 0 else 0
                        for ri in range(2):
                            k.copy("dve", nst[ri][:, :, col:col + 1],
                                   cbuf[c_][ri][:, 0:W_].rearrange("p (s l) -> p s l", s=2)[:, :, lc:lc + 1],
                                   r=[("sc", c_, ri)], w=["nst%d" % ri])
                    xb = btm_.bitcast(BF16)
                    for ri in range(2):
                        k.copy("act", xb[:, ri * 2048:ri * 2048 + W_], cbuf[c_][ri][:, 0:W_], r=[("sc", c_, ri), ktm], w=[ktm])
                    yield
                    for b0 in range(0, W_, 512):
                        bi = (t0 + b0) // 512
                        for ri in range(2):
                            n_ = ycnt.get((kc4, bi), 0)
                            ycnt[(kc4, bi)] = n_ + 1
                            k.mm(yacc[bi][0][:, :], tb[2 + ri][:], xb[:, ri * 2048 + b0:ri * 2048 + b0 + 512], n_ == 0, n_ == 15,
                                 r=[tbk[2 + ri], ktm], w=[yacc[bi][1]])
                    yield


            NBK = 320
            hbf = hflat
            creg = {"off": 0}

            def hcarve(ncols_f32, dt=F32):
                o = creg["off"]
                creg["off"] += ncols_f32
                assert creg["off"] <= 10240, creg["off"]
                ap = hf32[:, o:o + ncols_f32]
                return ap if dt == F32 else ap.bitcast(BF16)

            CHN = []
            for c_ in range(2):
                dct = {}
                dct["E"] = [[hcarve(64, BF16) for ri in range(2)] for _ in range(8)]
                dct["F"] = [[hcarve(64, BF16) for ri in range(2)] for _ in range(8)]
                dct["K"] = [hcarve(64, BF16) for _ in range(8)]
                dct["X"] = [hcarve(NBK) for _ in range(3)]
                dct["Xp"] = [hcarve(NBK // 2, BF16) for _ in range(2)]
                dct["Bp"] = [hcarve(128) for _ in range(2)]
                dct["Gp"] = [hcarve(128) for _ in range(2)]
                dct["T"] = [hcarve(128) for _ in range(2)]
                dct["TG"] = [hcarve(128) for _ in range(2)]
                dct["Bb"] = [hcarve(64, BF16) for _ in range(2)]
                dct["Cb"] = [hcarve(64, BF16) for _ in range(2)]
                CHN.append(dct)

            def cmul_inplace(eng, re_, im_, t1, t2, a_, b_, keys, rkeys=()):
                rk = list(keys) + list(rkeys)
                k.ts(eng, t1, im_, b_, None, ALU.mult, r=rk, w=keys)
                k.ts(eng, t2, re_, b_, None, ALU.mult, r=rk, w=keys)
                k.stt(eng, re_, re_, a_, t1, ALU.mult, ALU.subtract, r=rk, w=keys)
                k.stt(eng, im_, im_, a_, t2, ALU.mult, ALU.add, r=rk, w=keys)

            def s5_item2(c_, kc4, pp, yacc):
                d = c_
                pair = kc4 * 4 + pp
                col = d * 16 + pair
                H = CHN[c_]
                kt = "s5c%d_tab" % c_
                kBp, kGp, kTB, kTG, kBb, kCb = ["s5c%d_%s" % (c_, n_) for n_ in ("Bp", "Gp", "TB", "TG", "Bb", "Cb")]
                kE = lambda w__: "s5c%d_E%d" % (c_, w__)
                kF = lambda w__: "s5c%d_F%d" % (c_, w__)
                kK = lambda w__: "s5c%d_K%d" % (c_, w__)
                allE = [kE(w__) for w__ in range(8)]
                allF = [kF(w__) for w__ in range(8)]
                allK = [kK(w__) for w__ in range(8)]
                kx = "s5c%d_x" % c_
                bb, bt1, bk_, b1k = bbs[c_], bt1s[c_], "bbar%d" % c_, "bt1_%d" % c_
                cfr, cfi = W["cfr"][:, col:col + 1], W["cfi"][:, col:col + 1]
                a1, b1 = W["pr0"][:, col:col + 1], W["pi0"][:, col:col + 1]
                pk0 = ["s5_pr0", "s5_pi0"]
                k.ts("dve", bt1[:], bim[:, col, :], cfi, None, ALU.mult, r=["bim", "s5_cfi"], w=[b1k])
                k.stt("dve", bb[:, 0, :], bre[:, col, :], cfr, bt1[:], ALU.mult, ALU.subtract, r=["bre", "s5_cfr", b1k], w=[bk_])
                k.ts("dve", bt1[:], bre[:, col, :], cfi, None, ALU.mult, r=["bre", "s5_cfi"], w=[b1k])
                k.stt("dve", bb[:, 1, :], bim[:, col, :], cfr, bt1[:], ALU.mult, ALU.add, r=["bim", "s5_cfr", b1k], w=[bk_])
                for ri in range(2):
                    k.memset("pool", H["Bp"][ri], 0.0, w=[kBp])
                    for gl in range(2):
                        c0 = (pp * 2 + gl) * 16
                        k.copy("dve", H["Bp"][ri][gl * 64:gl * 64 + 64, c0:c0 + 16], bb[gl * 64:gl * 64 + 64, ri, :], r=[bk_], w=[kBp])
                for ri in range(2):
                    cblk, cbk = cblks[c_][0], "cblk%d_0" % c_
                    for gl in range(2):
                        k.dma(cblk[gl * 32:gl * 32 + 16, gl * 64:gl * 64 + 64],
                              sc_d[ri, (col * 2 + gl) * 16:(col * 2 + gl) * 16 + 16, :], w=[cbk])
                    p, pk = psum()
                    k.tr(p[:, 0:64], cblk[:], ident[0:64, 0:64], r=[cbk, "ident"], w=[pk])
                    k.memset("pool", H["Gp"][ri], 0.0, w=[kGp])
                    for gl in range(2):
                        c0 = (pp * 2 + gl) * 16
                        k.copy("act", H["Gp"][ri][:, c0:c0 + 16], p[:, gl * 32:gl * 32 + 16], r=[pk], w=[kGp])
                k.copy("act", H["Cb"][0], H["Gp"][0], r=[kGp], w=[kCb])
                k.act(H["Cb"][1], H["Gp"][1], AF.Identity, r=[kGp], w=[kCb], scale=-1.0)
                yield
                for w_ in range(8):
                    if w_ > 0:
                        cmul_inplace("dve", H["Bp"][0], H["Bp"][1], H["T"][0], H["T"][1], a1, b1, [kBp, kTB], pk0)
                    cmul_inplace("dve", H["Gp"][0], H["Gp"][1], H["TG"][0], H["TG"][1], a1, b1, [kGp, kTG], pk0)
                    for ri in range(2):
                        p, pk = psum()
                        k.tr(p[:, 0:128], H["Bp"][ri], ident[:], r=[kBp, "ident"], w=[pk])
                        k.copy("act", H["E"][w_][ri], p[:, 0:128], r=[pk], w=[kE(w_)])
                        k.copy("pool", H["Bb"][ri], H["Bp"][ri], r=[kBp], w=[kBb])
                    k.copy("pool", H["F"][w_][0], H["Gp"][0], r=[kGp], w=[kF(w_)])
                    k.act(H["F"][w_][1], H["Gp"][1], AF.Identity, r=[kGp], w=[kF(w_)], scale=-1.0)
                    p, pk = psum()
                    k.mm(p[:, 0:128], H["Bb"][0], H["Cb"][0], True, False, r=[kBb, kCb], w=[pk])
                    k.mm(p[:, 0:128], H["Bb"][1], H["Cb"][1], False, True, r=[kBb, kCb], w=[pk])
                    k.copy("act", H["K"][w_], p[:, 0:128], r=[pk], w=[kK(w_)])
                    if w_ % 2 == 1:
                        yield
                Xre, Xim, Xtm = H["X"]
                for seg in range(2):
                    J = 256 if seg == 0 else 64
                    t0 = 0 if seg == 0 else LS
                    j0 = 0 if seg == 0 else 256
                    for ri in range(2):
                        p, pk = psum()
                        for s_ in range(8):
                            w_ = (7 - s_) if d == 0 else s_
                            rhs = aoT[:, kc4, t0:t0 + 8 * J].rearrange("p (j s) -> p j s", s=8)[:, :, s_]
                            k.mm(p[:, 0:J], H["E"][w_][ri], rhs, s_ == 0, s_ == 7, r=[kE(w_), ("ao", kc4)], w=[pk])
                        k.copy("act", H["X"][ri][:, j0:j0 + J], p[:, 0:J], r=[pk], w=[kx])
                    yield
                jc = 0 if d == 0 else 255
                k.tt("dve", Xre[:, jc:jc + 1], Xre[:, jc:jc + 1], W["lx0r"][:, col:col + 1], ALU.add, r=[kx, "s5_lx0r"], w=[kx])
                k.tt("dve", Xim[:, jc:jc + 1], Xim[:, jc:jc + 1], W["lx0i"][:, col:col + 1], ALU.add, r=[kx, "s5_lx0i"], w=[kx])
                for seg in range(2):
                    nseq, Lb, j0 = (1, 256, 0) if seg == 0 else (2, 32, 256)
                    nlev = 8 if seg == 0 else 5
                    sub = lambda ap2: ap2[:, j0:j0 + nseq * Lb]
                    for lv in range(nlev):
                        sh = 1 << lv
                        pw = lv + 3
                        a_ = W["pr%d" % pw][:, col:col + 1]
                        b_ = W["pi%d" % pw][:, col:col + 1]
                        nb_ = W["pn%d" % pw][:, col:col + 1]
                        if d == 0:
                            V = lambda ap2, n0, n1: rv3(sub(ap2), nseq, Lb, n0, n1)
                            hi, shd = (sh, Lb), (0, Lb - sh)
                        else:
                            V = lambda ap2, n0, n1: fw3(sub(ap2), nseq, Lb, n0, n1)
                            hi, shd = (0, Lb - sh), (sh, Lb)
                        pk_ = ["s5_pr%d" % pw, "s5_pi%d" % pw, "s5_pn%d" % pw]
                        k.ts("dve", V(Xtm, *hi), V(Xre, *shd), b_, None, ALU.mult, r=[kx] + pk_, w=[kx])
                        k.stt("dve", V(Xre, *hi), V(Xre, *shd), a_, V(Xre, *hi), ALU.mult, ALU.add, r=[kx] + pk_, w=[kx])
                        k.stt("dve", V(Xre, *hi), V(Xim, *shd), nb_, V(Xre, *hi), ALU.mult, ALU.add, r=[kx] + pk_, w=[kx])
                        k.stt("dve", V(Xim, *hi), V(Xim, *shd), a_, V(Xim, *hi), ALU.mult, ALU.add, r=[kx] + pk_, w=[kx])
                        k.tt("dve", V(Xim, *hi), V(Xim, *hi), V(Xtm, *hi), ALU.add, r=[kx], w=[kx])
                    yield
                for ri in range(2):
                    pv = H["X"][ri][:, 256:320].rearrange("p (s l) -> p s l", s=2)
                    lc = 31 if d == 0 else 0
                    k.copy("dve", nst[ri][:, :, col:col + 1], pv[:, :, lc:lc + 1], r=[kx], w=["nst%d" % ri])
                    xp = H["Xp"][ri]
                    x0col = (x0r if ri == 0 else x0i)[:, col:col + 1]
                    if d == 0:
                        k.copy("act", xp[:, 1:256], H["X"][ri][:, 0:255], r=[kx], w=[kx])
                        k.copy("dve", xp[:, 0:1], x0col, r=[kx, "x0r", "x0i"], w=[kx])
                        xpv = xp[:, 256:320].rearrange("p (s l) -> p s l", s=2)
                        k.copy("act", xpv[:, :, 1:32], pv[:, :, 0:31], r=[kx], w=[kx])
                        k.memset("pool", xpv[:, :, 0:1], 0.0, w=[kx])
                    else:
                        k.copy("act", xp[:, 0:255], H["X"][ri][:, 1:256], r=[kx], w=[kx])
                        k.copy("dve", xp[:, 255:256], x0col, r=[kx, "x0r", "x0i"], w=[kx])
                        xpv = xp[:, 256:320].rearrange("p (s l) -> p s l", s=2)
                        k.copy("act", xpv[:, :, 0:31], pv[:, :, 1:32], r=[kx], w=[kx])
                        k.memset("pool", xpv[:, :, 31:32], 0.0, w=[kx])
                yield
                for bi in range(5):
                    jb = bi * 64
                    tb0 = bi * 512
                    yb, ybk = yacc[bi]
                    yv = yb[:, :].rearrange("p (j s) -> p j s", s=8)
                    uv = lambda off: aoT[:, kc4, tb0 + off:tb0 + off + 505].rearrange("p (j s) -> p j s", s=8) if False else None
                    mms = []
                    for s_ in range(8):
                        w_ = s_ if d == 0 else 7 - s_
                        for ri in range(2):
                            mms.append((yv[:, :, s_], H["F"][w_][ri], H["Xp"][ri][:, jb:jb + 64]))
                    ub = aoT[:, kc4, tb0:tb0 + 512].rearrange("p (j s) -> p j s", s=8)
                    for tau in range(8):
                        if d == 0:
                            mms.append((yv[:, :, tau:8], H["K"][tau], ub[:, :, 0:8 - tau]))
                        else:
                            mms.append((yv[:, :, 0:8 - tau], H["K"][tau], ub[:, :, tau:8]))
                    for (o_, l_, r_) in mms:
                        n_ = ycnt.get((kc4, bi), 0)
                        ycnt[(kc4, bi)] = n_ + 1
                        k.mm(o_, l_, r_, n_ == 0, n_ == 8 * 24 - 1, r=allF + allK + [kx, ("ao", kc4)], w=[ybk])
                    yield

            def s5_chain(c_, kc4, yacc):
                for pp in range(4):
                    yield from s5_item2(c_, kc4, pp, yacc)

            for kc4 in range(4):
                yacc = [(ps[b], ("ps", b)) for b in range(5)]
                gens = [s5_chain(0, kc4, yacc), s5_chain(1, kc4, yacc)]
                while gens:
                    for g_ in list(gens):
                        try:
                            next(g_)
                        except StopIteration:
                            gens.remove(g_)
                for b in range(NB):
                    sl = slice(b * 512, (b + 1) * 512)
                    yv, y2, y3 = tmpb[0], tmpb[1], tmpb[2]
                    k.stt("dve", yv[:], aoT[:, kc4, sl], dcol[:, kc4:kc4 + 1], yacc[b][0][:, :], ALU.mult, ALU.add,
                          r=[("ao", kc4), "s5d", yacc[b][1]], w=["ntmp0"])
                    k.tt("pool", y2[:], yv[:], yv[:], ALU.mult, r=["ntmp0"], w=["ntmp1"])
                    k.ts("pool", y2[:], y2[:], 0.044715, 1.0, ALU.mult, ALU.add, r=["ntmp1"], w=["ntmp1"])
                    k.tt("pool", y2[:], y2[:], yv[:], ALU.mult, r=["ntmp1", "ntmp0"], w=["ntmp1"])
                    k.act(y3[:], y2[:], AF.Sigmoid, r=["ntmp1"], w=["ntmp2"], scale=1.5957691216057308)
                    k.tt("dve", aoT[:, kc4, sl], yv[:], y3[:], ALU.mult, r=["ntmp0", "ntmp2"], w=[("ao", kc4)])
            k.ps_pool = list(range(8))
            wg = wuf[:, :].rearrange("p (a b) -> p a b", a=4)
            k.dma(wg, wglu_d[:, :].rearrange("(kc p) m -> p kc m", p=128), w=["wu"], q="pool")
            for b in range(NB):
                sl = slice(b * 512, (b + 1) * 512)
                pg = []
                for mc in range(4):
                    p, pk = psum()
                    pg.append((p, pk))
                    for kc in range(4):
                        k.mm(p[:, :], wg[:, kc, mc * 128:(mc + 1) * 128], aoT[:, kc, sl], kc == 0, kc == 3,
                             r=["wu", ("ao", kc)], w=[pk])
                for mc in range(4):
                    p, pk = pg[mc]
                    k.act(tmpb[mc % 3][:], p[:, :], AF.Sigmoid, r=[pk, "bglu"], w=["ntmp%d" % (mc % 3)], bias=bgl[:, mc:mc + 1], scale=1.0)
                    k.tt("dve", aoT[:, mc, sl], aoT[:, mc, sl], tmpb[mc % 3][:], ALU.mult, r=[("ao", mc), "ntmp%d" % (mc % 3)], w=[("ao", mc)])
            for ri, dst_d in ((0, nsre_d), (1, nsim_d)):
                p, pk = psum()
                k.tr(p[0:64, 0:128], nst[ri][:, :, :].rearrange("p s c -> p (s c)"), ident[:], r=["nst%d" % ri, "ident"], w=[pk])
                stt_ = tmpb[ri][0:64, 0:128]
                k.copy("dve", stt_, p[0:64, 0:128], r=[pk], w=["ntmp%d" % ri])
                k.dma(dst_d[:, :], stt_, r=["ntmp%d" % ri])
            k.barrier()
            ph.close()
            ph = ph_keep
        wo = [k.sb([128, 8, 128], BF16, "wo%d" % i, stack=ph) for i in range(2)]
        for mc in range(8):
            wt, wk_ = wo[mc % 2], "wo%d" % (mc % 2)
            k.dma(wt[:], woute_d[:, mc * 128:(mc + 1) * 128].rearrange("(kc p) m -> p kc m", p=128), w=[wk_], q="pool")
            for b in range(NB):
                cond = 0 if b < 4 else 1
                sl = slice(b * 512, (b + 1) * 512)
                p, pk = psum()
                kcs = list(range(4, 8)) if stage == "noS5" else list(range(8))
                for j, kc in enumerate(kcs):
                    src = aoT[:, kc, sl] if kc < 4 else boT[:, kc - 4, sl]
                    k.mm(p[:, :], wt[:, kc, :], src, j == 0, j == len(kcs) - 1,
                         r=[wk_, ("ao", kc) if kc < 4 else ("bo", kc - 4)], w=[pk])
                k.stt("dve", xT[:, mc, sl], p[:, :], gate_ap(0, 0, mc, cond), xT[:, mc, sl], ALU.mult, ALU.add,
                      r=[pk, "modv0", xk(b, mc)], w=[xk(b, mc)])
        k.barrier()
        ph.close()

    SL = {"m1a": 1, "m1b": 2, "m1c": 3, "m1d": 4, "m1e": 5}.get(stage[:3], 9)
    PL = int(stage[3:]) if (stage.startswith("m1d") and len(stage) > 3) else 9

    def mixer1():
        ph = ExitStack()
        for b in range(NB):
            norm_block(1, 0, b)
        SEGS = [(0, 16), (16, 2), (18, 2)]
        xflat = xT[:, :, :].rearrange("p c t -> p (c t)")
        allx = [xk(b, c) for b in range(NB) for c in range(8)]
        k.dma(xsp_d[:, :], xflat, r=allx)
        k.barrier()
        carve = {"off": 0}

        def cv(shape, dt=F32):
            n = 1
            for s_ in shape[1:]:
                n *= s_
            nf = n if dt == F32 else (n + 1) // 2
            ap = xflat[:, carve["off"]:carve["off"] + nf]
            carve["off"] += nf
            assert carve["off"] <= 8 * T, carve["off"]
            if dt != F32:
                ap = ap.bitcast(dt)[:, 0:n]
            if len(shape) == 3:
                ap = ap.rearrange("p (a b) -> p a b", a=shape[1])
            return ap[0:shape[0]]

        class _CV:
            def __init__(self, ap):
                self.ap = ap
            def __getitem__(self, key):
                return self.ap[key]

        def cvt_(shape, dt=F32):
            return _CV(cv(shape, dt))
        def cst(shape, name):
            return k.sb(shape, F32, name, stack=ph)
        d128 = cst([128, 128], "d128")
        k.op("pool", lambda: nc.gpsimd.iota(d128[:], pattern=[[1, 128]], base=0, channel_multiplier=-1,
                                              allow_small_or_imprecise_dtypes=True), w=["d128"])
        blk = cst([128, 128], "blk")
        k.memset("dve", blk[:], 0.0, w=["blk"])
        k.memset("dve", blk[0:64, 0:64], 1.0, w=["blk"])
        k.memset("dve", blk[64:128, 64:128], 1.0, w=["blk"])
        TRI = []
        for d in range(2):
            t_ = cst([128, 128], "tri%d" % d)
            k.ts("dve", t_[:], d128[:], 0.0, None, ALU.is_ge if d == 0 else ALU.is_le, r=["d128"], w=["tri%d" % d])
            k.tt("dve", t_[:], t_[:], blk[:], ALU.mult, r=["tri%d" % d, "blk"], w=["tri%d" % d])
            TRI.append(t_)
        HALF = []
        for hb in range(2):
            t_ = cst([128, 128], "half%d" % hb)
            k.memset("dve", t_[:], 0.0, w=["half%d" % hb])
            k.memset("dve", t_[hb * 64:hb * 64 + 64, :], 1.0, w=["half%d" % hb])
            HALF.append(t_)
        def Rr(ap):
            return ap
        SA128, BIAS_r, NSTR128, LE_r = [], [], [], []
        for d in range(2):
            t_ = cst([128, 128], "sa128_%d" % d)
            k.ts("dve", t_[:], d128[:], 0.0, None, ALU.is_lt if d == 0 else ALU.is_gt, r=["d128"], w=["sa128_%d" % d])
            k.tt("dve", t_[:], t_[:], blk[:], ALU.mult, r=["sa128_%d" % d, "blk"], w=["sa128_%d" % d])
            SA128.append(t_)
            t_ = cst([128, 128], "ler%d" % d)
            k.copy("dve", Rr(t_[:]), TRI[d][:], r=["tri%d" % d], w=["ler%d" % d])
            LE_r.append(t_)
            t_ = cst([128, 128], "biasr%d" % d)
            k.ts("dve", Rr(t_[:]), TRI[d][:], -1.0, 30000.0, ALU.add, ALU.mult, r=["tri%d" % d], w=["biasr%d" % d])
            BIAS_r.append(t_)
            t_ = cst([128, 128], "nstr128_%d" % d)
            k.ts("dve", t_[:], d128[:], 0.0, None, ALU.is_gt if d == 0 else ALU.is_lt, r=["d128"], w=["nstr128_%d" % d])
            k.stt("dve", t_[:], t_[:], -1.0, blk[:], ALU.mult, ALU.mult, r=["nstr128_%d" % d, "blk"], w=["nstr128_%d" % d])
            NSTR128.append(t_)
        identR = cst([128, 128], "identR")
        k.copy("dve", Rr(identR[:]), ident[:], r=["ident"], w=["identR"])
        b32 = cst([128, 128], "b32")
        k.memset("dve", b32[:], 0.0, w=["b32"])
        for q4 in range(4):
            k.memset("dve", b32[q4 * 32:q4 * 32 + 32, q4 * 32:q4 * 32 + 32], 1.0, w=["b32"])
        ident_bf = k.sb([128, 128], BF16, "ident_bf", stack=ph)
        k.copy("dve", ident_bf[:], ident[:], r=["ident"], w=["ident_bf"])
        ones_f = cst([1, 128], "ones_f")
        k.memset("dve", ones_f[:], 1.0, w=["ones_f"])
        grow = cst([1, 32], "grow")
        k.dma(grow[0:1, 0:16], alog_d[0:1, :], w=["grow"])
        k.dma(grow[0:1, 16:32], dtb_d[0:1, :], w=["grow"])
        p, pk = psum()
        k.mm(p[:, 0:32], ones_f[0:1, :], grow[0:1, :], True, True, r=["ones_f", "grow"], w=[pk])
        gcb = cst([128, 32], "gcb")
        k.act(gcb[:, 0:16], p[:, 0:16], AF.Exp, r=[pk], w=["gcb"])
        k.ts("dve", gcb[:, 0:16], gcb[:, 0:16], -1.0, None, ALU.mult, r=["gcb"], w=["gcb"])
        k.copy("dve", gcb[:, 16:32], p[:, 16:32], r=[pk], w=["gcb"])
        cw = load_cols(cw_d[:, :], 120, "convw", ph)
        ngc = load_cols(ng_d[:, :], 1, "dnng", ph)
        wgp = k.sb([128, 8, 32], BF16, "wgp", stack=ph)
        k.dma(wgp[:], wino_d[:, 4096:4128].rearrange("(kc p) m -> p kc m", p=128), w=["wgp"], q="pool")
        TS = {}
        for nme in ("g", "sqb", "gam", "egam", "c1", "kdec"):
            TS[nme] = cvt_([128, 20, 16])
        etot = cvt_([128, 40, 16])
        for t in (range(20) if SL >= 2 else []):
            p, pk = psum()
            for kc in range(8):
                k.mm(p[:, 0:32], hT[:, kc, t * 128:(t + 1) * 128], wgp[:, kc, :], kc == 0, kc == 7,
                     r=["wgp", ("h", t // 4, kc)], w=[pk])
            g_, sq_ = TS["g"][:, t, :], TS["sqb"][:, t, :]
            k.tt("dve", g_, p[:, 0:16], gcb[:, 16:32], ALU.add, r=[pk, "gcb"], w=["ts_g"])
            k.act(g_, g_, AF.Exp, r=["ts_g"], w=["ts_g"])
            k.act(g_, g_, AF.Ln, r=["ts_g"], w=["ts_g"], bias=1.0, scale=1.0)
            k.tt("dve", g_, g_, gcb[:, 0:16], ALU.mult, r=["ts_g", "gcb"], w=["ts_g"])
            k.act(sq_, p[:, 16:32], AF.Sigmoid, r=[pk], w=["ts_sqb"])
            k.act(sq_, sq_, AF.Sqrt, r=["ts_sqb"], w=["ts_sqb"])
            p2, p2k = psum()
            for d in range(2):
                k.mm(p2[:, d * 8:d * 8 + 8], TRI[d][:], TS["g"][:, t, d * 8:d * 8 + 8], True, True, r=["tri%d" % d, "ts_g"], w=[p2k])
            k.mm(p2[:, 16:32], blk[:], TS["g"][:, t, :], True, True, r=["blk", "ts_g"], w=[p2k])
            for hb in range(2):
                k.mm(p2[:, 32 + hb * 16:48 + hb * 16], HALF[hb][:], TS["g"][:, t, :], True, True, r=["half%d" % hb, "ts_g"], w=[p2k])
            k.copy("dve", TS["gam"][:, t, :], p2[:, 0:16], r=[p2k], w=["ts_gam"])
            k.act(TS["egam"][:, t, :], p2[:, 0:16], AF.Exp, r=[p2k], w=["ts_egam"])
            k.tt("dve", TS["kdec"][:, t, :], p2[:, 16:32], TS["gam"][:, t, :], ALU.subtract, r=[p2k, "ts_gam"], w=["ts_kdec"])
            k.act(TS["kdec"][:, t, :], TS["kdec"][:, t, :], AF.Exp, r=["ts_kdec"], w=["ts_kdec"])
            k.act(etot[:, 2 * t:2 * t + 2, :], p2[:, 32:64].rearrange("p (a c) -> p a c", a=2), AF.Exp, r=[p2k], w=["etot"])
            k.stt("dve", TS["c1"][:, t, :], TS["sqb"][:, t, :], -1.0, TS["egam"][:, t, :], ALU.mult, ALU.mult,
                  r=["ts_sqb", "ts_egam"], w=["ts_c1"])
        qhT = cvt_([128, T])
        khT = cvt_([128, T])
        vtm = cvt_([128, 20, 128])
        szT = cvt_([128, T], BF16)
        og_all = k.sb([128, 8, T], BF16, "og_all", stack=ph)
        otm = cvt_([128, 20, 128])
        Sst = [k.sb([128, 128], F32, "Sst%d" % i, stack=ph) for i in range(2)]
        NR = 2
        CH = {}
        Sbf = [cvt_([128, 128], BF16) for i in range(2)]
        for nme in ("kdk", "ksq", "kpT", "sv", "gM", "decT", "Bt", "Bn", "attnT", "Tt", "P", "Pt", "tmpm",
                    "Btd", "Bnd", "Bno", "M1", "Tdn", "qb", "kb"):
            shp = [128, 128]
            if nme in ("kpT", "gM", "Bt", "Bn", "Btd", "Bnd", "Bno", "P", "Pt", "Tt", "Tdn", "M1"):
                CH[nme] = [k.sb(shp, F32, "chr_%s%d" % (nme, i), stack=ph) for i in range(NR)]
            elif nme in ("attnT", "kdk", "qb", "kb"):
                CH[nme] = [cvt_(shp, BF16) for i in range(NR)]
            else:
                CH[nme] = [cvt_(shp) for i in range(NR)]
        rhs_t = [cvt_([128, 128]) for i in range(2)]
        vn_t = [cvt_([128, 128], BF16) for i in range(2)]
        o2_t = [cvt_([128, 128]) for i in range(2)]
        on_t = [cvt_([128, 128]) for i in range(2)]
        sm1 = [k.sb([128, 4], F32, "gsm%d" % i, stack=ph) for i in range(2)]
        rot = {"pre": 0, "seq": 0, "fin": 0}
        ppool = {0: [0, [0, 1, 2, 3]], 1: [0, [4, 5, 6, 7]]}

        def psd(d):
            st = ppool[d]
            st[0] = (st[0] + 1) % 4
            i_ = st[1][st[0]]
            return ps[i_], ("ps", i_)

        for h in (range(8) if SL >= 3 else []):
            if h == 0:
                wpn = cvt_([128, 8, 256], BF16)
                xpd = cvt_([128, T + 12], BF16)
                dg = [cvt_([128, 128], BF16) for j in range(5)]
            def load_panel(w_, hq=None):
                hq = h if hq is None else hq
                slot = w_ % 2
                k.dma(wpn[:, :, slot * 128:(slot + 1) * 128],
                      wino_d[:, w_ * 1024 + hq * 128:w_ * 1024 + (hq + 1) * 128].rearrange("(kc p) m -> p kc m", p=128),
                      w=["gwpn%d" % slot], q="pool")
            if h == 0:
                load_panel(3)
                load_panel(0)
            k.memset("pool", xpd[:], 0.0, w=["gxpd"])
            for b in range(NB):
                sl = slice(b * 512, (b + 1) * 512)
                p, pk = psum()
                for kc in range(8):
                    k.mm(p[:, :], wpn[:, kc, 128:256], hT[:, kc, sl], kc == 0, kc == 7, r=["gwpn1", ("h", b, kc)], w=[pk])
                k.act(szT[:, sl], p[:, :], AF.Silu, r=[pk], w=["szT"])
            load_panel(1)
            for which in range(3):
                wsl = slice((which % 2) * 128, (which % 2) * 128 + 128)
                for b in range(NB):
                    sl = slice(b * 512, (b + 1) * 512)
                    p, pk = psum()
                    for kc in range(8):
                        k.mm(p[:, :], wpn[:, kc, wsl], hT[:, kc, sl], kc == 0, kc == 7,
                             r=["gwpn%d" % (which % 2), ("h", b, kc)], w=[pk])
                    if b < 4:
                        k.copy("act", xpd[:, 2 + b * 512:2 + (b + 1) * 512], p[:, :], r=[pk], w=["gxpd"])
                    else:
                        k.copy("act", xpd[:, LS + 6:LS + 6 + LP], p[:, 0:LP], r=[pk], w=["gxpd"])
                        k.copy("act", xpd[:, LS + LP + 10:LS + 2 * LP + 10], p[:, LP:2 * LP], r=[pk], w=["gxpd"])
                if which == 0:
                    load_panel(2)
                if which == 2 and h < 7:
                    load_panel(0, h + 1)
                    load_panel(3, h + 1)
                for j in range(5):
                    cidx = j * 24 + which * 8 + h
                    k.ts("dve", dg[j][:], ident_bf[:], cw[:, cidx:cidx + 1], None, ALU.mult, r=["ident_bf", "convw"], w=["gdg%d" % j])
                for b in range(NB):
                    p, pk = psum()
                    pieces = [(0, 512, b * 512)] if b < 4 else [(0, LP, LS + 4), (LP, LP, LS + LP + 8)]
                    for (c0, n_, x0) in pieces:
                        for j in range(5):
                            k.mm(p[:, c0:c0 + n_], dg[j][:], xpd[:, x0 + j:x0 + j + n_], j == 0, j == 4,
                                 r=["gdg%d" % j, "gxpd"], w=[pk])
                    sl = slice(b * 512, (b + 1) * 512)
                    if which < 2:
                        dst = qhT if which == 0 else khT
                        dk_ = "qhT" if which == 0 else "khT"
                        k.act(dst[:, sl], p[:, :], AF.Silu, r=[pk], w=[dk_])
                        i = cnt["sq"] % 2
                        cnt["sq"] += 1
                        k.act(sqb[i][:], dst[:, sl], AF.Square, r=[dk_], w=["sq%d" % i])
                        pn, pnk = psum()
                        k.mm(pn[:, :], ones_bf[:], sqb[i][:], True, True, r=["ones", "sq%d" % i], w=[pnk])
                        k.act(tmpb[0][:], pn[:, :], AF.Sqrt, r=[pnk], w=["ntmp0"], bias=EPS, scale=1.0)
                        k.op("dve", lambda: nc.vector.reciprocal(out=tmpb[0][:], in_=tmpb[0][:]), r=["ntmp0"], w=["ntmp0"])
                        k.stt("dve", dst[:, sl], dst[:, sl], (128.0 ** -0.5) if which == 0 else 1.0, tmpb[0][:], ALU.mult, ALU.mult,
                              r=[dk_, "ntmp0"], w=[dk_])
                    else:
                        k.act(tmpb[1][:], p[:, :], AF.Silu, r=[pk], w=["ntmp1"])
                        pt_, ptk = psum()
                        for tt_ in range(4):
                            k.tr(pt_[:, tt_ * 128:(tt_ + 1) * 128], tmpb[1][:, tt_ * 128:(tt_ + 1) * 128], ident[:],
                                 r=["ntmp1", "ident"], w=[ptk])
                        k.copy("dve", vtm[:, b * 4:b * 4 + 4, :], pt_[:, :].rearrange("p (t c) -> p t c", t=4), r=[ptk], w=["gvtm"])
            def precompute(d, t):
                col = d * 8 + h
                i = d
                B_ = {n_: (CH[n_][i], "ch_%s%d" % (n_, i)) for n_ in CH}
                T_ = lambda n_: B_[n_][0][:]
                K_ = lambda n_: B_[n_][1]
                tok = slice(t * 128, (t + 1) * 128)
                sc = lambda n_: TS[n_][:, t, col:col + 1]
                k.ts("pool", T_("sv"), vtm[:, t, :], sc("sqb"), None, ALU.mult, r=["gvtm", "ts_sqb"], w=[K_("sv")])
                p, pk = psd(d)
                k.tr(p[:, 0:128], khT[:, tok], ident[:], r=["khT", "ident"], w=[pk])
                k.act(T_("kdk"), p[:, 0:128], AF.Identity, r=[pk, "ts_kdec"], w=[K_("kdk")], scale=sc("kdec"))
                k.copy("act", T_("qb"), qhT[:, tok], r=["qhT"], w=[K_("qb")])
                k.copy("pool", T_("kb"), khT[:, tok], r=["khT"], w=[K_("kb")])
                k.ts("dve", Rr(T_("gM")), SA128[d][:], TS["g"][:, t, col:col + 1], None, ALU.mult, r=["sa128_%d" % d, "ts_g"], w=[K_("gM")])
                yield
                pD, pDk = psd(d)
                k.mm(pD[:, 0:128], Rr(T_("gM")), Rr(LE_r[d][:]), True, True, r=[K_("gM"), "ler%d" % d], w=[pDk])
                pK, pKk = psd(d)
                k.mm(pK[:, 0:128], khT[:, tok], khT[:, tok], True, True, r=["khT"], w=[pKk])
                pQ, pQk = psd(d)
                k.mm(pQ[:, 0:128], T_("kb"), T_("qb"), True, True, r=[K_("kb"), K_("qb")], w=[pQk])
                k.act(T_("decT"), pD[:, 0:128], AF.Exp, r=[pDk], w=[K_("decT")])
                yield
                k.tt("dve", T_("tmpm"), pK[:, 0:128], T_("decT"), ALU.mult, r=[pKk, K_("decT")], w=[K_("tmpm")])
                k.stt("dve", T_("ksq"), T_("tmpm"), sc("sqb"), NSTR128[d][:], ALU.mult, ALU.mult,
                      r=[K_("tmpm"), "ts_sqb", "nstr128_%d" % d], w=[K_("ksq")])
                k.tt("dve", T_("tmpm"), pQ[:, 0:128], T_("decT"), ALU.mult, r=[pQk, K_("decT"), K_("ksq")], w=[K_("tmpm")])
                k.tt("pool", T_("attnT"), T_("tmpm"), TRI[d][:], ALU.mult, r=[K_("tmpm"), "tri%d" % d], w=[K_("attnT")])
                yield
                pB, pBk = psd(d)
                k.tr(pB[:, 0:128], T_("ksq"), ident[:], r=[K_("ksq"), "ident"], w=[pBk])
                k.act(T_("Bn"), pB[:, 0:128], AF.Identity, r=[pBk, "ts_sqb"], w=[K_("Bn")], scale=sc("sqb"))
                pB2, pB2k = psd(d)
                k.tr(pB2[:, 0:128], T_("Bn"), ident[:], r=[K_("Bn"), "ident"], w=[pB2k])
                k.copy("dve", T_("Bt"), pB2[:, 0:128], r=[pB2k], w=[K_("Bt")])
                k.tt("dve", Rr(T_("Btd")), T_("Bt"), b32[:], ALU.mult, r=[K_("Bt"), "b32"], w=[K_("Btd")])
                k.tt("dve", Rr(T_("Tt")), T_("Btd"), ident[:], ALU.add, r=[K_("Btd"), "ident"], w=[K_("Tt")])
                yield
                k.tt("pool", Rr(T_("Bnd")), T_("Bn"), b32[:], ALU.mult, r=[K_("Bn"), "b32"], w=[K_("Bnd")])
                k.tt("pool", Rr(T_("Bno")), T_("Bn"), T_("Bnd"), ALU.subtract, r=[K_("Bn"), K_("Bnd")], w=[K_("Bno")])
                cur = ("Bnd", "Btd")
                nxt = ("P", "Pt")
                NLV = 4
                for m in range(1, NLV + 1):
                    pP, pPk = psd(d)
                    k.mm(pP[:, 0:128], Rr(T_(cur[1])), Rr(T_(cur[0])), True, True, r=[K_(cur[0]), K_(cur[1])], w=[pPk])
                    k.copy("act", Rr(T_(nxt[0])), pP[:, 0:128], r=[pPk], w=[K_(nxt[0])])
                    if m < NLV:
                        pT_, pTk_ = psd(d)
                        k.mm(pT_[:, 0:128], Rr(T_(cur[0])), Rr(T_(cur[1])), True, True, r=[K_(cur[0]), K_(cur[1])], w=[pTk_])
                        k.copy("dve", Rr(T_(nxt[1])), pT_[:, 0:128], r=[pTk_], w=[K_(nxt[1])])
                    yield
                    pU, pUk = psd(d)
                    k.mm(pU[:, 0:128], Rr(T_(nxt[0])), Rr(T_("Tt")), True, True, r=[K_(nxt[0]), K_("Tt")], w=[pUk])
                    k.tt("dve", Rr(T_("Tt")), T_("Tt"), pU[:, 0:128], ALU.add, r=[K_("Tt"), pUk], w=[K_("Tt")])
                    cur, nxt = nxt, cur
                    yield
                pA_, pAk_ = psd(d)
                k.tr(pA_[:, 0:128], T_("Tt"), ident[:], r=[K_("Tt"), "ident"], w=[pAk_])
                k.copy("act", Rr(T_("Tdn")), pA_[:, 0:128], r=[pAk_], w=[K_("Tdn")])
                pM, pMk = psd(d)
                k.mm(pM[:, 0:128], Rr(T_("Bno")), Rr(T_("Tt")), True, True, r=[K_("Bno"), K_("Tt")], w=[pMk])
                k.copy("dve", Rr(T_("M1")), pM[:, 0:128], r=[pMk], w=[K_("M1")])
                yield
                pM2, pM2k = psd(d)
                k.mm(pM2[:, 0:128], Rr(T_("Tdn")), Rr(T_("M1")), True, True, r=[K_("Tdn"), K_("M1")], w=[pM2k])
                k.tt("dve", Rr(T_("Tt")), T_("Tt"), pM2[:, 0:128], ALU.add, r=[K_("Tt"), pM2k], w=[K_("Tt")])
                yield
                return B_

            def seq_step(d, t, hb, B_, S, Sk, first_dir_write):
                col = d * 8 + h
                rw = slice(hb * 64, hb * 64 + 64)
                t0 = t * 128 + hb * 64
                i = d
                sc = lambda n_: TS[n_][rw, t, col:col + 1]
                p1, p1k = psd(d)
                k.mm(p1[rw, 0:128], khT[:, t0:t0 + 64], S[:], True, True, r=["khT", Sk], w=[p1k])
                k.stt("dve", rhs_t[i][rw, :], p1[rw, 0:128], sc("c1"), B_["sv"][0][rw, :], ALU.mult, ALU.add,
                      r=[p1k, "ts_c1", B_["sv"][1]], w=["rhsp%d" % i])
                yield
                p2, p2k = psd(d)
                k.mm(p2[rw, 0:128], B_["Tt"][0][rw, rw], rhs_t[i][rw, :], True, True, r=[B_["Tt"][1], "rhsp%d" % i], w=[p2k])
                k.act(vn_t[i][rw, :], p2[rw, 0:128], AF.Identity, r=[p2k, "ts_sqb"], w=["vn%d" % i], scale=sc("sqb"))
                yield
                p3, p3k = psd(d)
                p4, p4k = psd(d)
                k.mm(p3[rw, 0:128], B_["qb"][0][:, hb * 64:hb * 64 + 64], Sbf[d][:], True, True, r=[B_["qb"][1], "Sbf%d" % d], w=[p3k])
                k.mm(p4[rw, 0:128], B_["attnT"][0][rw, rw], vn_t[i][rw, :], True, True, r=[B_["attnT"][1], "vn%d" % i], w=[p4k])
                k.copy("act", o2_t[i][rw, :], p4[rw, 0:128], r=[p4k], w=["o2t%d" % i])
                yield
                if False:
                    k.stt("dve", otm[rw, t, :], p3[rw, 0:128], sc("egam"), o2_t[i][rw, :], ALU.mult, ALU.add,
                          r=[p3k, "ts_egam", "o2t%d" % i], w=[("otm", t, hb)])
                else:
                    k.stt("dve", o2_t[i][rw, :], p3[rw, 0:128], sc("egam"), o2_t[i][rw, :], ALU.mult, ALU.add,
                          r=[p3k, "ts_egam", "o2t%d" % i], w=["o2t%d" % i])
                    k.tt("pool", otm[rw, t, :], otm[rw, t, :], o2_t[i][rw, :], ALU.add, r=[("otm", t, hb), "o2t%d" % i], w=[("otm", t, hb)])
                p5, p5k = psd(d)
                k.mm(p5[:, 0:128], B_["kdk"][0][rw, :], vn_t[i][rw, :], True, True, r=[B_["kdk"][1], "vn%d" % i], w=[p5k])
                k.stt("dve", S[:], S[:], etot[:, 2 * t + hb, col:col + 1], p5[:, 0:128], ALU.mult, ALU.add,
                      r=[Sk, "etot", p5k], w=[Sk])
                k.copy("act", Sbf[d][:], S[:], r=[Sk], w=["Sbf%d" % d])
                yield

            k.memset("pool", otm[:, :, :], 0.0, w=[("otm", t_, hb_) for t_ in range(20) for hb_ in range(2)])

            def dir_chain(d):
                col = d * 8 + h
                for si, (tf, nt) in enumerate(SEGS):
                    S, Sk = Sst[d], "Sst%d" % d
                    if si == 0:
                        k.dma(S[:], sdn_d[d * 8 + h, :, :], w=[Sk])
                    else:
                        k.memset("pool", S[:], 0.0, w=[Sk])
                    k.copy("act", Sbf[d][:], S[:], r=[Sk], w=["Sbf%d" % d])
                    order = range(tf, tf + nt) if d == 0 else range(tf + nt - 1, tf - 1, -1)
                    for t in order:
                        B_ = yield from precompute(d, t)
                        for hb in ((0, 1) if d == 0 else (1, 0)):
                            yield from seq_step(d, t, hb, B_, S, Sk, False)
                    if si > 0:
                        row0 = (((si - 1) * 2 + d) * 8 + h) * 128
                        k.dma(nd_d[row0:row0 + 128, :], S[:], r=[Sk])

            gens = [dir_chain(0), dir_chain(1)]
            while gens:
                for g_ in list(gens):
                    try:
                        next(g_)
                    except StopIteration:
                        gens.remove(g_)
            for t in (range(20) if SL >= 9 else []):
                i = rot["fin"] % 2
                rot["fin"] += 1
                b = t // 4
                cond = 0 if b < 4 else 1
                sm = sm1[i]
                k.act(on_t[i][:], otm[:, t, :], AF.Square, r=[("otm", t, 0), ("otm", t, 1)], w=["ont%d" % i, "gsm%d" % i], accum_out=sm[:, 0:1])
                k.act(sm[:, 1:2], sm[:, 0:1], AF.Sqrt, r=["gsm%d" % i], w=["gsm%d" % i], bias=EPS, scale=1.0 / 128)
                k.op("dve", lambda: nc.vector.reciprocal(out=sm[:, 2:3], in_=sm[:, 1:2]), r=["gsm%d" % i], w=["gsm%d" % i])
                k.ts("dve", on_t[i][:], otm[:, t, :], sm[:, 2:3], None, ALU.mult, r=[("otm", t, 0), ("otm", t, 1), "gsm%d" % i], w=["ont%d" % i])
                p, pk = psum()
                k.tr(p[:, 0:128], on_t[i][:], ident[:], r=["ont%d" % i, "ident"], w=[pk])
                k.stt("dve", og_all[:, h, t * 128:(t + 1) * 128], p[:, 0:128], ngc[:, 0:1], szT[:, t * 128:(t + 1) * 128], ALU.mult, ALU.mult,
                      r=[pk, "dnng", "szT"], w=[("og", h, b)])
        k.barrier()
        k.dma(xflat, xsp_d[:, :], w=allx)
        wo = [k.sb([128, 8, 128], BF16, "gwo%d" % i, stack=ph) for i in range(2)]
        for mc in range(8):
            wt, wk_ = wo[mc % 2], "gwo%d" % (mc % 2)
            k.dma(wt[:], woo_d[:, mc * 128:(mc + 1) * 128].rearrange("(kc p) m -> p kc m", p=128), w=[wk_], q="pool")
            for b in range(NB):
                cond = 0 if b < 4 else 1
                sl = slice(b * 512, (b + 1) * 512)
                p, pk = psum()
                for kc in range(8):
                    k.mm(p[:, :], wt[:, kc, :], og_all[:, kc, sl], kc == 0, kc == 7, r=[wk_, ("og", kc, b)], w=[pk])
                k.stt("dve", xT[:, mc, sl], p[:, :], gate_ap(1, 0, mc, cond), xT[:, mc, sl], ALU.mult, ALU.add,
                      r=[pk, "modv1", xk(b, mc)], w=[xk(b, mc)])
        k.barrier()
        ph.close()

    mixer0()
    mlp(0)
    mixer1()
    mlp(1)

    ystage = [k.sb([128, D], F32, "ystage%d" % i) for i in range(2)]
    yT = [k.sb([128, 8, 512], F32, "yT%d" % i) for i in range(1)]
    for b in range(NB):
        sl = slice(b * 512, (b + 1) * 512)
        rt, rk = rstd_block(b)
        for c in range(8):
            k.stt("dve", yT[0][:, c, :], xT[:, c, sl], gT[:, 32 + c:33 + c], rt[:], ALU.mult, ALU.mult,
                  r=[xk(b, c), rk, "gains"], w=[("yT", c)])
        for tt_ in range(4):
            t = b * 4 + tt_
            st = ystage[t % 2]
            sk = "ystage%d" % (t % 2)
            for half in range(2):
                p, pk = psum()
                for j in range(4):
                    c = half * 4 + j
                    k.tr(p[:, j * 128:(j + 1) * 128], yT[0][:, c, tt_ * 128:(tt_ + 1) * 128], ident[:],
                         r=[("yT", c), "ident"], w=[pk])
                k.copy("act" if half else "dve", st[:, half * 512:(half + 1) * 512], p[:, :], r=[pk], w=[sk + "h%d" % half])
            dst = ys_d[t * 128:(t + 1) * 128, :] if t < 16 else yp_d[(t - 16) * 128:(t - 15) * 128, :]
            k.dma(dst, st[:], r=[sk + "h0", sk + "h1"])
    k.finish()
    return k


_CACHE = {}
STAGE = "full"


def kernel(**inp):
    n = 8
    f = lambda a: np.ascontiguousarray(np.asarray(a, dtype=np.float32))
    if "k" not in _CACHE:
        _CACHE["k"] = build(STAGE)
    k = _CACHE["k"]
    gains = np.concatenate([f(inp["norm_mix_g"]).reshape(16, 128), f(inp["norm_ff_g"]).reshape(16, 128),
                            f(inp["final_norm_g"]).reshape(8, 128)], axis=0)
    shared = {
        "gains": gains,
        "w_mod": f(inp["w_mod"]), "b_mod": f(inp["b_mod"]).reshape(2, 48, 128),
        "w_ff1": f(inp["w_ff1"]), "w_ff2": f(inp["w_ff2"]),
        "s5lam": np.stack([f(inp["s5_lam_re"][0]).reshape(32, 128), f(inp["s5_lam_im"][0]).reshape(32, 128)]),
        "s5ldt": f(inp["s5_log_dt"][0]).reshape(32, 2),
        "s5b": np.stack([f(inp["s5_b_re"][0]).reshape(4096, 16), f(inp["s5_b_im"][0]).reshape(4096, 16)]),
        "s5c": np.stack([f(inp["s5_c_re"][0]).reshape(1024, 64), f(inp["s5_c_im"][0]).reshape(1024, 64)]),
        "s5d": f(inp["s5_d"][0]).reshape(4, 128), "s5bg": f(inp["s5_b_glu"][0]).reshape(4, 128),
        "s5wg": f(inp["s5_w_glu"][0]),
        "w_in_o": f(inp["w_in_o"][0]), "w_out_o": f(inp["w_out_o"][0]),
        "convw": f(inp["dn_conv_w"][0]).reshape(120, 128),
        "alog": f(inp["dn_a_log"][0]).reshape(1, 16), "dtb": f(inp["dn_dt_bias"][0]).reshape(1, 16),
        "dnng": f(inp["dn_norm_g"][0]).reshape(1, 128),
        "w_in_e": f(inp["w_in_e"][0]), "w_out_e": f(inp["w_out_e"][0]), "rpb": f(inp["na_rpb"][0]),
    }
    in_maps = []
    for i in range(n):
        m = dict(shared)
        m["xs"] = f(inp["x_sample"][i])
        m["xp"] = f(inp["x_prompt"][2 * i:2 * i + 2]).reshape(2 * LP, D)
        m["s5x0"] = np.stack([f(inp["state_s5_re"][i, 0]).reshape(32, 128), f(inp["state_s5_im"][i, 0]).reshape(32, 128)])
        m["sdn"] = f(inp["state_dn"][i, 0]).reshape(16, 128, 128)
        m["ck"] = f(inp["cache_na_k"][i, 0]).reshape(LP, 512)
        m["cv"] = f(inp["cache_na_v"][i, 0]).reshape(LP, 512)
        m["cond"] = np.concatenate([f(inp["c"][i]).reshape(8, 128), f(inp["c_ctx"]).reshape(8, 128)], axis=0)
        in_maps.append(m)
    res = run_bass_kernel_spmd(k.nc, in_maps, core_ids=list(range(n)))
    R = res.results
    y_s = np.stack([R[i]["ys"] for i in range(n)], axis=0)
    y_p = np.concatenate([R[i]["yp"].reshape(2, LP, D) for i in range(n)], axis=0)
    nk = np.concatenate([R[i]["nk"].reshape(2, 1, LP, 8, 64) for i in range(n)], axis=0)
    nv = np.concatenate([R[i]["nv"].reshape(2, 1, LP, 8, 64) for i in range(n)], axis=0)
    nsre = np.concatenate([R[i]["nsre"].reshape(2, 1, 2, 32, 64) for i in range(n)], axis=0)
    nsim = np.concatenate([R[i]["nsim"].reshape(2, 1, 2, 32, 64) for i in range(n)], axis=0)
    nd = np.concatenate([R[i]["nd"].reshape(2, 1, 2, 8, 128, 128) for i in range(n)], axis=0)
    return y_p, y_s, nk, nv, nsre, nsim, nd
```

```python
import numpy as np
from contextlib import ExitStack
import concourse.bass as bass
import concourse.mybir as mybir
from concourse.bass_utils import run_bass_kernel_spmd

F32 = mybir.dt.float32
BF16 = mybir.dt.bfloat16
F32R = mybir.dt.float32r
AF = mybir.ActivationFunctionType
ALU = mybir.AluOpType

D = 1024
T = 2560
LS = 2048
LP = 256
NB = 5
EPS = 1e-6
NDMA = 16


class KB:
    def __init__(self):
        self.nc = nc = bass.Bass("TRN2", target_bir_lowering=False)
        self.es = ExitStack()
        self.eng = {"pe": nc.tensor, "act": nc.scalar, "dve": nc.vector, "pool": nc.gpsimd, "sp": nc.sync}
        self.sem = {}
        for e in self.eng:
            self.sem[e] = self.es.enter_context(nc.semaphore("s_" + e))
        for i in range(NDMA):
            self.sem["d%d" % i] = self.es.enter_context(nc.semaphore("sd%d" % i))
        self.cnt = {s: 0 for s in self.sem}
        self.waited = {e: {} for e in self.eng}
        self.vc = {}
        self.seq = {}
        self.nseq = 0
        self.lastw = {}
        self.readers = {}
        self.dnext = 0
        self.uid = 0
        self.ps_next = 0

    def sb(self, shape, dt=F32, name=None, stack=None):
        self.uid += 1
        t = (stack or self.es).enter_context(self.nc.sbuf_tensor("%s_%d" % (name or "t", self.uid), list(shape), dt))
        return t

    def dram(self, name, shape, kind, dt=F32):
        return self.nc.dram_tensor(name, list(shape), dt, kind=kind).ap()

    def _learn(self, e, s, v):
        kn = self.waited[e]
        if kn.get(s, 0) < v:
            kn[s] = v
        for s2, v2 in self.vc.get((s, v), {}).items():
            if kn.get(s2, 0) < v2:
                kn[s2] = v2

    def wait(self, e, s, v):
        if self.waited[e].get(s, 0) < v:
            self.eng[e].wait_ge(self.sem[s], v)
            self._learn(e, s, v)

    def _need(self, e, r, w):
        need = {}
        for k in r:
            lw = self.lastw.get(k)
            if lw:
                need[lw[0]] = max(need.get(lw[0], 0), lw[1])
        for k in w:
            lw = self.lastw.get(k)
            if lw:
                need[lw[0]] = max(need.get(lw[0], 0), lw[1])
            for s, v in self.readers.get(k, {}).items():
                need[s] = max(need.get(s, 0), v)
        out = []
        for s, v in need.items():
            if e == "pe" and s == "pe":
                continue
            if self.waited[e].get(s, 0) < v:
                out.append((s, v))
        return out

    def _issue(self, e, need, fn):
        if not need:
            for s, v in need:
                self.wait(e, s, v)
            return fn()
        need = sorted(need, key=lambda sv: self.seq.get(sv, 0), reverse=True)
        ds, dv = need[0]
        self._learn(e, ds, dv)
        for s, v in need[1:]:
            self.wait(e, s, v)
        ins = fn()
        ins._wait_ge(self.sem[ds], dv)
        return ins

    def _record(self, s, idx, r, w):
        for k in r:
            self.readers.setdefault(k, {})[s] = idx
        for k in w:
            self.lastw[k] = (s, idx)
            self.readers[k] = {}

    def op(self, e, fn, r=(), w=()):
        psr = [k_ for k_ in r if isinstance(k_, tuple) and k_[0] == "ps"]
        if psr:
            w = list(w) + psr
        ins = self._issue(e, self._need(e, r, w), fn)
        self.cnt[e] += 1
        ins.then_inc(self.sem[e], 1)
        self.vc[(e, self.cnt[e])] = dict(self.waited[e])
        self.nseq += 1
        self.seq[(e, self.cnt[e])] = self.nseq
        self._record(e, self.cnt[e], r, w)
        return ins

    def dma(self, out, in_, r=(), w=(), q="sp", **kw):
        ch = self.dnext
        self.dnext = (ch + 1) % NDMA
        s = "d%d" % ch
        if self.cnt[s] > 0:
            self.wait(q, s, self.cnt[s])
        ins = self._issue(q, self._need(q, r, w), lambda: self.eng[q].dma_start(out=out, in_=in_, **kw))
        self.cnt[s] += 16
        ins.then_inc(self.sem[s], 16)
        self.vc[(s, self.cnt[s])] = dict(self.waited[q])
        self.nseq += 1
        self.seq[(s, self.cnt[s])] = self.nseq
        self._record(s, self.cnt[s], r, w)

    def barrier(self):
        for e in self.eng:
            for s in self.sem:
                if s != e and self.cnt[s] > 0:
                    self.wait(e, s, self.cnt[s])

    def finish(self):
        for s in self.sem:
            if s.startswith("d") and self.cnt[s] > 0:
                self.wait("sp", s, self.cnt[s])
        for e in ("pe", "act", "dve", "pool"):
            if self.cnt[e] > 0:
                self.wait("sp", e, self.cnt[e])

    def mm(self, out, lhsT, rhs, start, stop, r=(), w=()):
        return self.op("pe", lambda: self.nc.tensor.matmul(out, lhsT, rhs, start=start, stop=stop), r, w)

    def tr(self, out, in_, ident, r=(), w=()):
        return self.op("pe", lambda: self.nc.tensor.transpose(out, in_, ident), r, w)

    def act(self, out, in_, func, r=(), w=(), **kw):
        return self.op("act", lambda: self.nc.scalar.activation(out=out, in_=in_, func=func, **kw), r, w)

    def ts(self, e, out, in0, s1, s2, op0, op1=None, r=(), w=()):
        eng = self.eng[e]
        if op1 is None:
            return self.op(e, lambda: eng.tensor_scalar(out=out, in0=in0, scalar1=s1, scalar2=None, op0=op0), r, w)
        return self.op(e, lambda: eng.tensor_scalar(out=out, in0=in0, scalar1=s1, scalar2=s2, op0=op0, op1=op1), r, w)

    def tt(self, e, out, in0, in1, op, r=(), w=()):
        eng = self.eng[e]
        return self.op(e, lambda: eng.tensor_tensor(out=out, in0=in0, in1=in1, op=op), r, w)

    def stt(self, e, out, in0, scalar, in1, op0, op1, r=(), w=()):
        eng = self.eng[e]
        return self.op(e, lambda: eng.scalar_tensor_tensor(out=out, in0=in0, scalar=scalar, in1=in1, op0=op0, op1=op1), r, w)

    def copy(self, e, out, in_, r=(), w=()):
        if e == "act":
            return self.op(e, lambda: self.nc.scalar.copy(out=out, in_=in_), r, w)
        eng = self.eng[e]
        return self.op(e, lambda: eng.tensor_copy(out=out, in_=in_), r, w)

    def memset(self, e, ap, val, r=(), w=()):
        eng = self.eng[e]
        return self.op(e, lambda: eng.memset(ap, val), r, w)


def build(stage="full"):
    k = KB()
    nc = k.nc
    xs_d = k.dram("xs", [LS, D], "ExternalInput")
    xp_d = k.dram("xp", [2 * LP, D], "ExternalInput")
    cond_d = k.dram("cond", [16, 128], "ExternalInput")
    gains_d = k.dram("gains", [40, 128], "ExternalInput")
    wmod_d = k.dram("w_mod", [2, D, 6 * D], "ExternalInput")
    bmod_d = k.dram("b_mod", [2, 48, 128], "ExternalInput")
    wff1_d = k.dram("w_ff1", [2, D, 4 * D], "ExternalInput")
    wff2_d = k.dram("w_ff2", [2, 4 * D, D], "ExternalInput")
    wine_d = k.dram("w_in_e", [D, 2048], "ExternalInput")
    woute_d = k.dram("w_out_e", [D, D], "ExternalInput")
    ck_d = k.dram("ck", [LP, 512], "ExternalInput")
    cv_d = k.dram("cv", [LP, 512], "ExternalInput")
    rpb_d = k.dram("rpb", [8, 15, 31], "ExternalInput")
    lam_d = k.dram("s5lam", [2, 32, 128], "ExternalInput")
    x0_d = k.dram("s5x0", [2, 32, 128], "ExternalInput")
    ldt_d = k.dram("s5ldt", [32, 2], "ExternalInput")
    sb_d = k.dram("s5b", [2, 4096, 16], "ExternalInput")
    sc_d = k.dram("s5c", [2, 1024, 64], "ExternalInput")
    sd_d = k.dram("s5d", [4, 128], "ExternalInput")
    bg_d = k.dram("s5bg", [4, 128], "ExternalInput")
    wglu_d = k.dram("s5wg", [512, 512], "ExternalInput")
    nsre_d = k.dram("nsre", [64, 128], "ExternalOutput")
    nsim_d = k.dram("nsim", [64, 128], "ExternalOutput")
    wino_d = k.dram("w_in_o", [D, 4128], "ExternalInput")
    woo_d = k.dram("w_out_o", [D, D], "ExternalInput")
    cw_d = k.dram("convw", [120, 128], "ExternalInput")
    alog_d = k.dram("alog", [1, 16], "ExternalInput")
    dtb_d = k.dram("dtb", [1, 16], "ExternalInput")
    ng_d = k.dram("dnng", [1, 128], "ExternalInput")
    sdn_d = k.dram("sdn", [16, 128, 128], "ExternalInput")
    xsp_d = k.dram("xspill", [128, 8 * T], "Internal")
    nd_d = k.dram("nd", [4096, 128], "ExternalOutput")
    nk_d = k.dram("nk", [2 * LP, 512], "ExternalOutput")
    nv_d = k.dram("nv", [2 * LP, 512], "ExternalOutput")
    ys_d = k.dram("ys", [LS, D], "ExternalOutput")
    yp_d = k.dram("yp", [2 * LP, D], "ExternalOutput")

    ident = k.sb([128, 128], F32, "ident")
    k.op("pool", lambda: nc.gpsimd.iota(ident[:], pattern=[[1, 128]], base=0, channel_multiplier=-1,
                                          allow_small_or_imprecise_dtypes=True), w=["ident"])
    k.ts("dve", ident[:], ident[:], 0.0, None, ALU.is_equal, r=["ident"], w=["ident"])
    ones_bf = k.sb([128, 128], BF16, "ones")
    k.memset("dve", ones_bf[:], 1.0, w=["ones"])

    ps = [k.es.enter_context(nc.psum_tensor("ps%d" % i, [128, 512], F32)) for i in range(8)]

    k.ps_pool = list(range(8))

    def psum():
        k.ps_next = (k.ps_next + 1) % len(k.ps_pool)
        i = k.ps_pool[k.ps_next]
        return ps[i], ("ps", i)

    xT = k.sb([128, 8, T], F32, "xT")
    hT = k.sb([128, 8, T], BF16, "hT")

    def xk(b, c):
        return ("x", b, c)

    def load_cols(src_ap, nrows, name, stack=None):
        st = k.sb([nrows, 128], F32, name + "_rows", stack=stack)
        k.dma(st[:], src_ap, w=[name + "_rows"])
        p, pk = psum()
        k.tr(p[:, 0:nrows], st[:], ident[0:nrows, 0:nrows], r=[name + "_rows", "ident"], w=[pk])
        dst = k.sb([128, nrows], F32, name, stack=stack)
        k.copy("dve", dst[:], p[:, 0:nrows], r=[pk], w=[name])
        return dst

    condT = load_cols(cond_d[:, :], 16, "cond")
    gT = load_cols(gains_d[:, :], 40, "gains")
    bmodT = [load_cols(bmod_d[l, :, :], 48, "bmod%d" % l) for l in range(2)]
    scond = k.sb([128, 16], F32, "scond")
    k.act(scond[:], condT[:], AF.Silu, r=["cond"], w=["scond"])

    modv = [k.sb([128, 48, 2], F32, "modv%d" % l) for l in range(2)]
    gs = {}
    for l in range(2):
        for sub in range(2):
            gs[(l, sub)] = k.sb([128, 8, 2], F32, "gs%d%d" % (l, sub))
    sqb = [k.sb([128, 512], BF16, "sq%d" % i) for i in range(2)]
    tmpb = [k.sb([128, 512], F32, "ntmp%d" % i) for i in range(3)]
    rstd_t = [k.sb([128, 512], F32, "rstd%d" % i) for i in range(2)]
    ph = ExitStack()
    stage_x = [k.sb([128, D], F32, "xstage%d" % i, stack=ph) for i in range(2)]
    for t in range(T // 128):
        st = stage_x[t % 2]
        sk = "xstage%d" % (t % 2)
        src = xs_d[t * 128:(t + 1) * 128, :] if t < 16 else xp_d[(t - 16) * 128:(t - 15) * 128, :]
        k.dma(st[:], src, w=[sk])
        b = t // 4
        for half in range(2):
            p, pk = psum()
            for j in range(4):
                c = half * 4 + j
                k.tr(p[:, j * 128:(j + 1) * 128], st[:, c * 128:(c + 1) * 128], ident[:], r=[sk, "ident"], w=[pk])
            k.copy("act" if half else "dve", xT[:, half * 4:half * 4 + 4, t * 128:(t + 1) * 128],
                   p[:, :].rearrange("p (c t) -> p c t", c=4), r=[pk], w=[xk(b, c) for c in range(half * 4, half * 4 + 4)])

    wmp = [k.sb([128, 8, 512], BF16, "wmodp%d" % i, stack=ph) for i in range(2)]
    scond_bf = k.sb([128, 16], BF16, "scond_bf", stack=ph)
    k.copy("dve", scond_bf[:], scond[:], r=["scond"], w=["scond_bf"])
    npan = 0
    for l in range(2):
        pm, pmk = psum()
        for pj in range(12):
            wp = wmp[npan % 2]
            wk = "wmodp%d" % (npan % 2)
            npan += 1
            k.dma(wp[:], wmod_d[l, :, pj * 512:(pj + 1) * 512].rearrange("(kc p) m -> p kc m", p=128), w=[wk], q="pool")
            for jj in range(4):
                j = pj * 4 + jj
                for kc in range(8):
                    k.mm(pm[:, 2 * j:2 * j + 2], wp[:, kc, jj * 128:(jj + 1) * 128],
                         scond_bf[:, :].rearrange("p (c k) -> p c k", c=2)[:, :, kc],
                         start=(kc == 0), stop=(kc == 7), r=[wk, "scond_bf"], w=[pmk])
        k.tt("dve", modv[l][:], pm[:, 0:96].rearrange("p (j c) -> p j c", c=2),
             bmodT[l][:, :].unsqueeze(2).to_broadcast([128, 48, 2]), ALU.add, r=[pmk, "bmod%d" % l], w=["modv%d" % l])
    for l in range(2):
        for sub in range(2):
            t_ = gs[(l, sub)]
            gcol = gT[:, (sub * 2 + l) * 8:(sub * 2 + l) * 8 + 8]
            sc = modv[l][:, (sub * 3 + 1) * 8:(sub * 3 + 1) * 8 + 8, :]
            k.stt("dve", t_[:], sc, 1.0, gcol.unsqueeze(2).to_broadcast([128, 8, 2]), ALU.add, ALU.mult,
                  r=["modv%d" % l, "gains"], w=["gs%d%d" % (l, sub)])

    k.barrier()
    ph.close()

    def shift_ap(l, sub, c, cond):
        return modv[l][:, (sub * 3) * 8 + c, cond:cond + 1]

    def gate_ap(l, sub, c, cond):
        return modv[l][:, (sub * 3 + 2) * 8 + c, cond:cond + 1]

    cnt = {"sq": 0, "tmp": 0, "rstd": 0}

    def rstd_block(b):
        p, pk = psum()
        sl = slice(b * 512, (b + 1) * 512)
        for c in range(8):
            i = cnt["sq"] % 2
            cnt["sq"] += 1
            k.act(sqb[i][:], xT[:, c, sl], AF.Square, r=[xk(b, c)], w=["sq%d" % i])
            k.mm(p[:, :], ones_bf[:], sqb[i][:], start=(c == 0), stop=(c == 7), r=["ones", "sq%d" % i], w=[pk])
        i = cnt["rstd"] % 2
        cnt["rstd"] += 1
        rk = "rstd%d" % i
        k.act(rstd_t[i][:], p[:, :], AF.Sqrt, r=[pk], w=[rk], bias=EPS, scale=1.0 / D)
        k.op("dve", lambda: nc.vector.reciprocal(out=rstd_t[i][:], in_=rstd_t[i][:]), r=[rk], w=[rk])
        return rstd_t[i], rk

    def norm_block(l, sub, b):
        cond = 0 if b < 4 else 1
        sl = slice(b * 512, (b + 1) * 512)
        rt, rk = rstd_block(b)
        for c in range(8):
            i = cnt["tmp"] % 3
            cnt["tmp"] += 1
            tk = "ntmp%d" % i
            k.tt("dve", tmpb[i][:], xT[:, c, sl], rt[:], ALU.mult, r=[xk(b, c), rk], w=[tk])
            k.act(hT[:, c, sl], tmpb[i][:], AF.Identity, r=[tk, "gs%d%d" % (l, sub), "modv%d" % l], w=[("h", b, c)],
                  bias=shift_ap(l, sub, c, cond), scale=gs[(l, sub)][:, c, cond:cond + 1])

    NQ = 8
    mcnt = {"p": 0, "relu": 0}

    def mlp(l):
        ph = ExitStack()
        h1T = k.sb([128, 4, T], BF16, "h1T", stack=ph)
        w1p = [k.sb([128, 8, 512], BF16, "w1p%d" % i, stack=ph) for i in range(2)]
        w2p = [k.sb([128, 4, D], BF16, "w2p%d" % i, stack=ph) for i in range(2)]
        relu_t = [k.sb([128, 512], F32, "relu%d" % i, stack=ph) for i in range(2)]
        for b in range(NB):
            norm_block(l, 1, b)
        def load_q(q_):
            i_ = q_ % 2
            k.dma(w1p[i_][:], wff1_d[l, :, q_ * 512:(q_ + 1) * 512].rearrange("(kc p) m -> p kc m", p=128), w=["w1p%d" % i_], q="pool")
            k.dma(w2p[i_][:], wff2_d[l, q_ * 512:(q_ + 1) * 512, :].rearrange("(kc p) m -> p kc m", p=128), w=["w2p%d" % i_], q="pool")
        load_q(0)
        for q in range(NQ):
            i = q % 2
            w1, w1k, w2, w2k = w1p[i], "w1p%d" % i, w2p[i], "w2p%d" % i
            if q + 1 < NQ:
                load_q(q + 1)
            for fc in range(4):
                for b in range(NB):
                    sl = slice(b * 512, (b + 1) * 512)
                    p, pk = psum()
                    for kc in range(8):
                        k.mm(p[:, :], w1[:, kc, fc * 128:(fc + 1) * 128], hT[:, kc, sl], start=(kc == 0), stop=(kc == 7),
                             r=[w1k, ("h", b, kc)], w=[pk])
                    j = mcnt["relu"] % 2
                    mcnt["relu"] += 1
                    rk = "relu%d" % j
                    k.act(relu_t[j][:], p[:, :], AF.Relu, r=[pk], w=[rk])
                    k.tt("pool", h1T[:, fc, sl], relu_t[j][:], relu_t[j][:], ALU.mult, r=[rk], w=[("h1", b, fc)])
            for c in range(8):
                for b in range(NB):
                    cond = 0 if b < 4 else 1
                    sl = slice(b * 512, (b + 1) * 512)
                    p, pk = psum()
                    for fc in range(4):
                        k.mm(p[:, :], w2[:, fc, c * 128:(c + 1) * 128], h1T[:, fc, sl], start=(fc == 0), stop=(fc == 3),
                             r=[w2k, ("h1", b, fc)], w=[pk])
                    k.stt("dve", xT[:, c, sl], p[:, :], gate_ap(l, 1, c, cond), xT[:, c, sl], ALU.mult, ALU.add,
                          r=[pk, "modv%d" % l, xk(b, c)], w=[xk(b, c)])
        k.barrier()
        ph.close()

    def mixer0():
        ph = ExitStack()
        for b in range(NB):
            norm_block(0, 0, b)
        boT = k.sb([128, 4, T], BF16, "boT", stack=ph)
        ph_outer = ph
        ph = ExitStack()
        abig = k.sb([31, 127], F32, "abig", stack=ph)
        k.op("pool", lambda: nc.gpsimd.iota(abig[:], pattern=[[1, 127]], base=-78, channel_multiplier=1,
                                              allow_small_or_imprecise_dtypes=True), w=["abig"])
        k.ts("dve", abig[:], abig[:], 0.0, None, ALU.is_equal, r=["abig"], w=["abig"])
        maskb = k.sb([64, 64], F32, "maskb", stack=ph)
        csv = k.sb([64, 1], F32, "csv", stack=ph)
        k.op("pool", lambda: nc.gpsimd.iota(csv[:], pattern=[[0, 1]], base=-8, channel_multiplier=1,
                                              allow_small_or_imprecise_dtypes=True), w=["csv"])
        k.ts("dve", csv[:], csv[:], 0.0, 48.0, ALU.max, ALU.min, r=["csv"], w=["csv"])
        k.op("pool", lambda: nc.gpsimd.iota(maskb[:], pattern=[[1, 64]], base=0, channel_multiplier=0,
                                              allow_small_or_imprecise_dtypes=True), w=["maskb"])
        k.ts("dve", maskb[:], maskb[:], csv[:, 0:1], None, ALU.subtract, r=["maskb", "csv"], w=["maskb"])
        m2 = k.sb([64, 64], F32, "m2", stack=ph)
        k.ts("dve", m2[:], maskb[:], 16.0, None, ALU.is_lt, r=["maskb"], w=["m2"])
        k.ts("dve", maskb[:], maskb[:], 0.0, None, ALU.is_ge, r=["maskb"], w=["maskb"])
        k.tt("dve", maskb[:], maskb[:], m2[:], ALU.mult, r=["maskb", "m2"], w=["maskb"])
        k.ts("dve", maskb[:], maskb[:], -1.0, 30000.0, ALU.add, ALU.mult, r=["maskb"], w=["maskb"])
        rpbT = k.sb([31, 8, 15], F32, "rpbT", stack=ph)
        for h in range(8):
            st = k.sb([15, 31], F32, "rpbrow%d" % h, stack=ph)
            k.dma(st[:], rpb_d[h, :, :], w=["rpbrow%d" % h])
            p, pk = psum()
            k.tr(p[0:31, 0:15], st[:], ident[0:15, 0:15], r=["rpbrow%d" % h, "ident"], w=[pk])
            k.copy("dve", rpbT[:, h, :], p[0:31, 0:15], r=[pk], w=["rpbT"])
        abig_bf = k.sb([31, 127], BF16, "abig_bf", stack=ph)
        k.copy("dve", abig_bf[:], abig[:], r=["abig"], w=["abig_bf"])
        rpbT_bf = k.sb([31, 8, 15], BF16, "rpbT_bf", stack=ph)
        k.copy("dve", rpbT_bf[:], rpbT[:], r=["rpbT"], w=["rpbT_bf"])
        S_sb = [k.sb([128, 896], F32, "S%d" % i, stack=ph) for i in range(4)]
        for i in range(4):
            k.memset("pool", S_sb[i][:], -30000.0, w=["S%d" % i])
        PT_sb = [k.sb([128, 448], BF16, "PT%d" % i, stack=ph) for i in range(4)]
        small = [k.sb([128, 4], F32, "sm%d" % i, stack=ph) for i in range(4)]
        wq = k.sb([128, 8, 384], BF16, "wq", stack=ph)
        qT = k.sb([128, T], BF16, "qT", stack=ph)
        kT = k.sb([128, T], BF16, "kT", stack=ph)
        vtm = k.sb([128, 20, 128], BF16, "vtm", stack=ph)
        ckst = k.sb([128, 2, 128], F32, "ckst", stack=ph)
        ckT = k.sb([128, LP], BF16, "ckT", stack=ph)
        cvt = k.sb([128, 2, 128], BF16, "cvt", stack=ph)
        tab = k.sb([64, 2, 960], F32, "tab", stack=ph)
        kvst = [tmpb[0], tmpb[1]]
        it = {"n": 0, "kv": 0}

        ppool2 = {0: [0, [0, 1]], 1: [0, [2, 3]], 2: [0, [4, 5]], 3: [0, [6, 7]]}

        def psd2(hh):
            st = ppool2[hh]
            st[0] = (st[0] + 1) % 2
            i_ = st[1][st[0]]
            return ps[i_], ("ps", i_)

        def run_many(gl_):
            gens = list(gl_)
            while gens:
                for g_ in list(gens):
                    try:
                        next(g_)
                    except StopIteration:
                        gens.remove(g_)

        def run_pair(g0, g1):
            gens = [g0, g1]
            while gens:
                for g_ in list(gens):
                    try:
                        next(g_)
                    except StopIteration:
                        gens.remove(g_)

        def attend(c, hh, qsl, nq, band, ctx_kT, ctx_tiles, bias_ap, cid=None):
            pb = hh * 64
            i = hh if cid is None else cid
            S, Sk, PT, PTk, sm, smk = S_sb[i], "S%d" % i, PT_sb[i], "PT%d" % i, small[i], "sm%d" % i
            P, Pk = S, Sk
            nctx = len(ctx_tiles) * 128
            if band is not None:
                rs = band
                pA, pAk = psd2(i)
                k.mm(pA[0:nq, 0:512], qT[pb:pb + 64, qsl], kT[pb:pb + 64, rs * 64:rs * 64 + 512], True, True,
                     r=["qT", "kT"], w=[pAk])
                k.tt("dve", S[0:nq, 64:576], pA[0:nq, 0:512], bias_ap, ALU.add, r=[pAk, "tab"], w=[Sk])
            pB, pBk = psd2(i)
            k.mm(pB[0:nq, 0:nctx], qT[pb:pb + 64, qsl], ctx_kT, True, True, r=["qT", "kT", "ckT"], w=[pBk])
            k.copy("act", S[0:nq, 640:640 + nctx], pB[0:nq, 0:nctx], r=[pBk], w=[Sk])
            yield
            lo = 0 if band is not None else 640
            k.op("dve", lambda: nc.vector.tensor_reduce(out=sm[0:nq, 1:2], in_=S[0:nq, lo:640 + nctx], axis=mybir.AxisListType.X,
                                                          op=ALU.max, negate=True), r=[Sk], w=[smk])
            k.act(P[0:nq, lo:640 + nctx], S[0:nq, lo:640 + nctx], AF.Exp, r=[Sk, smk], w=[Pk, smk],
                  bias=sm[0:nq, 1:2], scale=1.0, accum_out=sm[0:nq, 2:3])
            yield
            k.op("dve", lambda: nc.vector.reciprocal(out=sm[0:nq, 3:4], in_=sm[0:nq, 2:3]), r=[smk], w=[smk])
            k.ts("dve", P[0:nq, lo:640 + nctx], P[0:nq, lo:640 + nctx], sm[0:nq, 3:4], None, ALU.mult, r=[Pk, smk], w=[Pk])
            yield
            chunks = []
            if band is not None:
                if rs % 2 == 0:
                    for j in range(4):
                        chunks.append((64 + j * 128, vtm[:, rs // 2 + j, pb:pb + 64]))
                else:
                    for j in range(5):
                        chunks.append((j * 128, vtm[:, (rs - 1) // 2 + j, pb:pb + 64]))
            for j, vt in enumerate(ctx_tiles):
                chunks.append((640 + j * 128, vt))
            pT, pTk = psd2(i)
            for j, (c0, _) in enumerate(chunks):
                k.tr(pT[:, j * nq:(j + 1) * nq], P[0:nq, c0:c0 + 128], ident[0:nq, 0:nq], r=[Pk, "ident"], w=[pTk])
            nch = len(chunks)
            k.copy("act", PT[:, 0:nch * nq], pT[:, 0:nch * nq], r=[pTk], w=[PTk])
            if band is not None:
                k.memset("pool", S[0:nq, 0:64], -30000.0, w=[Sk])
                k.memset("pool", S[0:nq, 576:640], -30000.0, w=[Sk])
            yield
            pO, pOk = psd2(i)
            for j, (_, vt) in enumerate(chunks):
                k.mm(pO[pb:pb + 64, 0:nq], vt, PT[:, j * nq:(j + 1) * nq], j == 0, j == nch - 1,
                     r=[PTk, "vtm", "cvt"], w=[pOk])
            k.copy("act", boT[pb:pb + 64, c, qsl], pO[pb:pb + 64, 0:nq], r=[pOk], w=[("bo", c)])
            yield

        for c in range(4):
            for w_ in range(3):
                k.dma(wq[:, :, w_ * 128:(w_ + 1) * 128],
                      wine_d[:, 512 * (w_ + 1) + c * 128:512 * (w_ + 1) + (c + 1) * 128].rearrange("(kc p) m -> p kc m", p=128),
                      w=["wq"], q="pool")
            for b in range(NB):
                sl = slice(b * 512, (b + 1) * 512)
                for w_, dst, dk_ in ((0, qT, "qT"), (1, kT, "kT")):
                    p, pk = psum()
                    for kc in range(8):
                        k.mm(p[:, :], wq[:, kc, w_ * 128:(w_ + 1) * 128], hT[:, kc, sl], kc == 0, kc == 7,
                             r=["wq", ("h", b, kc)], w=[pk])
                    k.act(dst[:, sl], p[:, :], AF.Identity, r=[pk], w=[dk_], scale=(0.125 if w_ == 0 else 1.0))
            for t in range(20):
                p, pk = psum()
                for kc in range(8):
                    k.mm(p[:, 0:128], hT[:, kc, t * 128:(t + 1) * 128], wq[:, kc, 256:384], kc == 0, kc == 7,
                         r=["wq", ("h", t // 4, kc)], w=[pk])
                k.copy("dve", vtm[:, t, :], p[:, 0:128], r=[pk], w=["vtm"])
            k.dma(ckst[:], ck_d[:, c * 128:(c + 1) * 128].rearrange("(t p) m -> p t m", p=128), w=["ckst"])
            p, pk = psum()
            for t in range(2):
                k.tr(p[:, t * 128:(t + 1) * 128], ckst[:, t, :], ident[:], r=["ckst", "ident"], w=[pk])
            k.copy("dve", ckT[:], p[:, 0:LP], r=[pk], w=["ckT"])
            k.dma(cvt[:], cv_d[:, c * 128:(c + 1) * 128].rearrange("(t p) m -> p t m", p=128), w=["cvt"], q="pool")
            for hh in range(2):
                h = 2 * c + hh
                pt_, ptk = [], []
                for half in range(2):
                    a, b_ = psum()
                    pt_.append(a)
                    ptk.append(b_)
                for kc in range(64):
                    for half in range(2):
                        d0, d1 = (0, 8) if half == 0 else (8, 15)
                        k.mm(pt_[half][0:64, 0:(d1 - d0) * 64].rearrange("p (d k) -> p d k", k=64)[:, :, kc],
                             abig_bf[:, 63 - kc:127 - kc], rpbT_bf[:, h, d0:d1], True, True, r=["abig_bf", "rpbT_bf"], w=[ptk[half]])
                for half in range(2):
                    d0, d1 = (0, 8) if half == 0 else (8, 15)
                    k.tt("dve", tab[:, hh, d0 * 64:d1 * 64].rearrange("p (d k) -> p d k", k=64),
                         pt_[half][0:64, 0:(d1 - d0) * 64].rearrange("p (d k) -> p d k", k=64),
                         maskb[:, :].unsqueeze(1).to_broadcast([64, d1 - d0, 64]), ALU.add,
                         r=[ptk[half], "maskb"], w=["tab"])
            for r2 in range(16):
                gl_ = []
                for rr in range(2):
                    r_ = r2 * 2 + rr
                    rs = min(max(r_ - 4, 0), 24)
                    dr0 = rs - r_ + 7
                    for hh in range(2):
                        gl_.append(attend(c, hh, slice(r_ * 64, r_ * 64 + 64), 64, rs, ckT[hh * 64:hh * 64 + 64, :],
                                          [cvt[:, 0, hh * 64:hh * 64 + 64], cvt[:, 1, hh * 64:hh * 64 + 64]],
                                          tab[:, hh, dr0 * 64:dr0 * 64 + 512], cid=rr * 2 + hh))
                run_many(gl_)
            for pbatch in range(2):
                t0 = LS + pbatch * LP
                gl_ = []
                for qb in range(2):
                    for hh in range(2):
                        gl_.append(attend(c, hh, slice(t0 + qb * 128, t0 + qb * 128 + 128), 128, None,
                                          kT[hh * 64:hh * 64 + 64, t0:t0 + LP],
                                          [vtm[:, 16 + pbatch * 2, hh * 64:hh * 64 + 64], vtm[:, 17 + pbatch * 2, hh * 64:hh * 64 + 64]],
                                          None, cid=qb * 2 + hh))
                run_many(gl_)
            for w_, dst_d in ((1, nk_d), (2, nv_d)):
                for t in range(16, 20):
                    p, pk = psum()
                    for kc in range(8):
                        k.mm(p[:, 0:128], hT[:, kc, t * 128:(t + 1) * 128], wq[:, kc, w_ * 128:(w_ + 1) * 128], kc == 0, kc == 7,
                             r=["wq", ("h", 4, kc)], w=[pk])
                    i = it["kv"] % 2
                    it["kv"] += 1
                    k.copy("dve", kvst[i][:, 0:128], p[:, 0:128], r=[pk], w=["ntmp%d" % i])
                    k.dma(dst_d[(t - 16) * 128:(t - 15) * 128, c * 128:(c + 1) * 128], kvst[i][:, 0:128], r=["ntmp%d" % i])
        k.barrier()
        ph.close()
        ph = ph_outer
        aoT = k.sb([128, 4, T], BF16, "aoT", stack=ph)
        if stage != "noS5":
            k.ps_pool = list(range(8))
            ph_keep = ph
            ph = ExitStack()
            hflat = hT[:, :, :].rearrange("p c t -> p (c t)")
            hf32 = hflat.bitcast(F32)
            bufs = [(hf32[:, 0:2048], hf32[:, 2048:4096]), (hf32[:, 4096:6144], hf32[:, 6144:8192])]
            xbf = (hflat[:, 16384:18432], hflat[:, 18432:20480])
            lre = load_cols(lam_d[0, :, :], 32, "lre", ph)
            lim = load_cols(lam_d[1, :, :], 32, "lim", ph)
            x0r = load_cols(x0_d[0, :, :], 32, "x0r", ph)
            x0i = load_cols(x0_d[1, :, :], 32, "x0i", ph)
            ldr = k.sb([32, 2], F32, "ldr", stack=ph)
            k.dma(ldr[:], ldt_d[:, :], w=["ldr"])
            ldx = k.sb([32, 128], F32, "ldx", stack=ph)
            k.copy("dve", ldx[:, :].rearrange("p (g n) -> p g n", g=2), ldr[:, :].unsqueeze(2).to_broadcast([32, 2, 64]),
                   r=["ldr"], w=["ldx"])
            p, pk = psum()
            k.tr(p[:, 0:32], ldx[:], ident[0:32, 0:32], r=["ldx", "ident"], w=[pk])
            dtt = k.sb([128, 32], F32, "dtt", stack=ph)
            k.act(dtt[:], p[:, 0:32], AF.Exp, r=[pk], w=["dtt"])
            W = {}

            def tl(name):
                W[name] = k.sb([128, 32], F32, "s5_" + name, stack=ph)
                return W[name]

            def e2(out, a, b_, op, eng="dve"):
                k.tt(eng, W[out][:], W[a][:], W[b_][:], op, r=["s5_" + a, "s5_" + b_], w=["s5_" + out])

            for nme in ("er", "th", "s16", "s8", "c8", "t1", "t2", "cr", "ci", "nr", "ni", "den", "cfr", "cfi", "lx0r", "lx0i"):
                tl(nme)
            k.tt("dve", W["th"][:], lim[:], dtt[:], ALU.mult, r=["lim", "dtt"], w=["s5_th"])
            k.tt("dve", W["t1"][:], lre[:], dtt[:], ALU.mult, r=["lre", "dtt"], w=["s5_t1"])
            k.act(W["er"][:], W["t1"][:], AF.Exp, r=["s5_t1"], w=["s5_er"])
            k.act(W["s16"][:], W["th"][:], AF.Sin, r=["s5_th"], w=["s5_s16"], scale=1.0 / 16)
            k.act(W["s8"][:], W["th"][:], AF.Sin, r=["s5_th"], w=["s5_s8"], scale=1.0 / 8)
            e2("t1", "s16", "s16", ALU.mult)
            k.ts("dve", W["c8"][:], W["t1"][:], -2.0, 1.0, ALU.mult, ALU.add, r=["s5_t1"], w=["s5_c8"])
            cs_, sn_ = "c8", "s8"
            for it_ in range(3):
                e2("t1", cs_, cs_, ALU.mult)
                e2("t2", sn_, sn_, ALU.mult)
                e2("ni", cs_, sn_, ALU.mult)
                e2("cr", "t1", "t2", ALU.subtract)
                k.ts("dve", W["ci"][:], W["ni"][:], 2.0, None, ALU.mult, r=["s5_ni"], w=["s5_ci"])
                if it_ < 2:
                    k.copy("dve", W["c8"][:], W["cr"][:], r=["s5_cr"], w=["s5_c8"])
                    k.copy("dve", W["s8"][:], W["ci"][:], r=["s5_ci"], w=["s5_s8"])
            pw = []
            for kk in range(11):
                pw.append((tl("pr%d" % kk), tl("pi%d" % kk), tl("pn%d" % kk)))
            e2("pr0", "er", "cr", ALU.mult)
            e2("pi0", "er", "ci", ALU.mult)
            for kk in range(11):
                k.ts("dve", W["pn%d" % kk][:], W["pi%d" % kk][:], -1.0, None, ALU.mult, r=["s5_pi%d" % kk], w=["s5_pn%d" % kk])
                if kk < 10:
                    e2("t1", "pr%d" % kk, "pr%d" % kk, ALU.mult)
                    e2("t2", "pi%d" % kk, "pi%d" % kk, ALU.mult)
                    e2("pr%d" % (kk + 1), "t1", "t2", ALU.subtract)
                    e2("t1", "pr%d" % kk, "pi%d" % kk, ALU.mult)
                    k.ts("dve", W["pi%d" % (kk + 1)][:], W["t1"][:], 2.0, None, ALU.mult, r=["s5_t1"], w=["s5_pi%d" % (kk + 1)])
            k.ts("dve", W["nr"][:], W["pr0"][:], -1.0, None, ALU.add, r=["s5_pr0"], w=["s5_nr"])
            k.copy("dve", W["ni"][:], W["pi0"][:], r=["s5_pi0"], w=["s5_ni"])
            k.tt("dve", W["t1"][:], lre[:], lre[:], ALU.mult, r=["lre"], w=["s5_t1"])
            k.tt("dve", W["t2"][:], lim[:], lim[:], ALU.mult, r=["lim"], w=["s5_t2"])
            e2("den", "t1", "t2", ALU.add)
            k.op("dve", lambda: nc.vector.reciprocal(out=W["den"][:], in_=W["den"][:]), r=["s5_den"], w=["s5_den"])
            k.tt("dve", W["t1"][:], W["nr"][:], lre[:], ALU.mult, r=["s5_nr", "lre"], w=["s5_t1"])
            k.tt("dve", W["t2"][:], W["ni"][:], lim[:], ALU.mult, r=["s5_ni", "lim"], w=["s5_t2"])
            e2("cfr", "t1", "t2", ALU.add)
            e2("cfr", "cfr", "den", ALU.mult)
            k.tt("dve", W["t1"][:], W["ni"][:], lre[:], ALU.mult, r=["s5_ni", "lre"], w=["s5_t1"])
            k.tt("dve", W["t2"][:], W["nr"][:], lim[:], ALU.mult, r=["s5_nr", "lim"], w=["s5_t2"])
            e2("cfi", "t1", "t2", ALU.subtract)
            e2("cfi", "cfi", "den", ALU.mult)
            k.tt("dve", W["t1"][:], W["pr3"][:], x0r[:], ALU.mult, r=["s5_pr3", "x0r"], w=["s5_t1"])
            k.tt("dve", W["t2"][:], W["pi3"][:], x0i[:], ALU.mult, r=["s5_pi3", "x0i"], w=["s5_t2"])
            e2("lx0r", "t1", "t2", ALU.subtract)
            k.tt("dve", W["t1"][:], W["pi3"][:], x0r[:], ALU.mult, r=["s5_pi3", "x0r"], w=["s5_t1"])
            k.tt("dve", W["t2"][:], W["pr3"][:], x0i[:], ALU.mult, r=["s5_pr3", "x0i"], w=["s5_t2"])
            e2("lx0i", "t1", "t2", ALU.add)
            bre = k.sb([128, 32, 16], F32, "bre", stack=ph)
            bim = k.sb([128, 32, 16], F32, "bim", stack=ph)
            k.dma(bre[:], sb_d[0, :, :].rearrange("(dp gn) c -> gn dp c", gn=128), w=["bre"])
            k.dma(bim[:], sb_d[1, :, :].rearrange("(dp gn) c -> gn dp c", gn=128), w=["bim"])
            dcol = load_cols(sd_d[:, :], 4, "s5d", ph)
            bgl = load_cols(bg_d[:, :], 4, "bglu", ph)
            nst = [k.sb([128, 2, 32], F32, "nst%d" % i, stack=ph) for i in range(2)]
            wuf = k.sb([128, 2048], BF16, "wu", stack=ph)
            wu = wuf[:, :].rearrange("p (a b) -> p a b", a=8)
            for half in range(2):
                k.dma(wu, wine_d[:, half * 256:(half + 1) * 256].rearrange("(kc p) m -> p kc m", p=128), w=["wu"], q="pool")
                for mc2 in range(2):
                    mc = half * 2 + mc2
                    for b in range(NB):
                        sl = slice(b * 512, (b + 1) * 512)
                        p, pk = psum()
                        for kc in range(8):
                            k.mm(p[:, :], wu[:, kc, mc2 * 128:(mc2 + 1) * 128], hT[:, kc, sl], kc == 0, kc == 7,
                                 r=["wu", ("h", b, kc)], w=[pk])
                        k.copy("act", aoT[:, mc, sl], p[:, :], r=[pk], w=[("ao", mc)])
            k.barrier()
            s5tmp = k.sb([128, 2048], F32, "s5tmp", stack=ph)
            cbuf = []
            for c_ in range(2):
                if c_ == 0:
                    cbuf.append((hf32[:, 0:2048], hf32[:, 2048:4096], hf32[:, 4096:6144]))
                else:
                    cbuf.append((hf32[:, 6144:8192], hf32[:, 8192:10240], s5tmp[:, :]))
            bst = [[k.sb([128, 128], F32, "bst%d_%d" % (c_, i), stack=ph) for i in range(1)] for c_ in range(2)]
            cblks = [[k.sb([64, 128], F32, "cblk%d_%d" % (c_, i), stack=ph) for i in range(1)] for c_ in range(2)]
            tabs = [[k.sb([128, 128], BF16, "s5tab%d_%d" % (c_, j), stack=ph) for j in range(4)] for c_ in range(2)]
            bbs = [k.sb([128, 2, 16], F32, "bbar%d" % c_, stack=ph) for c_ in range(2)]
            bt1s = [k.sb([128, 16], F32, "bt1_%d" % c_, stack=ph) for c_ in range(2)]
            for c_ in range(2):
                for i in range(1):
                    k.memset("pool", cblks[c_][i][:], 0.0, w=["cblk%d_%d" % (c_, i)])
            k.ps_pool = [5, 6, 7]

            def rv(ap2, n0, n1):
                sub = ap2[:, n0:n1]
                return bass.AP(sub.tensor, sub.offset + (n1 - n0 - 1), [list(sub.ap[0]), [-1, n1 - n0]])

            def rv3(ap2, nseq, L, n0, n1):
                sub = ap2[:, 0:nseq * L].rearrange("p (s l) -> p s l", s=nseq)[:, :, n0:n1]
                return bass.AP(sub.tensor, sub.offset + (n1 - n0 - 1), [list(sub.ap[0]), list(sub.ap[1]), [-1, n1 - n0]])

            def fw3(ap2, nseq, L, n0, n1):
                return ap2[:, 0:nseq * L].rearrange("p (s l) -> p s l", s=nseq)[:, :, n0:n1]

            ycnt = {}

            def s5_item(c_, kc4, pp, yacc):
                d = c_
                pair = kc4 * 4 + pp
                col = d * 16 + pair
                tb = tabs[c_]
                tbk = ["s5tab%d_%d" % (c_, j) for j in range(4)]
                bb, bt1, bk_, b1k = bbs[c_], bt1s[c_], "bbar%d" % c_, "bt1_%d" % c_
                cfr, cfi = W["cfr"][:, col:col + 1], W["cfi"][:, col:col + 1]
                k.ts("dve", bt1[:], bim[:, col, :], cfi, None, ALU.mult, r=["bim", "s5_cfi"], w=[b1k])
                k.stt("dve", bb[:, 0, :], bre[:, col, :], cfr, bt1[:], ALU.mult, ALU.subtract, r=["bre", "s5_cfr", b1k], w=[bk_])
                k.ts("dve", bt1[:], bre[:, col, :], cfi, None, ALU.mult, r=["bre", "s5_cfi"], w=[b1k])
                k.stt("dve", bb[:, 1, :], bim[:, col, :], cfr, bt1[:], ALU.mult, ALU.add, r=["bim", "s5_cfr", b1k], w=[bk_])
                for ri in range(2):
                    bs_, bsk = bst[c_][0], "bst%d_0" % c_
                    k.memset("pool", bs_[:], 0.0, w=[bsk])
                    for gl in range(2):
                        c0 = (pp * 2 + gl) * 16
                        k.copy("dve", bs_[gl * 64:gl * 64 + 64, c0:c0 + 16], bb[gl * 64:gl * 64 + 64, ri, :], r=[bk_], w=[bsk])
                    p, pk = psum()
                    k.tr(p[:, 0:128], bs_[:], ident[:], r=[bsk, "ident"], w=[pk])
                    k.copy("act", tb[ri][:], p[:, 0:128], r=[pk], w=[tbk[ri]])
                yield
                for ri in range(2):
                    cblk, cbk = cblks[c_][0], "cblk%d_0" % c_
                    for gl in range(2):
                        k.dma(cblk[gl * 32:gl * 32 + 16, gl * 64:gl * 64 + 64],
                              sc_d[ri, (col * 2 + gl) * 16:(col * 2 + gl) * 16 + 16, :], w=[cbk])
                    p, pk = psum()
                    k.tr(p[:, 0:64], cblk[:], ident[0:64, 0:64], r=[cbk, "ident"], w=[pk])
                    k.memset("pool", tb[2 + ri][:], 0.0, w=[tbk[2 + ri]])
                    for gl in range(2):
                        c0 = (pp * 2 + gl) * 16
                        if ri == 0:
                            k.copy("act", tb[2][:, c0:c0 + 16], p[:, gl * 32:gl * 32 + 16], r=[pk], w=[tbk[2]])
                        else:
                            k.act(tb[3][:, c0:c0 + 16], p[:, gl * 32:gl * 32 + 16], AF.Identity, r=[pk], w=[tbk[3]], scale=-1.0)
                yield
                bre_, bim_, btm_ = cbuf[c_]
                kre, kim, ktm = ("sc", c_, 0), ("sc", c_, 1), ("sc", c_, 2)
                for seg in range(2):
                    L = LS if seg == 0 else LP
                    nseq = 1 if seg == 0 else 2
                    t0 = 0 if seg == 0 else LS
                    W_ = L * nseq
                    for b0 in range(0, W_, 512):
                        for ri in range(2):
                            p, pk = psum()
                            k.mm(p[:, :], tb[ri][:], aoT[:, kc4, t0 + b0:t0 + b0 + 512], True, True,
                                 r=[tbk[ri], ("ao", kc4)], w=[pk])
                            k.copy("act", cbuf[c_][ri][:, b0:b0 + 512], p[:, :], r=[pk], w=[("sc", c_, ri)])
                        yield
                    if seg == 0:
                        tcol = 0 if d == 0 else L - 1
                        k.tt("dve", bre_[:, tcol:tcol + 1], bre_[:, tcol:tcol + 1], W["lx0r"][:, col:col + 1], ALU.add,
                             r=[kre, "s5_lx0r"], w=[kre])
                        k.tt("dve", bim_[:, tcol:tcol + 1], bim_[:, tcol:tcol + 1], W["lx0i"][:, col:col + 1], ALU.add,
                             r=[kim, "s5_lx0i"], w=[kim])
                    nlev = 11 if seg == 0 else 8
                    for lv in range(nlev):
                        sh = 1 << lv
                        a_ = W["pr%d" % lv][:, col:col + 1]
                        b_ = W["pi%d" % lv][:, col:col + 1]
                        nb_ = W["pn%d" % lv][:, col:col + 1]
                        if d == 0:
                            V = lambda ap2, n0, n1: rv3(ap2, nseq, L, n0, n1)
                            hi, shd = (sh, L), (0, L - sh)
                        else:
                            V = lambda ap2, n0, n1: fw3(ap2, nseq, L, n0, n1)
                            hi, shd = (0, L - sh), (sh, L)
                        pk_ = ["s5_pr%d" % lv, "s5_pi%d" % lv, "s5_pn%d" % lv]
                        k.act(V(btm_, *hi), V(bre_, *shd), AF.Identity, r=[kre] + pk_, w=[ktm], scale=b_)
                        k.stt("dve", V(bre_, *hi), V(bre_, *shd), a_, V(bre_, *hi), ALU.mult, ALU.add, r=[kre, ktm] + pk_, w=[kre])
                        k.stt("dve", V(bre_, *hi), V(bim_, *shd), nb_, V(bre_, *hi), ALU.mult, ALU.add, r=[kre, kim] + pk_, w=[kre])
                        k.stt("dve", V(bim_, *hi), V(bim_, *shd), a_, V(bim_, *hi), ALU.mult, ALU.add, r=[kim] + pk_, w=[kim])
                        k.tt("pool", V(bim_, *hi), V(bim_, *hi), V(btm_, *hi), ALU.add, r=[kim, ktm], w=[kim])
                        yield
                    if seg == 1:
                        lc = L - 1 if d == 0 else 0
                        for ri in range(2):
                            k.copy("dve", nst[ri][:, :, col:col + 1],
                                   cbuf[c_][ri][:, 0:W_].rearrange("p (s l) -> p s l", s=2)[:, :, lc:lc + 1],
                                   r=[("sc", c_, ri)], w=["nst%d" % ri])
                    xb = btm_.bitcast(BF16)
                    for ri in range(2):
                        k.copy("act", xb[:, ri * 2048:ri * 2048 + W_], cbuf[c_][ri][:, 0:W_], r=[("sc", c_, ri), ktm], w=[ktm])
                    yield
                    for b0 in range(0, W_, 512):
                        bi = (t0 + b0) // 512
                        for ri in range(2):
                            n_ = ycnt.get((kc4, bi), 0)
                            ycnt[(kc4, bi)] = n_ + 1
                            k.mm(yacc[bi][0][:, :], tb[2 + ri][:], xb[:, ri * 2048 + b0:ri * 2048 + b0 + 512], n_ == 0, n_ == 15,
                                 r=[tbk[2 + ri], ktm], w=[yacc[bi][1]])
                    yield


            NBK = 320
            hbf = hflat
            creg = {"off": 0}

            def hcarve(ncols_f32, dt=F32):
                o = creg["off"]
                creg["off"] += ncols_f32
                assert creg["off"] <= 10240, creg["off"]
                ap = hf32[:, o:o + ncols_f32]
                return ap if dt == F32 else ap.bitcast(BF16)

            CHN = []
            for c_ in range(2):
                dct = {}
                dct["E"] = [[hcarve(64, BF16) for ri in range(2)] for _ in range(8)]
                dct["F"] = [[hcarve(64, BF16) for ri in range(2)] for _ in range(8)]
                dct["K"] = [hcarve(64, BF16) for _ in range(8)]
                dct["X"] = [hcarve(NBK) for _ in range(3)]
                dct["Xp"] = [hcarve(NBK // 2, BF16) for _ in range(2)]
                dct["Bp"] = [hcarve(128) for _ in range(2)]
                dct["Gp"] = [hcarve(128) for _ in range(2)]
                dct["T"] = [hcarve(128) for _ in range(2)]
                dct["TG"] = [hcarve(128) for _ in range(2)]
                dct["Bb"] = [hcarve(64, BF16) for _ in range(2)]
                dct["Cb"] = [hcarve(64, BF16) for _ in range(2)]
                CHN.append(dct)

            def cmul_inplace(eng, re_, im_, t1, t2, a_, b_, keys, rkeys=()):
                rk = list(keys) + list(rkeys)
                k.ts(eng, t1, im_, b_, None, ALU.mult, r=rk, w=keys)
                k.ts(eng, t2, re_, b_, None, ALU.mult, r=rk, w=keys)
                k.stt(eng, re_, re_, a_, t1, ALU.mult, ALU.subtract, r=rk, w=keys)
                k.stt(eng, im_, im_, a_, t2, ALU.mult, ALU.add, r=rk, w=keys)

            def s5_item2(c_, kc4, pp, yacc):
                d = c_
                pair = kc4 * 4 + pp
                col = d * 16 + pair
                H = CHN[c_]
                kt = "s5c%d_tab" % c_
                kBp, kGp, kTB, kTG, kBb, kCb = ["s5c%d_%s" % (c_, n_) for n_ in ("Bp", "Gp", "TB", "TG", "Bb", "Cb")]
                kE = lambda w__: "s5c%d_E%d" % (c_, w__)
                kF = lambda w__: "s5c%d_F%d" % (c_, w__)
                kK = lambda w__: "s5c%d_K%d" % (c_, w__)
                allE = [kE(w__) for w__ in range(8)]
                allF = [kF(w__) for w__ in range(8)]
                allK = [kK(w__) for w__ in range(8)]
                kx = "s5c%d_x" % c_
                bb, bt1, bk_, b1k = bbs[c_], bt1s[c_], "bbar%d" % c_, "bt1_%d" % c_
                cfr, cfi = W["cfr"][:, col:col + 1], W["cfi"][:, col:col + 1]
                a1, b1 = W["pr0"][:, col:col + 1], W["pi0"][:, col:col + 1]
                pk0 = ["s5_pr0", "s5_pi0"]
                k.ts("dve", bt1[:], bim[:, col, :], cfi, None, ALU.mult, r=["bim", "s5_cfi"], w=[b1k])
                k.stt("dve", bb[:, 0, :], bre[:, col, :], cfr, bt1[:], ALU.mult, ALU.subtract, r=["bre", "s5_cfr", b1k], w=[bk_])
                k.ts("dve", bt1[:], bre[:, col, :], cfi, None, ALU.mult, r=["bre", "s5_cfi"], w=[b1k])
                k.stt("dve", bb[:, 1, :], bim[:, col, :], cfr, bt1[:], ALU.mult, ALU.add, r=["bim", "s5_cfr", b1k], w=[bk_])
                for ri in range(2):
                    k.memset("pool", H["Bp"][ri], 0.0, w=[kBp])
                    for gl in range(2):
                        c0 = (pp * 2 + gl) * 16
                        k.copy("dve", H["Bp"][ri][gl * 64:gl * 64 + 64, c0:c0 + 16], bb[gl * 64:gl * 64 + 64, ri, :], r=[bk_], w=[kBp])
                for ri in range(2):
                    cblk, cbk = cblks[c_][0], "cblk%d_0" % c_
                    for gl in range(2):
                        k.dma(cblk[gl * 32:gl * 32 + 16, gl * 64:gl * 64 + 64],
                              sc_d[ri, (col * 2 + gl) * 16:(col * 2 + gl) * 16 + 16, :], w=[cbk])
                    p, pk = psum()
                    k.tr(p[:, 0:64], cblk[:], ident[0:64, 0:64], r=[cbk, "ident"], w=[pk])
                    k.memset("pool", H["Gp"][ri], 0.0, w=[kGp])
                    for gl in range(2):
                        c0 = (pp * 2 + gl) * 16
                        k.copy("act", H["Gp"][ri][:, c0:c0 + 16], p[:, gl * 32:gl * 32 + 16], r=[pk], w=[kGp])
                k.copy("act", H["Cb"][0], H["Gp"][0], r=[kGp], w=[kCb])
                k.act(H["Cb"][1], H["Gp"][1], AF.Identity, r=[kGp], w=[kCb], scale=-1.0)
                yield
                for w_ in range(8):
                    if w_ > 0:
                        cmul_inplace("dve", H["Bp"][0], H["Bp"][1], H["T"][0], H["T"][1], a1, b1, [kBp, kTB], pk0)
                    cmul_inplace("dve", H["Gp"][0], H["Gp"][1], H["TG"][0], H["TG"][1], a1, b1, [kGp, kTG], pk0)
                    for ri in range(2):
                        p, pk = psum()
                        k.tr(p[:, 0:128], H["Bp"][ri], ident[:], r=[kBp, "ident"], w=[pk])
                        k.copy("act", H["E"][w_][ri], p[:, 0:128], r=[pk], w=[kE(w_)])
                        k.copy("pool", H["Bb"][ri], H["Bp"][ri], r=[kBp], w=[kBb])
                    k.copy("pool", H["F"][w_][0], H["Gp"][0], r=[kGp], w=[kF(w_)])
                    k.act(H["F"][w_][1], H["Gp"][1], AF.Identity, r=[kGp], w=[kF(w_)], scale=-1.0)
                    p, pk = psum()
                    k.mm(p[:, 0:128], H["Bb"][0], H["Cb"][0], True, False, r=[kBb, kCb], w=[pk])
                    k.mm(p[:, 0:128], H["Bb"][1], H["Cb"][1], False, True, r=[kBb, kCb], w=[pk])
                    k.copy("act", H["K"][w_], p[:, 0:128], r=[pk], w=[kK(w_)])
                    if w_ % 2 == 1:
                        yield
                Xre, Xim, Xtm = H["X"]
                for seg in range(2):
                    J = 256 if seg == 0 else 64
                    t0 = 0 if seg == 0 else LS
                    j0 = 0 if seg == 0 else 256
                    for ri in range(2):
                        p, pk = psum()
                        for s_ in range(8):
                            w_ = (7 - s_) if d == 0 else s_
                            rhs = aoT[:, kc4, t0:t0 + 8 * J].rearrange("p (j s) -> p j s", s=8)[:, :, s_]
                            k.mm(p[:, 0:J], H["E"][w_][ri], rhs, s_ == 0, s_ == 7, r=[kE(w_), ("ao", kc4)], w=[pk])
                        k.copy("act", H["X"][ri][:, j0:j0 + J], p[:, 0:J], r=[pk], w=[kx])
                    yield
                jc = 0 if d == 0 else 255
                k.tt("dve", Xre[:, jc:jc + 1], Xre[:, jc:jc + 1], W["lx0r"][:, col:col + 1], ALU.add, r=[kx, "s5_lx0r"], w=[kx])
                k.tt("dve", Xim[:, jc:jc + 1], Xim[:, jc:jc + 1], W["lx0i"][:, col:col + 1], ALU.add, r=[kx, "s5_lx0i"], w=[kx])
                for seg in range(2):
                    nseq, Lb, j0 = (1, 256, 0) if seg == 0 else (2, 32, 256)
                    nlev = 8 if seg == 0 else 5
                    sub = lambda ap2: ap2[:, j0:j0 + nseq * Lb]
                    for lv in range(nlev):
                        sh = 1 << lv
                        pw = lv + 3
                        a_ = W["pr%d" % pw][:, col:col + 1]
                        b_ = W["pi%d" % pw][:, col:col + 1]
                        nb_ = W["pn%d" % pw][:, col:col + 1]
                        if d == 0:
                            V = lambda ap2, n0, n1: rv3(sub(ap2), nseq, Lb, n0, n1)
                            hi, shd = (sh, Lb), (0, Lb - sh)
                        else:
                            V = lambda ap2, n0, n1: fw3(sub(ap2), nseq, Lb, n0, n1)
                            hi, shd = (0, Lb - sh), (sh, Lb)
                        pk_ = ["s5_pr%d" % pw, "s5_pi%d" % pw, "s5_pn%d" % pw]
                        k.ts("dve", V(Xtm, *hi), V(Xre, *shd), b_, None, ALU.mult, r=[kx] + pk_, w=[kx])
                        k.stt("dve", V(Xre, *hi), V(Xre, *shd), a_, V(Xre, *hi), ALU.mult, ALU.add, r=[kx] + pk_, w=[kx])
                        k.stt("dve", V(Xre, *hi), V(Xim, *shd), nb_, V(Xre, *hi), ALU.mult, ALU.add, r=[kx] + pk_, w=[kx])
                        k.stt("dve", V(Xim, *hi), V(Xim, *shd), a_, V(Xim, *hi), ALU.mult, ALU.add, r=[kx] + pk_, w=[kx])
                        k.tt("dve", V(Xim, *hi), V(Xim, *hi), V(Xtm, *hi), ALU.add, r=[kx], w=[kx])
                    yield
                for ri in range(2):
                    pv = H["X"][ri][:, 256:320].rearrange("p (s l) -> p s l", s=2)
                    lc = 31 if d == 0 else 0
                    k.copy("dve", nst[ri][:, :, col:col + 1], pv[:, :, lc:lc + 1], r=[kx], w=["nst%d" % ri])
                    xp = H["Xp"][ri]
                    x0col = (x0r if ri == 0 else x0i)[:, col:col + 1]
                    if d == 0:
                        k.copy("act", xp[:, 1:256], H["X"][ri][:, 0:255], r=[kx], w=[kx])
                        k.copy("dve", xp[:, 0:1], x0col, r=[kx, "x0r", "x0i"], w=[kx])
                        xpv = xp[:, 256:320].rearrange("p (s l) -> p s l", s=2)
                        k.copy("act", xpv[:, :, 1:32], pv[:, :, 0:31], r=[kx], w=[kx])
                        k.memset("pool", xpv[:, :, 0:1], 0.0, w=[kx])
                    else:
                        k.copy("act", xp[:, 0:255], H["X"][ri][:, 1:256], r=[kx], w=[kx])
                        k.copy("dve", xp[:, 255:256], x0col, r=[kx, "x0r", "x0i"], w=[kx])
                        xpv = xp[:, 256:320].rearrange("p (s l) -> p s l", s=2)
                        k.copy("act", xpv[:, :, 0:31], pv[:, :, 1:32], r=[kx], w=[kx])
                        k.memset("pool", xpv[:, :, 31:32], 0.0, w=[kx])
                yield
                for bi in range(5):
                    jb = bi * 64
                    tb0 = bi * 512
                    yb, ybk = yacc[bi]
                    yv = yb[:, :].rearrange("p (j s) -> p j s", s=8)
                    uv = lambda off: aoT[:, kc4, tb0 + off:tb0 + off + 505].rearrange("p (j s) -> p j s", s=8) if False else None
                    mms = []
                    for s_ in range(8):
                        w_ = s_ if d == 0 else 7 - s_
                        for ri in range(2):
                            mms.append((yv[:, :, s_], H["F"][w_][ri], H["Xp"][ri][:, jb:jb + 64]))
                    ub = aoT[:, kc4, tb0:tb0 + 512].rearrange("p (j s) -> p j s", s=8)
                    for tau in range(8):
                        if d == 0:
                            mms.append((yv[:, :, tau:8], H["K"][tau], ub[:, :, 0:8 - tau]))
                        else:
                            mms.append((yv[:, :, 0:8 - tau], H["K"][tau], ub[:, :, tau:8]))
                    for (o_, l_, r_) in mms:
                        n_ = ycnt.get((kc4, bi), 0)
                        ycnt[(kc4, bi)] = n_ + 1
                        k.mm(o_, l_, r_, n_ == 0, n_ == 8 * 24 - 1, r=allF + allK + [kx, ("ao", kc4)], w=[ybk])
                    yield

            def s5_chain(c_, kc4, yacc):
                for pp in range(4):
                    yield from s5_item2(c_, kc4, pp, yacc)

            for kc4 in range(4):
                yacc = [(ps[b], ("ps", b)) for b in range(5)]
                gens = [s5_chain(0, kc4, yacc), s5_chain(1, kc4, yacc)]
                while gens:
                    for g_ in list(gens):
                        try:
                            next(g_)
                        except StopIteration:
                            gens.remove(g_)
                for b in range(NB):
                    sl = slice(b * 512, (b + 1) * 512)
                    yv, y2, y3 = tmpb[0], tmpb[1], tmpb[2]
                    k.stt("dve", yv[:], aoT[:, kc4, sl], dcol[:, kc4:kc4 + 1], yacc[b][0][:, :], ALU.mult, ALU.add,
                          r=[("ao", kc4), "s5d", yacc[b][1]], w=["ntmp0"])
                    k.tt("pool", y2[:], yv[:], yv[:], ALU.mult, r=["ntmp0"], w=["ntmp1"])
                    k.ts("pool", y2[:], y2[:], 0.044715, 1.0, ALU.mult, ALU.add, r=["ntmp1"], w=["ntmp1"])
                    k.tt("pool", y2[:], y2[:], yv[:], ALU.mult, r=["ntmp1", "ntmp0"], w=["ntmp1"])
                    k.act(y3[:], y2[:], AF.Sigmoid, r=["ntmp1"], w=["ntmp2"], scale=1.5957691216057308)
                    k.tt("dve", aoT[:, kc4, sl], yv[:], y3[:], ALU.mult, r=["ntmp0", "ntmp2"], w=[("ao", kc4)])
            k.ps_pool = list(range(8))
            wg = wuf[:, :].rearrange("p (a b) -> p a b", a=4)
            k.dma(wg, wglu_d[:, :].rearrange("(kc p) m -> p kc m", p=128), w=["wu"], q="pool")
            for b in range(NB):
                sl = slice(b * 512, (b + 1) * 512)
                pg = []
                for mc in range(4):
                    p, pk = psum()
                    pg.append((p, pk))
                    for kc in range(4):
                        k.mm(p[:, :], wg[:, kc, mc * 128:(mc + 1) * 128], aoT[:, kc, sl], kc == 0, kc == 3,
                             r=["wu", ("ao", kc)], w=[pk])
                for mc in range(4):
                    p, pk = pg[mc]
                    k.act(tmpb[mc % 3][:], p[:, :], AF.Sigmoid, r=[pk, "bglu"], w=["ntmp%d" % (mc % 3)], bias=bgl[:, mc:mc + 1], scale=1.0)
                    k.tt("dve", aoT[:, mc, sl], aoT[:, mc, sl], tmpb[mc % 3][:], ALU.mult, r=[("ao", mc), "ntmp%d" % (mc % 3)], w=[("ao", mc)])
            for ri, dst_d in ((0, nsre_d), (1, nsim_d)):
                p, pk = psum()
                k.tr(p[0:64, 0:128], nst[ri][:, :, :].rearrange("p s c -> p (s c)"), ident[:], r=["nst%d" % ri, "ident"], w=[pk])
                stt_ = tmpb[ri][0:64, 0:128]
                k.copy("dve", stt_, p[0:64, 0:128], r=[pk], w=["ntmp%d" % ri])
                k.dma(dst_d[:, :], stt_, r=["ntmp%d" % ri])
            k.barrier()
            ph.close()
            ph = ph_keep
        wo = [k.sb([128, 8, 128], BF16, "wo%d" % i, stack=ph) for i in range(2)]
        for mc in range(8):
            wt, wk_ = wo[mc % 2], "wo%d" % (mc % 2)
            k.dma(wt[:], woute_d[:, mc * 128:(mc + 1) * 128].rearrange("(kc p) m -> p kc m", p=128), w=[wk_], q="pool")
            for b in range(NB):
                cond = 0 if b < 4 else 1
                sl = slice(b * 512, (b + 1) * 512)
                p, pk = psum()
                kcs = list(range(4, 8)) if stage == "noS5" else list(range(8))
                for j, kc in enumerate(kcs):
                    src = aoT[:, kc, sl] if kc < 4 else boT[:, kc - 4, sl]
                    k.mm(p[:, :], wt[:, kc, :], src, j == 0, j == len(kcs) - 1,
                         r=[wk_, ("ao", kc) if kc < 4 else ("bo", kc - 4)], w=[pk])
                k.stt("dve", xT[:, mc, sl], p[:, :], gate_ap(0, 0, mc, cond), xT[:, mc, sl], ALU.mult, ALU.add,
                      r=[pk, "modv0", xk(b, mc)], w=[xk(b, mc)])
        k.barrier()
        ph.close()

    SL = {"m1a": 1, "m1b": 2, "m1c": 3, "m1d": 4, "m1e": 5}.get(stage[:3], 9)
    PL = int(stage[3:]) if (stage.startswith("m1d") and len(stage) > 3) else 9

    def mixer1():
        ph = ExitStack()
        for b in range(NB):
            norm_block(1, 0, b)
        SEGS = [(0, 16), (16, 2), (18, 2)]
        xflat = xT[:, :, :].rearrange("p c t -> p (c t)")
        allx = [xk(b, c) for b in range(NB) for c in range(8)]
        k.dma(xsp_d[:, :], xflat, r=allx)
        k.barrier()
        carve = {"off": 0}

        def cv(shape, dt=F32):
            n = 1
            for s_ in shape[1:]:
                n *= s_
            nf = n if dt == F32 else (n + 1) // 2
            ap = xflat[:, carve["off"]:carve["off"] + nf]
            carve["off"] += nf
            assert carve["off"] <= 8 * T, carve["off"]
            if dt != F32:
                ap = ap.bitcast(dt)[:, 0:n]
            if len(shape) == 3:
                ap = ap.rearrange("p (a b) -> p a b", a=shape[1])
            return ap[0:shape[0]]

        class _CV:
            def __init__(self, ap):
                self.ap = ap
            def __getitem__(self, key):
                return self.ap[key]

        def cvt_(shape, dt=F32):
            return _CV(cv(shape, dt))
        def cst(shape, name):
            return k.sb(shape, F32, name, stack=ph)
        d128 = cst([128, 128], "d128")
        k.op("pool", lambda: nc.gpsimd.iota(d128[:], pattern=[[1, 128]], base=0, channel_multiplier=-1,
                                              allow_small_or_imprecise_dtypes=True), w=["d128"])
        blk = cst([128, 128], "blk")
        k.memset("dve", blk[:], 0.0, w=["blk"])
        k.memset("dve", blk[0:64, 0:64], 1.0, w=["blk"])
        k.memset("dve", blk[64:128, 64:128], 1.0, w=["blk"])
        TRI = []
        for d in range(2):
            t_ = cst([128, 128], "tri%d" % d)
            k.ts("dve", t_[:], d128[:], 0.0, None, ALU.is_ge if d == 0 else ALU.is_le, r=["d128"], w=["tri%d" % d])
            k.tt("dve", t_[:], t_[:], blk[:], ALU.mult, r=["tri%d" % d, "blk"], w=["tri%d" % d])
            TRI.append(t_)
        HALF = []
        for hb in range(2):
            t_ = cst([128, 128], "half%d" % hb)
            k.memset("dve", t_[:], 0.0, w=["half%d" % hb])
            k.memset("dve", t_[hb * 64:hb * 64 + 64, :], 1.0, w=["half%d" % hb])
            HALF.append(t_)
        def Rr(ap):
            return ap
        SA128, BIAS_r, NSTR128, LE_r = [], [], [], []
        for d in range(2):
            t_ = cst([128, 128], "sa128_%d" % d)
            k.ts("dve", t_[:], d128[:], 0.0, None, ALU.is_lt if d == 0 else ALU.is_gt, r=["d128"], w=["sa128_%d" % d])
            k.tt("dve", t_[:], t_[:], blk[:], ALU.mult, r=["sa128_%d" % d, "blk"], w=["sa128_%d" % d])
            SA128.append(t_)
            t_ = cst([128, 128], "ler%d" % d)
            k.copy("dve", Rr(t_[:]), TRI[d][:], r=["tri%d" % d], w=["ler%d" % d])
            LE_r.append(t_)
            t_ = cst([128, 128], "biasr%d" % d)
            k.ts("dve", Rr(t_[:]), TRI[d][:], -1.0, 30000.0, ALU.add, ALU.mult, r=["tri%d" % d], w=["biasr%d" % d])
            BIAS_r.append(t_)
            t_ = cst([128, 128], "nstr128_%d" % d)
            k.ts("dve", t_[:], d128[:], 0.0, None, ALU.is_gt if d == 0 else ALU.is_lt, r=["d128"], w=["nstr128_%d" % d])
            k.stt("dve", t_[:], t_[:], -1.0, blk[:], ALU.mult, ALU.mult, r=["nstr128_%d" % d, "blk"], w=["nstr128_%d" % d])
            NSTR128.append(t_)
        identR = cst([128, 128], "identR")
        k.copy("dve", Rr(identR[:]), ident[:], r=["ident"], w=["identR"])
        b32 = cst([128, 128], "b32")
        k.memset("dve", b32[:], 0.0, w=["b32"])
        for q4 in range(4):
            k.memset("dve", b32[q4 * 32:q4 * 32 + 32, q4 * 32:q4 * 32 + 32], 1.0, w=["b32"])
        ident_bf = k.sb([128, 128], BF16, "ident_bf", stack=ph)
        k.copy("dve", ident_bf[:], ident[:], r=["ident"], w=["ident_bf"])
        ones_f = cst([1, 128], "ones_f")
        k.memset("dve", ones_f[:], 1.0, w=["ones_f"])
        grow = cst([1, 32], "grow")
        k.dma(grow[0:1, 0:16], alog_d[0:1, :], w=["grow"])
        k.dma(grow[0:1, 16:32], dtb_d[0:1, :], w=["grow"])
        p, pk = psum()
        k.mm(p[:, 0:32], ones_f[0:1, :], grow[0:1, :], True, True, r=["ones_f", "grow"], w=[pk])
        gcb = cst([128, 32], "gcb")
        k.act(gcb[:, 0:16], p[:, 0:16], AF.Exp, r=[pk], w=["gcb"])
        k.ts("dve", gcb[:, 0:16], gcb[:, 0:16], -1.0, None, ALU.mult, r=["gcb"], w=["gcb"])
        k.copy("dve", gcb[:, 16:32], p[:, 16:32], r=[pk], w=["gcb"])
        cw = load_cols(cw_d[:, :], 120, "convw", ph)
        ngc = load_cols(ng_d[:, :], 1, "dnng", ph)
        wgp = k.sb([128, 8, 32], BF16, "wgp", stack=ph)
        k.dma(wgp[:], wino_d[:, 4096:4128].rearrange("(kc p) m -> p kc m", p=128), w=["wgp"], q="pool")
        TS = {}
        for nme in ("g", "sqb", "gam", "egam", "c1", "kdec"):
            TS[nme] = cvt_([128, 20, 16])
        etot = cvt_([128, 40, 16])
        for t in (range(20) if SL >= 2 else []):
            p, pk = psum()
            for kc in range(8):
                k.mm(p[:, 0:32], hT[:, kc, t * 128:(t + 1) * 128], wgp[:, kc, :], kc == 0, kc == 7,
                     r=["wgp", ("h", t // 4, kc)], w=[pk])
            g_, sq_ = TS["g"][:, t, :], TS["sqb"][:, t, :]
            k.tt("dve", g_, p[:, 0:16], gcb[:, 16:32], ALU.add, r=[pk, "gcb"], w=["ts_g"])
            k.act(g_, g_, AF.Exp, r=["ts_g"], w=["ts_g"])
            k.act(g_, g_, AF.Ln, r=["ts_g"], w=["ts_g"], bias=1.0, scale=1.0)
            k.tt("dve", g_, g_, gcb[:, 0:16], ALU.mult, r=["ts_g", "gcb"], w=["ts_g"])
            k.act(sq_, p[:, 16:32], AF.Sigmoid, r=[pk], w=["ts_sqb"])
            k.act(sq_, sq_, AF.Sqrt, r=["ts_sqb"], w=["ts_sqb"])
            p2, p2k = psum()
            for d in range(2):
                k.mm(p2[:, d * 8:d * 8 + 8], TRI[d][:], TS["g"][:, t, d * 8:d * 8 + 8], True, True, r=["tri%d" % d, "ts_g"], w=[p2k])
            k.mm(p2[:, 16:32], blk[:], TS["g"][:, t, :], True, True, r=["blk", "ts_g"], w=[p2k])
            for hb in range(2):
                k.mm(p2[:, 32 + hb * 16:48 + hb * 16], HALF[hb][:], TS["g"][:, t, :], True, True, r=["half%d" % hb, "ts_g"], w=[p2k])
            k.copy("dve", TS["gam"][:, t, :], p2[:, 0:16], r=[p2k], w=["ts_gam"])
            k.act(TS["egam"][:, t, :], p2[:, 0:16], AF.Exp, r=[p2k], w=["ts_egam"])
            k.tt("dve", TS["kdec"][:, t, :], p2[:, 16:32], TS["gam"][:, t, :], ALU.subtract, r=[p2k, "ts_gam"], w=["ts_kdec"])
            k.act(TS["kdec"][:, t, :], TS["kdec"][:, t, :], AF.Exp, r=["ts_kdec"], w=["ts_kdec"])
            k.act(etot[:, 2 * t:2 * t + 2, :], p2[:, 32:64].rearrange("p (a c) -> p a c", a=2), AF.Exp, r=[p2k], w=["etot"])
            k.stt("dve", TS["c1"][:, t, :], TS["sqb"][:, t, :], -1.0, TS["egam"][:, t, :], ALU.mult, ALU.mult,
                  r=["ts_sqb", "ts_egam"], w=["ts_c1"])
        qhT = cvt_([128, T])
        khT = cvt_([128, T])
        vtm = cvt_([128, 20, 128])
        szT = cvt_([128, T], BF16)
        og_all = k.sb([128, 8, T], BF16, "og_all", stack=ph)
        otm = cvt_([128, 20, 128])
        Sst = [k.sb([128, 128], F32, "Sst%d" % i, stack=ph) for i in range(2)]
        NR = 2
        CH = {}
        Sbf = [cvt_([128, 128], BF16) for i in range(2)]
        for nme in ("kdk", "ksq", "kpT", "sv", "gM", "decT", "Bt", "Bn", "attnT", "Tt", "P", "Pt", "tmpm",
                    "Btd", "Bnd", "Bno", "M1", "Tdn", "qb", "kb"):
            shp = [128, 128]
            if nme in ("kpT", "gM", "Bt", "Bn", "Btd", "Bnd", "Bno", "P", "Pt", "Tt", "Tdn", "M1"):
                CH[nme] = [k.sb(shp, F32, "chr_%s%d" % (nme, i), stack=ph) for i in range(NR)]
            elif nme in ("attnT", "kdk", "qb", "kb"):
                CH[nme] = [cvt_(shp, BF16) for i in range(NR)]
            else:
                CH[nme] = [cvt_(shp) for i in range(NR)]
        rhs_t = [cvt_([128, 128]) for i in range(2)]
        vn_t = [cvt_([128, 128], BF16) for i in range(2)]
        o2_t = [cvt_([128, 128]) for i in range(2)]
        on_t = [cvt_([128, 128]) for i in range(2)]
        sm1 = [k.sb([128, 4], F32, "gsm%d" % i, stack=ph) for i in range(2)]
        rot = {"pre": 0, "seq": 0, "fin": 0}
        ppool = {0: [0, [0, 1, 2, 3]], 1: [0, [4, 5, 6, 7]]}

        def psd(d):
            st = ppool[d]
            st[0] = (st[0] + 1) % 4
            i_ = st[1][st[0]]
            return ps[i_], ("ps", i_)

        for h in (range(8) if SL >= 3 else []):
            if h == 0:
                wpn = cvt_([128, 8, 256], BF16)
                xpd = cvt_([128, T + 12], BF16)
                dg = [cvt_([128, 128], BF16) for j in range(5)]
            def load_panel(w_, hq=None):
                hq = h if hq is None else hq
                slot = w_ % 2
                k.dma(wpn[:, :, slot * 128:(slot + 1) * 128],
                      wino_d[:, w_ * 1024 + hq * 128:w_ * 1024 + (hq + 1) * 128].rearrange("(kc p) m -> p kc m", p=128),
                      w=["gwpn%d" % slot], q="pool")
            if h == 0:
                load_panel(3)
                load_panel(0)
            k.memset("pool", xpd[:], 0.0, w=["gxpd"])
            for b in range(NB):
                sl = slice(b * 512, (b + 1) * 512)
                p, pk = psum()
                for kc in range(8):
                    k.mm(p[:, :], wpn[:, kc, 128:256], hT[:, kc, sl], kc == 0, kc == 7, r=["gwpn1", ("h", b, kc)], w=[pk])
                k.act(szT[:, sl], p[:, :], AF.Silu, r=[pk], w=["szT"])
            load_panel(1)
            for which in range(3):
                wsl = slice((which % 2) * 128, (which % 2) * 128 + 128)
                for b in range(NB):
                    sl = slice(b * 512, (b + 1) * 512)
                    p, pk = psum()
                    for kc in range(8):
                        k.mm(p[:, :], wpn[:, kc, wsl], hT[:, kc, sl], kc == 0, kc == 7,
                             r=["gwpn%d" % (which % 2), ("h", b, kc)], w=[pk])
                    if b < 4:
                        k.copy("act", xpd[:, 2 + b * 512:2 + (b + 1) * 512], p[:, :], r=[pk], w=["gxpd"])
                    else:
                        k.copy("act", xpd[:, LS + 6:LS + 6 + LP], p[:, 0:LP], r=[pk], w=["gxpd"])
                        k.copy("act", xpd[:, LS + LP + 10:LS + 2 * LP + 10], p[:, LP:2 * LP], r=[pk], w=["gxpd"])
                if which == 0:
                    load_panel(2)
                if which == 2 and h < 7:
                    load_panel(0, h + 1)
                    load_panel(3, h + 1)
                for j in range(5):
                    cidx = j * 24 + which * 8 + h
                    k.ts("dve", dg[j][:], ident_bf[:], cw[:, cidx:cidx + 1], None, ALU.mult, r=["ident_bf", "convw"], w=["gdg%d" % j])
                for b in range(NB):
                    p, pk = psum()
                    pieces = [(0, 512, b * 512)] if b < 4 else [(0, LP, LS + 4), (LP, LP, LS + LP + 8)]
                    for (c0, n_, x0) in pieces:
                        for j in range(5):
                            k.mm(p[:, c0:c0 + n_], dg[j][:], xpd[:, x0 + j:x0 + j + n_], j == 0, j == 4,
                                 r=["gdg%d" % j, "gxpd"], w=[pk])
                    sl = slice(b * 512, (b + 1) * 512)
                    if which < 2:
                        dst = qhT if which == 0 else khT
                        dk_ = "qhT" if which == 0 else "khT"
                        k.act(dst[:, sl], p[:, :], AF.Silu, r=[pk], w=[dk_])
                        i = cnt["sq"] % 2
                        cnt["sq"] += 1
                        k.act(sqb[i][:], dst[:, sl], AF.Square, r=[dk_], w=["sq%d" % i])
                        pn, pnk = psum()
                        k.mm(pn[:, :], ones_bf[:], sqb[i][:], True, True, r=["ones", "sq%d" % i], w=[pnk])
                        k.act(tmpb[0][:], pn[:, :], AF.Sqrt, r=[pnk], w=["ntmp0"], bias=EPS, scale=1.0)
                        k.op("dve", lambda: nc.vector.reciprocal(out=tmpb[0][:], in_=tmpb[0][:]), r=["ntmp0"], w=["ntmp0"])
                        k.stt("dve", dst[:, sl], dst[:, sl], (128.0 ** -0.5) if which == 0 else 1.0, tmpb[0][:], ALU.mult, ALU.mult,
                              r=[dk_, "ntmp0"], w=[dk_])
                    else:
                        k.act(tmpb[1][:], p[:, :], AF.Silu, r=[pk], w=["ntmp1"])
                        pt_, ptk = psum()
                        for tt_ in range(4):
                            k.tr(pt_[:, tt_ * 128:(tt_ + 1) * 128], tmpb[1][:, tt_ * 128:(tt_ + 1) * 128], ident[:],
                                 r=["ntmp1", "ident"], w=[ptk])
                        k.copy("dve", vtm[:, b * 4:b * 4 + 4, :], pt_[:, :].rearrange("p (t c) -> p t c", t=4), r=[ptk], w=["gvtm"])
            def precompute(d, t):
                col = d * 8 + h
                i = d
                B_ = {n_: (CH[n_][i], "ch_%s%d" % (n_, i)) for n_ in CH}
                T_ = lambda n_: B_[n_][0][:]
                K_ = lambda n_: B_[n_][1]
                tok = slice(t * 128, (t + 1) * 128)
                sc = lambda n_: TS[n_][:, t, col:col + 1]
                k.ts("pool", T_("sv"), vtm[:, t, :], sc("sqb"), None, ALU.mult, r=["gvtm", "ts_sqb"], w=[K_("sv")])
                p, pk = psd(d)
                k.tr(p[:, 0:128], khT[:, tok], ident[:], r=["khT", "ident"], w=[pk])
                k.act(T_("kdk"), p[:, 0:128], AF.Identity, r=[pk, "ts_kdec"], w=[K_("kdk")], scale=sc("kdec"))
                k.copy("act", T_("qb"), qhT[:, tok], r=["qhT"], w=[K_("qb")])
                k.copy("pool", T_("kb"), khT[:, tok], r=["khT"], w=[K_("kb")])
                yield
                k.ts("dve", Rr(T_("gM")), SA128[d][:], TS["g"][:, t, col:col + 1], None, ALU.mult, r=["sa128_%d" % d, "ts_g"], w=[K_("gM")])
                pD, pDk = psd(d)
                k.mm(pD[:, 0:128], Rr(T_("gM")), Rr(LE_r[d][:]), True, True, r=[K_("gM"), "ler%d" % d], w=[pDk])
                pK, pKk = psd(d)
                k.mm(pK[:, 0:128], khT[:, tok], khT[:, tok], True, True, r=["khT"], w=[pKk])
                pQ, pQk = psd(d)
                k.mm(pQ[:, 0:128], T_("kb"), T_("qb"), True, True, r=[K_("kb"), K_("qb")], w=[pQk])
                k.act(T_("decT"), pD[:, 0:128], AF.Exp, r=[pDk], w=[K_("decT")])
                yield
                k.tt("dve", T_("tmpm"), pK[:, 0:128], T_("decT"), ALU.mult, r=[pKk, K_("decT")], w=[K_("tmpm")])
                k.stt("dve", T_("ksq"), T_("tmpm"), sc("sqb"), NSTR128[d][:], ALU.mult, ALU.mult,
                      r=[K_("tmpm"), "ts_sqb", "nstr128_%d" % d], w=[K_("ksq")])
                k.tt("dve", T_("tmpm"), pQ[:, 0:128], T_("decT"), ALU.mult, r=[pQk, K_("decT"), K_("ksq")], w=[K_("tmpm")])
                k.tt("pool", T_("attnT"), T_("tmpm"), TRI[d][:], ALU.mult, r=[K_("tmpm"), "tri%d" % d], w=[K_("attnT")])
                yield
                pB, pBk = psd(d)
                k.tr(pB[:, 0:128], T_("ksq"), ident[:], r=[K_("ksq"), "ident"], w=[pBk])
                k.act(T_("Bn"), pB[:, 0:128], AF.Identity, r=[pBk, "ts_sqb"], w=[K_("Bn")], scale=sc("sqb"))
                pB2, pB2k = psd(d)
                k.tr(pB2[:, 0:128], T_("Bn"), ident[:], r=[K_("Bn"), "ident"], w=[pB2k])
                k.copy("dve", T_("Bt"), pB2[:, 0:128], r=[pB2k], w=[K_("Bt")])
                k.tt("dve", Rr(T_("Btd")), T_("Bt"), b32[:], ALU.mult, r=[K_("Bt"), "b32"], w=[K_("Btd")])
                k.tt("dve", Rr(T_("Tt")), T_("Btd"), ident[:], ALU.add, r=[K_("Btd"), "ident"], w=[K_("Tt")])
                yield
                k.tt("pool", Rr(T_("Bnd")), T_("Bn"), b32[:], ALU.mult, r=[K_("Bn"), "b32"], w=[K_("Bnd")])
                k.tt("pool", Rr(T_("Bno")), T_("Bn"), T_("Bnd"), ALU.subtract, r=[K_("Bn"), K_("Bnd")], w=[K_("Bno")])
                cur = ("Bnd", "Btd")
                nxt = ("P", "Pt")
                NLV = 4
                for m in range(1, NLV + 1):
                    pP, pPk = psd(d)
                    k.mm(pP[:, 0:128], Rr(T_(cur[1])), Rr(T_(cur[0])), True, True, r=[K_(cur[0]), K_(cur[1])], w=[pPk])
                    k.copy("act", Rr(T_(nxt[0])), pP[:, 0:128], r=[pPk], w=[K_(nxt[0])])
                    if m < NLV:
                        pT_, pTk_ = psd(d)
                        k.mm(pT_[:, 0:128], Rr(T_(cur[0])), Rr(T_(cur[1])), True, True, r=[K_(cur[0]), K_(cur[1])], w=[pTk_])
                        k.copy("dve", Rr(T_(nxt[1])), pT_[:, 0:128], r=[pTk_], w=[K_(nxt[1])])
                    yield
                    pU, pUk = psd(d)
                    k.mm(pU[:, 0:128], Rr(T_(nxt[0])), Rr(T_("Tt")), True, True, r=[K_(nxt[0]), K_("Tt")], w=[pUk])
                    k.tt("dve", Rr(T_("Tt")), T_("Tt"), pU[:, 0:128], ALU.add, r=[K_("Tt"), pUk], w=[K_("Tt")])
                    cur, nxt = nxt, cur
                    yield
                pA_, pAk_ = psd(d)
                k.tr(pA_[:, 0:128], T_("Tt"), ident[:], r=[K_("Tt"), "ident"], w=[pAk_])
                k.copy("act", Rr(T_("Tdn")), pA_[:, 0:128], r=[pAk_], w=[K_("Tdn")])
                pM, pMk = psd(d)
                k.mm(pM[:, 0:128], Rr(T_("Bno")), Rr(T_("Tt")), True, True, r=[K_("Bno"), K_("Tt")], w=[pMk])
                k.copy("dve", Rr(T_("M1")), pM[:, 0:128], r=[pMk], w=[K_("M1")])
                yield
                pM2, pM2k = psd(d)
                k.mm(pM2[:, 0:128], Rr(T_("Tdn")), Rr(T_("M1")), True, True, r=[K_("Tdn"), K_("M1")], w=[pM2k])
                k.tt("dve", Rr(T_("Tt")), T_("Tt"), pM2[:, 0:128], ALU.add, r=[K_("Tt"), pM2k], w=[K_("Tt")])
                yield
                return B_

            def seq_step(d, t, hb, B_, S, Sk, first_dir_write):
                col = d * 8 + h
                rw = slice(hb * 64, hb * 64 + 64)
                t0 = t * 128 + hb * 64
                i = d
                sc = lambda n_: TS[n_][rw, t, col:col + 1]
                p1, p1k = psd(d)
                k.mm(p1[rw, 0:128], khT[:, t0:t0 + 64], S[:], True, True, r=["khT", Sk], w=[p1k])
                p3, p3k = psd(d)
                k.mm(p3[rw, 0:128], B_["qb"][0][:, hb * 64:hb * 64 + 64], Sbf[d][:], True, True, r=[B_["qb"][1], "Sbf%d" % d], w=[p3k])
                k.stt("dve", rhs_t[i][rw, :], p1[rw, 0:128], sc("c1"), B_["sv"][0][rw, :], ALU.mult, ALU.add,
                      r=[p1k, "ts_c1", B_["sv"][1]], w=["rhsp%d" % i])
                yield
                p2, p2k = psd(d)
                k.mm(p2[rw, 0:128], B_["Tt"][0][rw, rw], rhs_t[i][rw, :], True, True, r=[B_["Tt"][1], "rhsp%d" % i], w=[p2k])
                k.act(vn_t[i][rw, :], p2[rw, 0:128], AF.Identity, r=[p2k, "ts_sqb"], w=["vn%d" % i], scale=sc("sqb"))
                yield
                p4, p4k = psd(d)
                k.mm(p4[rw, 0:128], B_["attnT"][0][rw, rw], vn_t[i][rw, :], True, True, r=[B_["attnT"][1], "vn%d" % i], w=[p4k])
                k.copy("act", o2_t[i][rw, :], p4[rw, 0:128], r=[p4k], w=["o2t%d" % i])
                yield
                if False:
                    k.stt("dve", otm[rw, t, :], p3[rw, 0:128], sc("egam"), o2_t[i][rw, :], ALU.mult, ALU.add,
                          r=[p3k, "ts_egam", "o2t%d" % i], w=[("otm", t, hb)])
                else:
                    k.stt("dve", o2_t[i][rw, :], p3[rw, 0:128], sc("egam"), o2_t[i][rw, :], ALU.mult, ALU.add,
                          r=[p3k, "ts_egam", "o2t%d" % i], w=["o2t%d" % i])
                    k.tt("pool", otm[rw, t, :], otm[rw, t, :], o2_t[i][rw, :], ALU.add, r=[("otm", t, hb), "o2t%d" % i], w=[("otm", t, hb)])
                p5, p5k = psd(d)
                k.mm(p5[:, 0:128], B_["kdk"][0][rw, :], vn_t[i][rw, :], True, True, r=[B_["kdk"][1], "vn%d" % i], w=[p5k])
                k.stt("dve", S[:], S[:], etot[:, 2 * t + hb, col:col + 1], p5[:, 0:128], ALU.mult, ALU.add,
                      r=[Sk, "etot", p5k], w=[Sk])
                k.copy("act", Sbf[d][:], S[:], r=[Sk], w=["Sbf%d" % d])
                yield

            k.memset("pool", otm[:, :, :], 0.0, w=[("otm", t_, hb_) for t_ in range(20) for hb_ in range(2)])

            def dir_chain(d):
                col = d * 8 + h
                for si, (tf, nt) in enumerate(SEGS):
                    S, Sk = Sst[d], "Sst%d" % d
                    if si == 0:
                        k.dma(S[:], sdn_d[d * 8 + h, :, :], w=[Sk])
                    else:
                        k.memset("pool", S[:], 0.0, w=[Sk])
                    k.copy("act", Sbf[d][:], S[:], r=[Sk], w=["Sbf%d" % d])
                    order = range(tf, tf + nt) if d == 0 else range(tf + nt - 1, tf - 1, -1)
                    for t in order:
                        B_ = yield from precompute(d, t)
                        for hb in ((0, 1) if d == 0 else (1, 0)):
                            yield from seq_step(d, t, hb, B_, S, Sk, False)
                    if si > 0:
                        row0 = (((si - 1) * 2 + d) * 8 + h) * 128
                        k.dma(nd_d[row0:row0 + 128, :], S[:], r=[Sk])

            gens = [dir_chain(0), dir_chain(1)]
            while gens:
                for g_ in list(gens):
                    try:
                        next(g_)
                    except StopIteration:
                        gens.remove(g_)
            for t in (range(20) if SL >= 9 else []):
                i = rot["fin"] % 2
                rot["fin"] += 1
                b = t // 4
                cond = 0 if b < 4 else 1
                sm = sm1[i]
                k.act(on_t[i][:], otm[:, t, :], AF.Square, r=[("otm", t, 0), ("otm", t, 1)], w=["ont%d" % i, "gsm%d" % i], accum_out=sm[:, 0:1])
                k.act(sm[:, 1:2], sm[:, 0:1], AF.Sqrt, r=["gsm%d" % i], w=["gsm%d" % i], bias=EPS, scale=1.0 / 128)
                k.op("dve", lambda: nc.vector.reciprocal(out=sm[:, 2:3], in_=sm[:, 1:2]), r=["gsm%d" % i], w=["gsm%d" % i])
                k.ts("dve", on_t[i][:], otm[:, t, :], sm[:, 2:3], None, ALU.mult, r=[("otm", t, 0), ("otm", t, 1), "gsm%d" % i], w=["ont%d" % i])
                p, pk = psum()
                k.tr(p[:, 0:128], on_t[i][:], ident[:], r=["ont%d" % i, "ident"], w=[pk])
                k.stt("dve", og_all[:, h, t * 128:(t + 1) * 128], p[:, 0:128], ngc[:, 0:1], szT[:, t * 128:(t + 1) * 128], ALU.mult, ALU.mult,
                      r=[pk, "dnng", "szT"], w=[("og", h, b)])
        k.barrier()
        k.dma(xflat, xsp_d[:, :], w=allx)
        wo = [k.sb([128, 8, 128], BF16, "gwo%d" % i, stack=ph) for i in range(2)]
        for mc in range(8):
            wt, wk_ = wo[mc % 2], "gwo%d" % (mc % 2)
            k.dma(wt[:], woo_d[:, mc * 128:(mc + 1) * 128].rearrange("(kc p) m -> p kc m", p=128), w=[wk_], q="pool")
            for b in range(NB):
                cond = 0 if b < 4 else 1
                sl = slice(b * 512, (b + 1) * 512)
                p, pk = psum()
                for kc in range(8):
                    k.mm(p[:, :], wt[:, kc, :], og_all[:, kc, sl], kc == 0, kc == 7, r=[wk_, ("og", kc, b)], w=[pk])
                k.stt("dve", xT[:, mc, sl], p[:, :], gate_ap(1, 0, mc, cond), xT[:, mc, sl], ALU.mult, ALU.add,
                      r=[pk, "modv1", xk(b, mc)], w=[xk(b, mc)])
        k.barrier()
        ph.close()

    mixer0()
    mlp(0)
    mixer1()
    mlp(1)

    ystage = [k.sb([128, D], F32, "ystage%d" % i) for i in range(2)]
    yT = [k.sb([128, 8, 512], F32, "yT%d" % i) for i in range(1)]
    for b in range(NB):
        sl = slice(b * 512, (b + 1) * 512)
        rt, rk = rstd_block(b)
        for c in range(8):
            k.stt("dve", yT[0][:, c, :], xT[:, c, sl], gT[:, 32 + c:33 + c], rt[:], ALU.mult, ALU.mult,
                  r=[xk(b, c), rk, "gains"], w=[("yT", c)])
        for tt_ in range(4):
            t = b * 4 + tt_
            st = ystage[t % 2]
            sk = "ystage%d" % (t % 2)
            for half in range(2):
                p, pk = psum()
                for j in range(4):
                    c = half * 4 + j
                    k.tr(p[:, j * 128:(j + 1) * 128], yT[0][:, c, tt_ * 128:(tt_ + 1) * 128], ident[:],
                         r=[("yT", c), "ident"], w=[pk])
                k.copy("act" if half else "dve", st[:, half * 512:(half + 1) * 512], p[:, :], r=[pk], w=[sk + "h%d" % half])
            dst = ys_d[t * 128:(t + 1) * 128, :] if t < 16 else yp_d[(t - 16) * 128:(t - 15) * 128, :]
            k.dma(dst, st[:], r=[sk + "h0", sk + "h1"])
    k.finish()
    return k


_CACHE = {}
STAGE = "full"


def kernel(**inp):
    n = 8
    f = lambda a: np.ascontiguousarray(np.asarray(a, dtype=np.float32))
    if "k" not in _CACHE:
        _CACHE["k"] = build(STAGE)
    k = _CACHE["k"]
    gains = np.concatenate([f(inp["norm_mix_g"]).reshape(16, 128), f(inp["norm_ff_g"]).reshape(16, 128),
                            f(inp["final_norm_g"]).reshape(8, 128)], axis=0)
    shared = {
        "gains": gains,
        "w_mod": f(inp["w_mod"]), "b_mod": f(inp["b_mod"]).reshape(2, 48, 128),
        "w_ff1": f(inp["w_ff1"]), "w_ff2": f(inp["w_ff2"]),
        "s5lam": np.stack([f(inp["s5_lam_re"][0]).reshape(32, 128), f(inp["s5_lam_im"][0]).reshape(32, 128)]),
        "s5ldt": f(inp["s5_log_dt"][0]).reshape(32, 2),
        "s5b": np.stack([f(inp["s5_b_re"][0]).reshape(4096, 16), f(inp["s5_b_im"][0]).reshape(4096, 16)]),
        "s5c": np.stack([f(inp["s5_c_re"][0]).reshape(1024, 64), f(inp["s5_c_im"][0]).reshape(1024, 64)]),
        "s5d": f(inp["s5_d"][0]).reshape(4, 128), "s5bg": f(inp["s5_b_glu"][0]).reshape(4, 128),
        "s5wg": f(inp["s5_w_glu"][0]),
        "w_in_o": f(inp["w_in_o"][0]), "w_out_o": f(inp["w_out_o"][0]),
        "convw": f(inp["dn_conv_w"][0]).reshape(120, 128),
        "alog": f(inp["dn_a_log"][0]).reshape(1, 16), "dtb": f(inp["dn_dt_bias"][0]).reshape(1, 16),
        "dnng": f(inp["dn_norm_g"][0]).reshape(1, 128),
        "w_in_e": f(inp["w_in_e"][0]), "w_out_e": f(inp["w_out_e"][0]), "rpb": f(inp["na_rpb"][0]),
    }
    in_maps = []
    for i in range(n):
        m = dict(shared)
        m["xs"] = f(inp["x_sample"][i])
        m["xp"] = f(inp["x_prompt"][2 * i:2 * i + 2]).reshape(2 * LP, D)
        m["s5x0"] = np.stack([f(inp["state_s5_re"][i, 0]).reshape(32, 128), f(inp["state_s5_im"][i, 0]).reshape(32, 128)])
        m["sdn"] = f(inp["state_dn"][i, 0]).reshape(16, 128, 128)
        m["ck"] = f(inp["cache_na_k"][i, 0]).reshape(LP, 512)
        m["cv"] = f(inp["cache_na_v"][i, 0]).reshape(LP, 512)
        m["cond"] = np.concatenate([f(inp["c"][i]).reshape(8, 128), f(inp["c_ctx"]).reshape(8, 128)], axis=0)
        in_maps.append(m)
    res = run_bass_kernel_spmd(k.nc, in_maps, core_ids=list(range(n)))
    R = res.results
    y_s = np.stack([R[i]["ys"] for i in range(n)], axis=0)
    y_p = np.concatenate([R[i]["yp"].reshape(2, LP, D) for i in range(n)], axis=0)
    nk = np.concatenate([R[i]["nk"].reshape(2, 1, LP, 8, 64) for i in range(n)], axis=0)
    nv = np.concatenate([R[i]["nv"].reshape(2, 1, LP, 8, 64) for i in range(n)], axis=0)
    nsre = np.concatenate([R[i]["nsre"].reshape(2, 1, 2, 32, 64) for i in range(n)], axis=0)
    nsim = np.concatenate([R[i]["nsim"].reshape(2, 1, 2, 32, 64) for i in range(n)], axis=0)
    nd = np.concatenate([R[i]["nd"].reshape(2, 1, 2, 8, 128, 128) for i in range(n)], axis=0)
    return y_p, y_s, nk, nv, nsre, nsim, nd
```
